# Optimizing a Trainium2 kernel written in Bass

```python
import math
import jax, jax.numpy as jnp
from jax import lax
import numpy as np

D_MODEL = 4096
BATCH = 2
SEQ = 8192
DEPTH = 2

CHUNK = 64
N_META = 16
N_BRANCH = 3
BRANCH_WIDTH = D_MODEL // 2
EPS = 1e-6

SSD_HEAD_DIM = 64
SSD_HEADS = BRANCH_WIDTH // SSD_HEAD_DIM
SSD_GROUPS = 4
SSD_HEADS_PER_GROUP = SSD_HEADS // SSD_GROUPS
SSD_STATE = 128
SSD_CONV = 4
SSD_CONV_DIM = BRANCH_WIDTH + 2 * SSD_GROUPS * SSD_STATE

MLA_HEADS = 16
MLA_NOPE = 128
MLA_ROPE = 64
MLA_V = BRANCH_WIDTH // MLA_HEADS
MLA_Q_RANK = D_MODEL // 4
MLA_KV_RANK = 512
ROPE_BASE = 10000.0
Q_BLOCK = 128

POOL_WINDOWS = (2, 4, 8, 16)
POOL_GROUPS = 4
POOL_GROUP_DIM = BRANCH_WIDTH // POOL_GROUPS

IN_SIZES = (
    BRANCH_WIDTH,
    SSD_CONV_DIM,
    SSD_HEADS,
    MLA_Q_RANK,
    MLA_KV_RANK,
    MLA_ROPE,
    BRANCH_WIDTH,
    BRANCH_WIDTH,
    BRANCH_WIDTH,
)
IN_DIM = sum(IN_SIZES)

kernel_name = "hybrid_ssd_mla_pool_streaming_trunk"


def rms_norm(x, w):
    xf = x.astype(jnp.float32)
    y = xf * lax.rsqrt(jnp.mean(xf * xf, axis=-1, keepdims=True) + EPS)
    return (y * w.astype(jnp.float32)).astype(x.dtype)


def chunk_ids(m):
    p = jnp.arange(m)
    return jnp.where(p < N_META, 0, (p - N_META) // CHUNK + 1)


def rotary(x, cos, sin):
    x1, x2 = jnp.split(x, 2, axis=-1)
    return jnp.concatenate([x1 * cos - x2 * sin, x2 * cos + x1 * sin], axis=-1)


def ssd_mixer(z, xbc, dt_raw, conv_w, conv_b, dt_bias, a_log, d_skip, norm_w):
    b, n, _ = xbc.shape
    f32 = jnp.float32
    xbc = lax.conv_general_dilated(
        xbc, conv_w, window_strides=(1,), padding=[(SSD_CONV - 1, 0)],
        dimension_numbers=("NWC", "WIO", "NWC"), feature_group_count=SSD_CONV_DIM)
    xbc = jax.nn.silu(xbc + conv_b)
    xs, bm, cm = jnp.split(xbc, [BRANCH_WIDTH, BRANCH_WIDTH + SSD_GROUPS * SSD_STATE], axis=-1)
    xs = xs.reshape(b, n, SSD_HEADS, SSD_HEAD_DIM).astype(f32)
    dt = jax.nn.softplus(dt_raw.astype(f32) + dt_bias.astype(f32))
    da = dt * (-jnp.exp(a_log.astype(f32)))
    pad = CHUNK - N_META
    nc = (n + pad) // CHUNK

    def chunked(t):
        t = jnp.pad(t, ((0, 0), (pad, 0)) + ((0, 0),) * (t.ndim - 2))
        return t.reshape((b, nc, CHUNK) + t.shape[2:])

    xdt = chunked(xs * dt[..., None]).reshape(
        b, nc, CHUNK, SSD_GROUPS, SSD_HEADS_PER_GROUP, SSD_HEAD_DIM)
    bc = chunked(bm.astype(f32).reshape(b, n, SSD_GROUPS, SSD_STATE))
    cc = chunked(cm.astype(f32).reshape(b, n, SSD_GROUPS, SSD_STATE))
    a_c = chunked(da).reshape(b, nc, CHUNK, SSD_GROUPS, SSD_HEADS_PER_GROUP)
    a_c = a_c.transpose(0, 3, 4, 1, 2)
    a_cum = jnp.cumsum(a_c, axis=-1)
    causal = jnp.tril(jnp.ones((CHUNK, CHUNK), dtype=bool))
    seg = a_cum[..., :, None] - a_cum[..., None, :]
    decay_in = jnp.exp(jnp.where(causal, seg, -jnp.inf))
    y_diag = jnp.einsum("bclgn,bcsgn,bgrcls,bcsgrp->bclgrp", cc, bc, decay_in, xdt)
    decay_to_end = jnp.exp(a_cum[..., -1:] - a_cum)
    chunk_states = jnp.einsum("bclgn,bgrcl,bclgrp->cbgrpn", bc, decay_to_end, xdt)
    chunk_decay = jnp.exp(a_cum[..., -1]).transpose(3, 0, 1, 2)

    def step(state, inp):
        s_c, d_c = inp
        return state * d_c[..., None, None] + s_c, state

    init = jnp.zeros(chunk_states.shape[1:], f32)
    _, start_states = lax.scan(step, init, (chunk_states, chunk_decay))
    y_off = jnp.einsum("bclgn,cbgrpn,bgrcl->bclgrp", cc, start_states, jnp.exp(a_cum))
    y = (y_diag + y_off).reshape(b, nc * CHUNK, SSD_HEADS, SSD_HEAD_DIM)[:, pad:]
    y = y + d_skip.astype(f32)[:, None] * xs
    y = y.reshape(b, n, BRANCH_WIDTH) * jax.nn.silu(z.astype(f32))
    yg = y.reshape(b, n, SSD_GROUPS, BRANCH_WIDTH // SSD_GROUPS)
    yg = yg * lax.rsqrt(jnp.mean(yg * yg, axis=-1, keepdims=True) + EPS)
    return (yg.reshape(b, n, BRANCH_WIDTH) * norm_w.astype(f32)).astype(z.dtype)


def mla_mixer(q_lat, kv_lat, k_rope_raw, gate, q_norm_w, w_q_b, kv_norm_w, w_kv_b):
    b, n, _ = q_lat.shape
    pos = jnp.arange(n, dtype=jnp.float32)
    inv_freq = jnp.power(ROPE_BASE, -jnp.arange(0, MLA_ROPE, 2, dtype=jnp.float32) / MLA_ROPE)
    ang = pos[:, None] * inv_freq[None, :]
    cos = jnp.cos(ang).astype(q_lat.dtype)
    sin = jnp.sin(ang).astype(q_lat.dtype)
    q = (rms_norm(q_lat, q_norm_w) @ w_q_b).reshape(b, n, MLA_HEADS, MLA_NOPE + MLA_ROPE)
    q_nope, q_pe = jnp.split(q, [MLA_NOPE], axis=-1)
    q_pe = rotary(q_pe, cos[:, None, :], sin[:, None, :])
    kv = (rms_norm(kv_lat, kv_norm_w) @ w_kv_b).reshape(b, n, MLA_HEADS, MLA_NOPE + MLA_V)
    k_nope, v = jnp.split(kv, [MLA_NOPE], axis=-1)
    k_pe = rotary(k_rope_raw, cos, sin)
    scale = 1.0 / math.sqrt(MLA_NOPE + MLA_ROPE)
    nq = -(-n // Q_BLOCK) * Q_BLOCK
    nblk = nq // Q_BLOCK
    qpad = nq - n

    def blocks(t):
        t = jnp.pad(t, ((0, 0), (0, qpad), (0, 0), (0, 0)))
        return t.reshape((b, nblk, Q_BLOCK) + t.shape[2:]).swapaxes(0, 1)

    qn_blk = blocks(q_nope)
    qp_blk = blocks(q_pe)
    qc_blk = chunk_ids(nq).reshape(nblk, Q_BLOCK)
    kc = chunk_ids(n)

    def attend(args):
        qn_b, qp_b, qc_b = args
        s = (jnp.einsum("bqhd,bkhd->bhqk", qn_b, k_nope, preferred_element_type=jnp.float32)
             + jnp.einsum("bqhd,bkd->bhqk", qp_b, k_pe, preferred_element_type=jnp.float32)) * scale
        s = jnp.where(qc_b[:, None] >= kc[None, :], s, -jnp.inf)
        p = jax.nn.softmax(s, axis=-1).astype(v.dtype)
        return jnp.einsum("bhqk,bkhd->bqhd", p, v)

    o = lax.map(attend, (qn_blk, qp_blk, qc_blk))
    o = o.swapaxes(0, 1).reshape(b, nq, MLA_HEADS * MLA_V)[:, :n]
    return o * jax.nn.silu(gate)


def pool_mixer(u, gate, w_pool, pool_scale):
    b, n, _ = u.shape
    uf = u.astype(jnp.float32)
    csum = jnp.cumsum(uf, axis=1)
    steps = jnp.arange(1, n + 1)
    pooled = []
    for g, w in enumerate(POOL_WINDOWS):
        cg = csum[..., g * POOL_GROUP_DIM:(g + 1) * POOL_GROUP_DIM]
        lag = jnp.pad(cg, ((0, 0), (w, 0), (0, 0)))[:, :n]
        count = jnp.minimum(steps, w).astype(jnp.float32)[None, :, None]
        pooled.append((cg - lag) / count)
    pooled = jnp.stack(pooled, axis=2)
    mixed = (pooled - uf.reshape(b, n, POOL_GROUPS, POOL_GROUP_DIM)).astype(u.dtype)
    y = jnp.einsum("bngc,gcd->bngd", mixed, w_pool).reshape(b, n, BRANCH_WIDTH) * pool_scale
    return y * jax.nn.silu(gate)


def hybrid_layer(h, pre_w, w_in, conv_w, conv_b, dt_bias, a_log, d_skip, ssd_norm_w,
                 q_norm_w, w_q_b, kv_norm_w, w_kv_b, w_pool, pool_scale,
                 w_gate, w_branch, w_out, post_w):
    xn = rms_norm(h, pre_w)
    proj = xn @ w_in
    offsets = [int(o) for o in np.cumsum(IN_SIZES)[:-1]]
    (z, xbc, dt_raw, q_lat, kv_lat, k_rope, mla_gate, pool_u, pool_gate) = jnp.split(proj, offsets, axis=-1)
    y_ssd = ssd_mixer(z, xbc, dt_raw, conv_w, conv_b, dt_bias, a_log, d_skip, ssd_norm_w)
    y_mla = mla_mixer(q_lat, kv_lat, k_rope, mla_gate, q_norm_w, w_q_b, kv_norm_w, w_kv_b)
    y_pool = pool_mixer(pool_u, pool_gate, w_pool, pool_scale)
    branches = (y_ssd, y_mla, y_pool)
    merged = jax.nn.sigmoid(xn @ w_gate[0]) * (branches[0] @ w_branch[0])
    for i in range(1, N_BRANCH):
        merged = merged + jax.nn.sigmoid(xn @ w_gate[i]) * (branches[i] @ w_branch[i])
    out = merged @ w_out
    return h + rms_norm(out, post_w)


def setup_inputs(seed: int = 0) -> dict:
    key = jax.random.key(seed)
    ks = jax.random.split(key, 20)
    f32 = jnp.float32

    def nrm(k, shape, scale):
        return jax.random.normal(k, shape, f32) * scale

    def gain(k, shape):
        return 1.0 + 0.02 * jax.random.normal(k, shape, f32)

    dt0 = jnp.exp(jax.random.uniform(ks[6], (DEPTH, SSD_HEADS), f32, math.log(1e-3), math.log(1e-1)))
    dt_bias = dt0 + jnp.log(-jnp.expm1(-dt0))
    return {
        "x": nrm(ks[0], (BATCH, SEQ, D_MODEL), 1.0),
        "meta_tokens": nrm(ks[1], (N_META, D_MODEL), 1.0),
        "pre_norm_w": gain(ks[2], (DEPTH, D_MODEL)),
        "w_in": nrm(ks[3], (DEPTH, D_MODEL, IN_DIM), D_MODEL ** -0.5),
        "conv_w": nrm(ks[4], (DEPTH, SSD_CONV, 1, SSD_CONV_DIM), SSD_CONV ** -0.5),
        "conv_b": nrm(ks[5], (DEPTH, SSD_CONV_DIM), 0.02),
        "dt_bias": dt_bias,
        "a_log": jnp.log(jax.random.uniform(ks[7], (DEPTH, SSD_HEADS), f32, 1.0, 16.0)),
        "d_skip": gain(ks[8], (DEPTH, SSD_HEADS)),
        "ssd_norm_w": gain(ks[9], (DEPTH, BRANCH_WIDTH)),
        "q_norm_w": gain(ks[10], (DEPTH, MLA_Q_RANK)),
        "w_q_b": nrm(ks[11], (DEPTH, MLA_Q_RANK, MLA_HEADS * (MLA_NOPE + MLA_ROPE)), MLA_Q_RANK ** -0.5),
        "kv_norm_w": gain(ks[12], (DEPTH, MLA_KV_RANK)),
        "w_kv_b": nrm(ks[13], (DEPTH, MLA_KV_RANK, MLA_HEADS * (MLA_NOPE + MLA_V)), MLA_KV_RANK ** -0.5),
        "w_pool": nrm(ks[14], (DEPTH, POOL_GROUPS, POOL_GROUP_DIM, POOL_GROUP_DIM), POOL_GROUP_DIM ** -0.5),
        "pool_scale": gain(ks[15], (DEPTH, BRANCH_WIDTH)),
        "w_gate": nrm(ks[16], (DEPTH, N_BRANCH, D_MODEL, D_MODEL), D_MODEL ** -0.5),
        "w_branch": nrm(ks[17], (DEPTH, N_BRANCH, BRANCH_WIDTH, D_MODEL), BRANCH_WIDTH ** -0.5),
        "w_out": nrm(ks[18], (DEPTH, D_MODEL, D_MODEL), D_MODEL ** -0.5),
        "post_norm_w": gain(ks[19], (DEPTH, D_MODEL)),
    }


def reference(x, meta_tokens, pre_norm_w, w_in, conv_w, conv_b, dt_bias, a_log, d_skip,
              ssd_norm_w, q_norm_w, w_q_b, kv_norm_w, w_kv_b, w_pool, pool_scale,
              w_gate, w_branch, w_out, post_norm_w):
    b = x.shape[0]
    meta = jnp.broadcast_to(meta_tokens.astype(x.dtype)[None], (b, N_META, D_MODEL))
    h = jnp.concatenate([meta, x], axis=1)
    for i in range(DEPTH):
        h = hybrid_layer(h, pre_norm_w[i], w_in[i], conv_w[i], conv_b[i], dt_bias[i], a_log[i],
                         d_skip[i], ssd_norm_w[i], q_norm_w[i], w_q_b[i], kv_norm_w[i], w_kv_b[i],
                         w_pool[i], pool_scale[i], w_gate[i], w_branch[i], w_out[i], post_norm_w[i])
    return h[:, N_META:]
```

```python
import contextlib
import numpy as np
import ml_dtypes
import concourse.bass as bass
import concourse.mybir as mybir
from concourse.bass_utils import run_bass_kernel_spmd

F32 = mybir.dt.float32
BF16 = mybir.dt.bfloat16
AF = mybir.ActivationFunctionType
ALU = mybir.AluOpType

D = 4096
BW = 2048
EPS = 1e-6
NH, HD, NG, NST, XBC = 32, 64, 4, 128, 3072
MH, NOPE, ROPE, VD, QR, KVR = 16, 128, 64, 128, 1024, 512
IN_DIM = 12896
O_Z, O_XBC, O_DT, O_Q, O_KV, O_KR, O_MG, O_PU, O_PG = 0, 2048, 5120, 5152, 6176, 6688, 6752, 8800, 10848
HALO = 16
NEG = -30000.0

SSD_STOP = 9.0
V_CW, V_CB, V_DS, V_SN, V_QN, V_KN, V_PS, V_DTB, V_IF, NV = 0, 96, 120, 136, 152, 160, 164, 180, 181, 182
C_M0, C_VIS, C_OH, C_POS = 0, 1, 5, 9


class Buf:
    __slots__ = ("name", "lw", "rd", "excl")

    def __init__(self, name="", excl=False):
        self.name = name
        self.lw = None
        self.rd = []
        self.excl = excl


class Eng:
    def __init__(self, name, h, sem):
        self.name, self.h, self.sem = name, h, sem
        self.cnt = 0
        self.known = {}

    def wait_ev(self, ev):
        sem, val, _ = ev
        if self.known.get(id(sem), 0) < val:
            self.h.wait_ge(sem, val)
            self.known[id(sem)] = val


class Ctx:
    def __init__(self, nc, n_dma_sems=8):
        self.nc = nc
        self.es = contextlib.ExitStack()
        mk = lambda n: self.es.enter_context(nc.semaphore(n))
        self.pe = Eng("pe", nc.tensor, mk("s_pe"))
        self.act = Eng("act", nc.scalar, mk("s_act"))
        self.dve = Eng("dve", nc.vector, mk("s_dve"))
        self.pool = Eng("pool", nc.gpsimd, mk("s_pool"))
        self.sp = Eng("sp", nc.sync, mk("s_sp"))
        self.engs = [self.pe, self.act, self.dve, self.pool, self.sp]
        self.dsems = {}
        for e in (self.sp, self.pool, self.act):
            self.dsems[e.name] = [[mk(f"d_{e.name}{i}"), 0] for i in range(n_dma_sems)]
        self.drr = {e.name: 0 for e in (self.sp, self.pool, self.act)}
        self.n_ops = 0
        self.cc_sem = mk("s_cc")
        self.cc_cnt = 0

    def close(self):
        self.es.close()

    def _deps(self, eng, reads, writes, same_raw):
        for r in reads:
            if r.lw is not None and (r.lw[2] != eng.name or same_raw):
                eng.wait_ev(r.lw)
        for w in writes:
            if w.lw is not None and (w.lw[2] != eng.name or same_raw):
                eng.wait_ev(w.lw)
            for ev in w.rd:
                if ev[2] != eng.name:
                    eng.wait_ev(ev)

    def _commit(self, ev, reads, writes):
        for r in reads:
            r.rd.append(ev)
            if len(r.rd) > 48:
                last = {}
                for e in r.rd:
                    if id(e[0]) not in last or last[id(e[0])][1] < e[1]:
                        last[id(e[0])] = e
                r.rd = list(last.values())
        for w in writes:
            w.lw = ev
            w.rd = []

    def op(self, eng, fn, reads=(), writes=()):
        ex = [r for r in reads if r.excl]
        if ex:
            writes = list(writes) + ex
            reads = [r for r in reads if not r.excl]
        self._deps(eng, reads, writes, same_raw=(eng is not self.pe))
        ins = fn()
        ins.then_inc(eng.sem, 1)
        eng.cnt += 1
        ev = (eng.sem, eng.cnt, eng.name)
        self._commit(ev, reads, writes)
        self.n_ops += 1
        return ev

    def chain(self, eng, fns, reads=(), writes=()):
        ev = None
        for fn in fns:
            ev = self.op(eng, fn, reads=reads, writes=writes)
        return ev

    def dma(self, eng, out, in_, reads=(), writes=(), **kw):
        pool = self.dsems[eng.name]
        i = self.drr[eng.name]
        self.drr[eng.name] = (i + 1) % len(pool)
        slot = pool[i]
        if slot[1] > 0:
            eng.wait_ev((slot[0], slot[1], "dma"))
        self._deps(eng, reads, writes, same_raw=True)
        eng.h.dma_start(out=out, in_=in_, **kw).then_inc(slot[0], 16)
        slot[1] += 16
        ev = (slot[0], slot[1], "dma_" + eng.name + str(i))
        self._commit(ev, reads, writes)
        self.n_ops += 1
        return ev

    def coll(self, kind, groups, src, dst, reads=(), writes=()):
        eng = self.pool
        if self.cc_cnt > 0:
            eng.wait_ev((self.cc_sem, self.cc_cnt, "coll"))
        self._deps(eng, reads, writes, same_raw=True)
        self.nc.gpsimd.collective_compute(kind, ALU.bypass, replica_groups=groups, ins=[src.opt()], outs=[dst.opt()]).then_inc(self.cc_sem)
        self.cc_cnt += 1
        ev = (self.cc_sem, self.cc_cnt, "coll")
        self._commit(ev, reads, writes)
        return ev

    def barrier(self):
        evs = []
        for e in self.engs:
            if e.cnt > 0:
                evs.append((e.sem, e.cnt, e.name))
        for name, pool in self.dsems.items():
            for s in pool:
                if s[1] > 0:
                    evs.append((s[0], s[1], "dma"))
        if self.cc_cnt > 0:
            evs.append((self.cc_sem, self.cc_cnt, "coll"))
        for e in self.engs:
            for ev in evs:
                if ev[0] is not e.sem:
                    e.wait_ev(ev)


class Phase:
    _seq = [0]

    def __init__(self, P, name):
        Phase._seq[0] += 1
        self.P, self.name = P, f"{name}x{Phase._seq[0]}"
        self.es = contextlib.ExitStack()
        self.n = 0

    def __enter__(self):
        self.P.c.barrier()
        return self

    def __exit__(self, *a):
        self.P.c.barrier()
        self.es.close()
        return False

    def sb(self, shape, dt, name=None):
        self.n += 1
        return self.es.enter_context(self.P.nc.sbuf_tensor(f"{self.name}_{name or 's'}{self.n}", list(shape), dt))

    def ps(self, shape, dt, name=None):
        self.n += 1
        return self.es.enter_context(self.P.nc.psum_tensor(f"{self.name}_{name or 'p'}{self.n}", list(shape), dt))


def ttiles(TS, n):
    out = [(0, HALO)]
    t = HALO
    while t < TS:
        m = min(n, TS - t)
        out.append((t, m))
        t += m
    return out


class Prog:
    def __init__(self, SEG, NSEG, mode="B", dbg=False):
        self.SEG, self.NSEG, self.mode, self.dbg = SEG, NSEG, mode, dbg
        self.TS = TS = HALO + SEG
        self.NK = HALO + NSEG * SEG
        self.NKG = HALO + (NSEG - 1) * SEG
        self.NKC = self.NKG + SEG
        nc = self.nc = bass.Bass("TRN2", target_bir_lowering=False)
        self.c = Ctx(nc)
        skind = "ExternalOutput" if dbg else "Internal"
        dt_ = lambda n, s, t, k: nc.dram_tensor(n, list(s), t, kind=k).ap()
        self._dt = dt_
        NK = self.NK
        self.ispec = {
            "h": ([TS, D], F32), "cmeta": ([128, 16], F32), "vecs": ([128, NV], F32), "rows": ([1, 2 * D + 32], F32),
            "w_in": ([D, IN_DIM], F32), "w_gate": ([3, D, D], F32), "w_branch": ([3, BW, D], F32),
            "w_out": ([D, D], F32), "w_q_b": ([QR, MH * (NOPE + ROPE)], F32), "w_kv_b": ([KVR, MH * (NOPE + VD)], F32),
            "w_pool": ([4, 512, 512], F32), "kv_all": ([KVR + ROPE, NK], BF16), "L_all": ([NSEG, 128, BW], F32),
            "D_all": ([1, NSEG * NH], F32),
        }
        self.inputs = {}
        self.L = 0
        self.per_layer = {"vecs", "rows", "w_in", "w_gate", "w_branch", "w_out", "w_q_b", "w_kv_b", "w_pool"}
        if mode == "F":
            self.h_out = dt_("h_out", [TS, D], F32, "ExternalOutput")
            self.h1 = dt_("h1", [TS, D], F32, "Internal")
            self.L_out = dt_("L_loc", [128, BW], F32, "Internal")
            self.D_out = dt_("D_loc", [128, NH], F32, "Internal")
            self.kvc = [dt_(f"kvc{k}", [128 if k < 4 else ROPE, TS], BF16, "Internal") for k in range(5)]
            self.kvc_g = [dt_(f"kvcg{k}", [NSEG * (128 if k < 4 else ROPE), TS], BF16, "Internal") for k in range(5)]
            self.L_g = dt_("L_g", [NSEG * 128, BW], F32, "Internal")
            self.D_g = dt_("D_g", [NSEG * 128, NH], F32, "Internal")
            self.tail_loc = dt_("tail_loc", [HALO, D], F32, "Internal")
            self.tails_g = dt_("tails_g", [NSEG * HALO, D], F32, "Internal")
            self.groups = [list(range(b * NSEG, (b + 1) * NSEG)) for b in range(2)]
        elif mode == "B":
            self.h_out = dt_("h_out", [TS, D], F32, "ExternalOutput")
        else:
            self.kvx_out = dt_("kvx", [KVR + ROPE, TS], BF16, "ExternalOutput")
            self.L_out = dt_("L_out", [128, BW], F32, "ExternalOutput")
            self.D_out = dt_("D_out", [128, NH], F32, "ExternalOutput")
        self.xnT = dt_("xnT", [D, TS], BF16, skind)
        self.projT = dt_("projT", [IN_DIM, TS], BF16, skind)
        self.dtT = dt_("dtT", [NH, TS], F32, skind)
        self.xbcT = dt_("xbcT", [XBC, TS], BF16, skind)
        self.kvx = dt_("kvx_s", [KVR + ROPE, TS], BF16, skind) if mode in ("B", "F") else self.kvx_out
        self.h_src = None
        self.h_dst = None
        if mode in ("B", "F"):
            self.sigT = dt_("sigT", [3 * D, TS], BF16, skind)
            self.brT = dt_("brT", [3 * BW, TS], BF16, skind)
            self.mergedT = dt_("mergedT", [D, TS], BF16, skind)
            self.outF = dt_("outF", [TS, D], F32, skind)
        self.bufs = {}

    def I(self, name):
        key = f"{name}_L{self.L}" if (self.mode == "F" and name in self.per_layer) else name
        if key not in self.inputs:
            shp, t = self.ispec[name]
            self.inputs[key] = self._dt(key, shp, t, "ExternalInput")
        return self.inputs[key]

    def S(self, attr, name, shape, dt):
        if not hasattr(self, attr) or getattr(self, attr) is None:
            setattr(self, attr, self._dt(name, shape, dt, "Internal"))
        return getattr(self, attr)

    def hsrc(self):
        return self.h_src if self.h_src is not None else self.I("h")

    def hdst(self):
        return self.h_dst if self.h_dst is not None else self.h_out

    def kv_pieces(self, k):
        SEG, NSEG, TS = self.SEG, self.NSEG, self.TS
        r0 = k * 128
        nr = 128 if k < 4 else ROPE
        if self.mode != "F":
            return [(0, self.NKG, self.I("kv_all")[r0:r0 + nr, 0:self.NKG])]
        g = self.kvc_g[k]
        out = [(0, HALO, g[0:nr, 0:HALO])]
        for j in range(NSEG - 1):
            out.append((HALO + j * SEG, SEG, g[j * nr:(j + 1) * nr, HALO:TS]))
        return out

    def B(self, name):
        if name not in self.bufs:
            self.bufs[name] = Buf(name)
        return self.bufs[name]

    def load_consts(self, ph):
        nc, c = self.nc, self.c
        self.cmeta = ph.sb([128, 16], F32, "cmeta")
        self.vecs = ph.sb([128, NV], F32, "vecs")
        bc = self.B("consts")
        c.dma(c.sp, self.cmeta[:], self.I("cmeta")[:, :], writes=[bc])
        c.dma(c.sp, self.vecs[:], self.I("vecs")[:, :], writes=[bc])
        self.identf = ph.sb([128, 128], F32, "identf")
        self.ident = ph.sb([128, 128], BF16, "ident")
        self.onesf = ph.sb([128, 128], F32, "onesf")
        self.onesb = ph.sb([128, 128], BF16, "onesb")
        self.epsc = ph.sb([128, 1], F32, "epsc")

        c.chain(c.pool, [
            lambda: nc.gpsimd.memset(self.identf[:], 0.0),
            lambda: nc.gpsimd.memset(self.onesf[:], 1.0),
            lambda: nc.gpsimd.memset(self.onesb[:], 1.0),
            lambda: nc.gpsimd.memset(self.epsc[:], EPS),
            lambda: nc.gpsimd.affine_select(out=self.identf[:], in_=self.identf[:], pattern=[[-1, 128]],
                                            compare_op=ALU.not_equal, fill=1.0, base=0, channel_multiplier=1),
            lambda: nc.gpsimd.tensor_copy(out=self.ident[:], in_=self.identf[:]),
        ], writes=[bc])
        c.barrier()

    def ph_norm(self):
        nc, c, TS = self.nc, self.c, self.TS
        with Phase(self, "n1") as ph:
            prew = ph.sb([128, D], F32, "prew")
            bpw = self.B("prew")
            c.dma(c.sp, prew[:], self.I("rows")[0:1, 0:D].partition_broadcast(128), writes=[bpw])
            ht = [ph.sb([128, D], F32, f"ht{i}") for i in range(2)]
            junk = ph.sb([128, D], BF16, "junk")
            xnb = [ph.sb([128, D], BF16, f"xnb{i}") for i in range(2)]
            ss = [ph.sb([128, 1], F32, f"ss{i}") for i in range(2)]
            xT = [ph.sb([128, 32, 128], BF16, f"xT{i}") for i in range(2)]
            pT = [ph.ps([128, 8, 128], BF16, f"pT{i}") for i in range(4)]
            bht = [self.B(f"n1ht{i}") for i in range(2)]
            bxn = [self.B(f"n1xn{i}") for i in range(2)]
            bss = [self.B(f"n1ss{i}") for i in range(2)]
            bxT = [self.B(f"n1xT{i}") for i in range(2)]
            bpT = [self.B(f"n1pT{i}") for i in range(4)]
            bj = self.B("n1junk")
            xnT_v = self.xnT.rearrange("(kc p) t -> p kc t", p=128)
            npt = 0
            for it, (t0, nt) in enumerate(ttiles(TS, 128)):
                i = it % 2
                c.dma(c.sp, ht[i][0:nt, :], self.hsrc()[t0:t0 + nt, :], reads=[self.B("hres")], writes=[bht[i]])
                c.op(c.act, lambda: nc.scalar.activation(out=junk[0:nt, :], in_=ht[i][0:nt, :], func=AF.Square,
                                                         accum_out=ss[i][0:nt, :]),
                     reads=[bht[i]], writes=[bj, bss[i]])

                c.op(c.act, lambda: nc.scalar.activation(out=ss[i][0:nt, :], in_=ss[i][0:nt, :], func=AF.Sqrt,
                                                         scale=1.0 / D, bias=self.epsc[0:nt, :]),
                     reads=[bss[i]], writes=[bss[i]])
                c.op(c.dve, lambda: nc.vector.reciprocal(out=ss[i][0:nt, :], in_=ss[i][0:nt, :]),
                     reads=[bss[i]], writes=[bss[i]])
                c.op(c.dve, lambda: nc.vector.scalar_tensor_tensor(out=xnb[i][0:nt, :], in0=ht[i][0:nt, :],
                                                                   scalar=ss[i][0:nt, 0:1], in1=prew[0:nt, :],
                                                                   op0=ALU.mult, op1=ALU.mult),
                     reads=[bht[i], bss[i], bpw], writes=[bxn[i]])
                for g in range(4):
                    j = npt % 4
                    npt += 1

                    def ft():
                        for k in range(8):
                            ins = nc.tensor.transpose(pT[j][:, k, 0:nt], xnb[i][0:nt, (g * 8 + k) * 128:(g * 8 + k + 1) * 128],
                                                      self.ident[0:nt, 0:nt])
                        return ins
                    c.op(c.pe, ft, reads=[bxn[i]], writes=[bpT[j]])
                    if g % 2 == 0:
                        c.op(c.act, lambda: nc.scalar.copy(out=xT[i][:, g * 8:(g + 1) * 8, 0:nt], in_=pT[j][:, :, 0:nt]),
                             reads=[bpT[j]], writes=[bxT[i]])
                    else:
                        c.op(c.dve, lambda: nc.vector.tensor_copy(out=xT[i][:, g * 8:(g + 1) * 8, 0:nt], in_=pT[j][:, :, 0:nt]),
                             reads=[bpT[j]], writes=[bxT[i]])
                c.dma(c.pool, xnT_v[:, :, t0:t0 + nt], xT[i][:, :, 0:nt], reads=[bxT[i]], writes=[self.B("xnT")])

    def gemm_A(self, ph, actT, KC, blocks, tag, n_tile=512, tts=None, width=None):
        nc, c, TS = self.nc, self.c, (width or self.TS)
        wst = [ph.sb([128, KC, 128], F32, f"wst{i}") for i in range(2)]
        wbf = [ph.sb([128, KC, 128], BF16, f"wbf{i}") for i in range(2)]
        ost_b = [ph.sb([128, TS], BF16, f"ostb{i}") for i in range(2)]
        ost_f = [ph.sb([128, TS], F32, f"ostf{i}") for i in range(2)] if any(b.get("odt", BF16) == F32 for b in blocks) else None
        pss = [ph.ps([128, 512], F32, f"ps{i}") for i in range(4)]
        bw = [self.B(f"{tag}wst{i}") for i in range(2)]
        bwb = [self.B(f"{tag}wbf{i}") for i in range(2)]
        bo = [self.B(f"{tag}ost{i}") for i in range(2)]
        bp = [self.B(f"{tag}ps{i}") for i in range(4)]
        bact = self.B("actres")
        if any(b.get("pm") is not None for b in blocks):
            pmb = [ph.sb([128, TS], BF16, f"pmb{i}") for i in range(2)]
            pmf = [ph.sb([128, TS], F32, f"pmf{i}") for i in range(2)]
            evf = [ph.sb([128, 512], F32, f"evf{i}") for i in range(2)]
            bpmb = [self.B(f"{tag}pmb{i}") for i in range(2)]
            bpm = [self.B(f"{tag}pm{i}") for i in range(2)]
            bev = [self.B(f"{tag}ev{i}") for i in range(2)]
        tts = tts or ttiles(TS, n_tile)
        npp = 0
        for ib, blk in enumerate(blocks):
            i = ib % 2
            mw = blk["mw"]
            Wv = blk["W"].rearrange("(kc p) m -> p kc m", p=128)
            c.dma(c.sp, wst[i][:, :, 0:mw], Wv, writes=[bw[i]])
            half = KC // 2 if KC >= 2 else KC
            c.op(c.dve, lambda: nc.vector.tensor_copy(out=wbf[i][:, 0:half, 0:mw], in_=wst[i][:, 0:half, 0:mw]),
                 reads=[bw[i]], writes=[bwb[i]])
            if half < KC:
                c.op(c.pool, lambda: nc.gpsimd.tensor_copy(out=wbf[i][:, half:KC, 0:mw], in_=wst[i][:, half:KC, 0:mw]),
                     reads=[bw[i]], writes=[bwb[i]])
            odt = blk.get("odt", BF16)
            ost = ost_b[i] if odt == BF16 else ost_f[i]
            if blk.get("pm") is not None:
                src, pfunc = blk["pm"]
                c.dma(c.sp, pmb[i][0:mw, :], src, writes=[bpmb[i]])
                c.op(c.act, lambda: nc.scalar.activation(out=pmf[i][0:mw, :], in_=pmb[i][0:mw, :], func=pfunc),
                     reads=[bpmb[i]], writes=[bpm[i]])
            for (t0, nt) in tts:
                j = npp % 4
                npp += 1

                def fm():
                    for k in range(KC):
                        ins = nc.tensor.matmul(pss[j][0:mw, 0:nt], lhsT=wbf[i][:, k, 0:mw], rhs=actT[:, k, t0:t0 + nt],
                                               start=(k == 0), stop=(k == KC - 1))
                    return ins
                c.op(c.pe, fm, reads=[bwb[i], bact], writes=[bp[j]])
                kw = {}
                if blk.get("bias") is not None:
                    kw["bias"] = blk["bias"]
                if blk.get("pm") is None:
                    c.op(c.act, lambda: nc.scalar.activation(out=ost[0:mw, t0:t0 + nt], in_=pss[j][0:mw, 0:nt],
                                                             func=blk.get("func", AF.Copy), scale=blk.get("scale", 1.0), **kw),
                         reads=[bp[j]], writes=[bo[i]])
                else:
                    jj = npp % 2
                    c.op(c.act, lambda: nc.scalar.activation(out=evf[jj][0:mw, 0:nt], in_=pss[j][0:mw, 0:nt],
                                                             func=blk.get("func", AF.Copy), scale=blk.get("scale", 1.0), **kw),
                         reads=[bp[j]], writes=[bev[jj]])
                    c.op(c.dve, lambda: nc.vector.tensor_tensor(out=ost[0:mw, t0:t0 + nt], in0=evf[jj][0:mw, 0:nt],
                                                                in1=pmf[i][0:mw, t0:t0 + nt], op=ALU.mult),
                         reads=[bev[jj], bpm[i]], writes=[bo[i]])
            c.dma(c.pool, blk["out"], ost[0:mw, :], reads=[bo[i]], writes=[self.B(blk.get("obuf", "gemm_out"))])

    def load_actT(self, ph, src, KC, name="actT"):
        c = self.c
        t = ph.sb([128, KC, self.TS], BF16, name)
        v = src.rearrange("(kc p) t -> p kc t", p=128)
        step = max(1, KC // 4)
        for k0 in range(0, KC, step):
            c.dma(c.sp, t[:, k0:k0 + step, :], v[:, k0:k0 + step, :], writes=[self.B("actres")])
        return t

    def ph_inproj(self, col_ranges, with_gates):
        nc, c = self.nc, self.c
        with Phase(self, "g2") as ph:
            actT = self.load_actT(ph, self.xnT, 32)
            blocks = []
            for (c0, c1) in col_ranges:
                cc = c0
                while cc < c1:
                    lim = c1
                    for b in (O_DT, O_Q, O_KR, O_MG):
                        if cc < b < lim:
                            lim = b
                    mw = min(128, lim - cc)
                    if cc == O_DT:
                        blocks.append(dict(W=self.I("w_in")[:, cc:cc + mw], mw=mw, out=self.dtT[:, :], odt=F32))
                    else:
                        blocks.append(dict(W=self.I("w_in")[:, cc:cc + mw], mw=mw, out=self.projT[cc:cc + mw, :]))
                    cc += mw
            if with_gates:
                for i in range(3):
                    for m in range(32):
                        blocks.append(dict(W=self.I("w_gate")[i, :, m * 128:(m + 1) * 128], mw=128,
                                           out=self.sigT[i * D + m * 128:i * D + (m + 1) * 128, :], func=AF.Sigmoid))
            self.gemm_A(ph, actT, 32, blocks, "g2")


    def TB(self, ph, shape, dt, name, psum=False):
        t = ph.ps(shape, dt, name) if psum else ph.sb(shape, dt, name)
        b = self.B(f"{ph.name}_{name}_{ph.n}")
        b.excl = psum
        return t, b

    def ph_conv(self):
        nc, c, TS = self.nc, self.c, self.TS
        with Phase(self, "cv") as ph:
            xin = [self.TB(ph, [128, TS + 3], BF16, f"xin{i}") for i in range(2)]
            acc = [self.TB(ph, [128, TS], F32, f"acc{i}") for i in range(2)]
            ot = [self.TB(ph, [128, TS], BF16, f"ot{i}") for i in range(2)]
            for i in range(2):
                c.op(c.pool, lambda: nc.gpsimd.memset(xin[i][0][:, 0:3], 0.0), writes=[xin[i][1]])
            for kc in range(24):
                i = kc % 2
                x, bx = xin[i]
                a, ba = acc[i]
                o, bo = ot[i]
                c.dma(c.sp, x[:, 3:3 + TS], self.projT[O_XBC + kc * 128:O_XBC + (kc + 1) * 128, :], writes=[bx])
                w = lambda k: self.vecs[:, V_CW + kc * 4 + k:V_CW + kc * 4 + k + 1]

                fns = [lambda: nc.vector.tensor_scalar(out=a[:], in0=x[:, 0:TS], scalar1=w(0), scalar2=None, op0=ALU.mult)]
                for k in range(1, 4):
                    fns.append(lambda k=k: nc.vector.scalar_tensor_tensor(out=a[:], in0=x[:, k:k + TS], scalar=w(k), in1=a[:],
                                                                          op0=ALU.mult, op1=ALU.add))
                c.chain(c.dve, fns, reads=[bx], writes=[ba])
                c.op(c.act, lambda: nc.scalar.activation(out=o[:], in_=a[:], func=AF.Silu,
                                                         bias=self.vecs[:, V_CB + kc:V_CB + kc + 1]),
                     reads=[ba], writes=[bo])
                c.dma(c.pool, self.xbcT[kc * 128:(kc + 1) * 128, :], o[:], reads=[bo], writes=[self.B("xbcT")])

    def ph_ssd(self, with_output):
        nc, c, TS, SEG, NSEG = self.nc, self.c, self.TS, self.SEG, self.NSEG
        with Phase(self, "sd") as ph:
            xbc = self.load_actT(ph, self.xbcT, 24, "xbc")
            bxbc = self.B("actres")
            if with_output:
                zc = [self.TB(ph, [128, 16, 64], BF16, f"zc{i}") for i in range(2)]
                zv = self.projT[0:BW, :].rearrange("(kc p) t -> p kc t", p=128)
            dts, bdts = self.TB(ph, [32, TS], F32, "dts")
            negA, bnA = self.TB(ph, [128, 32], F32, "negA")
            c.dma(c.sp, dts[:], self.dtT[:, :], writes=[bdts])
            c.dma(c.sp, negA[:], self.I("rows")[0:1, 2 * D:2 * D + 32].partition_broadcast(128), writes=[bnA])

            c.chain(c.act, [
                lambda: nc.scalar.activation(out=dts[:], in_=dts[:], func=AF.Exp, bias=self.vecs[0:32, V_DTB:V_DTB + 1]),
                lambda: nc.scalar.activation(out=dts[:], in_=dts[:], func=AF.Ln, bias=self.onesf[0:32, 0:1]),
            ], reads=[bdts], writes=[bdts])
            c.op(c.act, lambda: nc.scalar.activation(out=negA[:], in_=negA[:], func=AF.Exp), reads=[bnA], writes=[bnA])
            c.op(c.dve, lambda: nc.vector.tensor_scalar(out=negA[:], in0=negA[:], scalar1=-1.0, scalar2=None, op0=ALU.mult),
                 reads=[bnA], writes=[bnA])
            c.op(c.dve, lambda: nc.vector.tensor_scalar(out=dts[:, 0:HALO], in0=dts[:, 0:HALO],
                                                        scalar1=self.cmeta[0:32, C_M0:C_M0 + 1], scalar2=None, op0=ALU.mult),
                 reads=[bdts], writes=[bdts])
            tri, btri = self.TB(ph, [64, 64], F32, "tri")
            t2, bt2 = self.TB(ph, [64, 64], F32, "t2")
            ones64 = self.onesf

            c.chain(c.pool, [
                lambda: nc.gpsimd.memset(tri[:], 1.0),
                lambda: nc.gpsimd.memset(t2[:], 1.0),
                lambda: nc.gpsimd.affine_select(out=tri[:], in_=tri[:], pattern=[[1, 64]], compare_op=ALU.is_ge, fill=0.0,
                                                base=0, channel_multiplier=-1),
                lambda: nc.gpsimd.affine_select(out=t2[:], in_=t2[:], pattern=[[-1, 64]], compare_op=ALU.is_gt, fill=0.0,
                                                base=0, channel_multiplier=1),
            ], writes=[btri, bt2])
            S, bS = self.TB(ph, [128, BW], F32, "S")
            Sb, bSb = self.TB(ph, [128, BW], BF16, "Sb")
            tacc, btacc = self.TB(ph, [128, 32], F32, "tacc")
            stmp, bstmp = self.TB(ph, [128, 512], F32, "stmp")
            c.op(c.dve, lambda: nc.vector.memset(S[:], 0.0), writes=[bS])
            c.op(c.dve, lambda: nc.vector.memset(tacc[:], 0.0), writes=[btacc])
            if with_output and NSEG > 1:
                Dall, bD = self.TB(ph, [128, NSEG * NH], F32, "Dall")
                if self.mode == "F":
                    for j in range(NSEG):
                        c.dma(c.sp, Dall[:, j * NH:(j + 1) * NH], self.D_g[j * 128:j * 128 + 1, :].partition_broadcast(128),
                              reads=[self.B("ssdg")], writes=[bD])
                else:
                    c.dma(c.sp, Dall[:], self.I("D_all")[0:1, :].partition_broadcast(128), writes=[bD])
                T, bT = self.TB(ph, [128, BW], F32, "T")
                Lj, bLj = self.TB(ph, [128, BW], F32, "Lj")
                c.op(c.dve, lambda: nc.vector.memset(T[:], 0.0), writes=[bT])
                for j in range(NSEG - 1):
                    Lsrc = self.L_g[j * 128:(j + 1) * 128, :] if self.mode == "F" else self.I("L_all")[j, :, :]
                    c.dma(c.sp, Lj[:], Lsrc, reads=[self.B("ssdg")], writes=[bLj])
                    Tv = T[:].rearrange("p (h d) -> p h d", d=HD)
                    c.op(c.dve, lambda: nc.vector.tensor_tensor(
                        out=Tv, in0=Tv, in1=Dall[:, j * NH:(j + 1) * NH].unsqueeze(2).to_broadcast([128, NH, HD]), op=ALU.mult),
                        reads=[bT, bD], writes=[bT])
                    c.op(c.dve, lambda: nc.vector.tensor_tensor(out=T[:], in0=T[:], in1=Lj[:], op=ALU.add),
                         reads=[bT, bLj], writes=[bT])
                    c.op(c.dve, lambda: nc.vector.scalar_tensor_tensor(out=S[:], in0=T[:], scalar=self.cmeta[:, C_OH + j:C_OH + j + 1],
                                                                       in1=S[:], op0=ALU.mult, op1=ALU.add),
                         reads=[bT, bS], writes=[bS])
            c.op(c.act, lambda: nc.scalar.copy(out=Sb[:], in_=S[:]), reads=[bS], writes=[bSb])
            pA, bpA = self.TB(ph, [128, 512], F32, "pA", psum=True)
            pX, bpX = self.TB(ph, [128, 1024], BF16, "pX", psum=True)
            pR, bpR = self.TB(ph, [128, 512], F32, "pR", psum=True)
            pY, bpY = self.TB(ph, [128, 512], F32, "pY", psum=True)
            pYo, bpYo = self.TB(ph, [128, 512], F32, "pYo", psum=True)
            pSt, bpSt = self.TB(ph, [128, 512], F32, "pSt", psum=True)
            pYT, bpYT = self.TB(ph, [128, 16, 64], F32, "pYT", psum=True)
            dtk, bdtk = self.TB(ph, [64, 32], F32, "dtk")
            dak, bdak = self.TB(ph, [64, 32], F32, "dak")
            acs, bacs = self.TB(ph, [64, 32], F32, "acs")
            ea, bea = self.TB(ph, [64, 32], F32, "ea")
            dte, bdte = self.TB(ph, [64, 32], F32, "dte")
            cdec, bcdec = self.TB(ph, [128, 32], F32, "cdec")
            xdt, bxdt = self.TB(ph, [64, 512], BF16, "xdt")
            xdtw, bxdtw = self.TB(ph, [64, 512], BF16, "xdtw")
            btok, bbtok = self.TB(ph, [64, 128], BF16, "btok")
            Xg, bXg = self.TB(ph, [64, 512], F32, "Xg")
            dec, bdec = self.TB(ph, [64, 512], F32, "dec")
            gm, bgm = self.TB(ph, [64, 64], F32, "gm")
            mt, bmt = self.TB(ph, [64, 512], BF16, "mt")
            ytmp, bytmp = self.TB(ph, [64, 512], F32, "ytmp")
            ytok, bytok = self.TB(ph, [64, BW], F32, "ytok")
            y1, by1 = self.TB(ph, [128, 16, 64], F32, "y1")
            sz, bsz = self.TB(ph, [128, 16, 64], F32, "sz")
            sq, bsq = self.TB(ph, [128, 16, 64], BF16, "sq")
            rs, brs = self.TB(ph, [128, 4, 64], F32, "rs")
            yo = [self.TB(ph, [128, 16, 64], BF16, f"yo{i}") for i in range(2)]
            brv = self.brT[0:BW, :].rearrange("(kc p) t -> p kc t", p=128) if with_output else None
            chunks = [(0, HALO)] + [(HALO + 64 * i, 64) for i in range(SEG // 64)]
            if SSD_STOP == 0:
                chunks = []
            for ic, (t0, L) in enumerate(chunks):
                c.op(c.pe, lambda: nc.tensor.transpose(pA[0:L, 0:32], dts[0:32, t0:t0 + L], self.identf[0:32, 0:32]),
                     reads=[bdts], writes=[bpA])
                c.op(c.act, lambda: nc.scalar.copy(out=dtk[0:L, :], in_=pA[0:L, 0:32]), reads=[bpA], writes=[bdtk])
                c.op(c.dve, lambda: nc.vector.tensor_tensor(out=dak[0:L, :], in0=dtk[0:L, :], in1=negA[0:L, :], op=ALU.mult),
                     reads=[bdtk, bnA], writes=[bdak])

                def fcs():
                    nc.tensor.matmul(pA[0:L, 32:64], lhsT=tri[0:L, 0:L], rhs=dak[0:L, :], start=True, stop=True)
                    nc.tensor.matmul(pA[0:L, 64:96], lhsT=ones64[0:L, 0:L], rhs=dak[0:L, :], start=True, stop=True)
                    return nc.tensor.matmul(pA[0:128, 96:128], lhsT=ones64[0:L, 0:128], rhs=dak[0:L, :], start=True, stop=True)
                c.op(c.pe, fcs, reads=[bdak, btri], writes=[bpA])
                c.op(c.act, lambda: nc.scalar.copy(out=acs[0:L, :], in_=pA[0:L, 32:64]), reads=[bpA], writes=[bacs])
                c.op(c.act, lambda: nc.scalar.activation(out=ea[0:L, :], in_=pA[0:L, 32:64], func=AF.Exp), reads=[bpA], writes=[bea])
                c.op(c.dve, lambda: nc.vector.tensor_tensor(out=dte[0:L, :], in0=pA[0:L, 64:96], in1=acs[0:L, :], op=ALU.subtract),
                     reads=[bpA, bacs], writes=[bdte])
                c.op(c.act, lambda: nc.scalar.activation(out=dte[0:L, :], in_=dte[0:L, :], func=AF.Exp), reads=[bdte], writes=[bdte])
                c.op(c.act, lambda: nc.scalar.activation(out=cdec[:], in_=pA[0:128, 96:128], func=AF.Exp), reads=[bpA], writes=[bcdec])
                c.op(c.dve, lambda: nc.vector.tensor_tensor(out=tacc[:], in0=tacc[:], in1=pA[0:128, 96:128], op=ALU.add),
                     reads=[bpA, btacc], writes=[btacc])
                if SSD_STOP <= 1:
                    continue
                for g in range(NG):
                    hs = slice(8 * g, 8 * g + 8)
                    gs = slice(512 * g, 512 * (g + 1))
                    def ftr():
                        for q in range(4):
                            nc.tensor.transpose(pX[0:L, q * 128:(q + 1) * 128], xbc[:, 4 * g + q, t0:t0 + L], self.ident[:, :])
                        return nc.tensor.transpose(pX[0:L, 512:640], xbc[:, 16 + g, t0:t0 + L], self.ident[:, :])
                    c.op(c.pe, ftr, reads=[bxbc], writes=[bpX])
                    if SSD_STOP <= 1.1:
                        continue
                    bc8 = lambda t: t[0:L, hs].unsqueeze(2).to_broadcast([L, 8, HD])
                    v3 = lambda t: t[0:L, 0:512].rearrange("p (h d) -> p h d", d=HD)
                    c.op(c.dve, lambda: nc.vector.tensor_tensor(out=v3(xdt), in0=v3(pX), in1=bc8(dtk), op=ALU.mult),
                         reads=[bpX, bdtk], writes=[bxdt])
                    c.op(c.act, lambda: nc.scalar.copy(out=btok[0:L, :], in_=pX[0:L, 512:640]), reads=[bpX], writes=[bbtok])
                    c.op(c.dve, lambda: nc.vector.tensor_tensor(out=v3(xdtw), in0=v3(xdt), in1=bc8(dte), op=ALU.mult),
                         reads=[bxdt, bdte], writes=[bxdtw])
                    if with_output and SSD_STOP > 2:
                        vL = lambda t: t[0:L, 0:8 * L].rearrange("p (h l) -> p h l", l=L)
                        c.op(c.dve, lambda: nc.vector.tensor_tensor(
                            out=vL(Xg), in0=dak[0:L, hs].unsqueeze(2).to_broadcast([L, 8, L]),
                            in1=tri[0:L, 0:L].unsqueeze(1).to_broadcast([L, 8, L]), op=ALU.mult),
                            reads=[bdak, btri], writes=[bXg])
                        c.op(c.pe, lambda: nc.tensor.matmul(pR[0:L, 0:8 * L], lhsT=t2[0:L, 0:L], rhs=Xg[0:L, 0:8 * L], start=True, stop=True),
                             reads=[bXg, bt2], writes=[bpR])
                        c.op(c.act, lambda: nc.scalar.activation(out=dec[0:L, 0:8 * L], in_=pR[0:L, 0:8 * L], func=AF.Exp),
                             reads=[bpR], writes=[bdec])
                        c.op(c.pe, lambda: nc.tensor.matmul(pA[0:L, 128:128 + L], lhsT=xbc[:, 16 + g, t0:t0 + L], rhs=xbc[:, 20 + g, t0:t0 + L],
                                                            start=True, stop=True),
                             reads=[bxbc], writes=[bpA])
                        c.op(c.dve, lambda: nc.vector.tensor_tensor(out=gm[0:L, 0:L], in0=pA[0:L, 128:128 + L], in1=tri[0:L, 0:L], op=ALU.mult),
                             reads=[bpA, btri], writes=[bgm])
                        c.op(c.dve, lambda: nc.vector.tensor_tensor(out=vL(mt), in0=vL(dec),
                                                                    in1=gm[0:L, 0:L].unsqueeze(1).to_broadcast([L, 8, L]), op=ALU.mult),
                             reads=[bdec, bgm], writes=[bmt])

                        def fy():
                            for h8 in range(8):
                                ins = nc.tensor.matmul(pY[0:L, h8 * HD:(h8 + 1) * HD], lhsT=mt[0:L, h8 * L:(h8 + 1) * L],
                                                       rhs=xdt[0:L, h8 * HD:(h8 + 1) * HD], start=True, stop=True)
                            return ins
                        c.op(c.pe, fy, reads=[bmt, bxdt], writes=[bpY])
                        c.op(c.pe, lambda: nc.tensor.matmul(pYo[0:L, :], lhsT=xbc[:, 20 + g, t0:t0 + L], rhs=Sb[:, gs], start=True, stop=True),
                             reads=[bxbc, bSb], writes=[bpYo])
                        c.op(c.dve, lambda: nc.vector.tensor_tensor(out=v3(ytmp), in0=v3(pYo), in1=bc8(ea), op=ALU.mult),
                             reads=[bpYo, bea], writes=[bytmp])
                        c.op(c.dve, lambda: nc.vector.tensor_tensor(out=ytok[0:L, gs], in0=ytmp[0:L, :], in1=pY[0:L, :], op=ALU.add),
                             reads=[bytmp, bpY], writes=[bytok])
                    if SSD_STOP <= 1.2:
                        continue
                    c.op(c.pe, lambda: nc.tensor.matmul(pSt[:, :], lhsT=btok[0:L, :], rhs=xdtw[0:L, :], start=True, stop=True),
                         reads=[bbtok, bxdtw], writes=[bpSt])
                    Sv = S[:, gs].rearrange("p (h d) -> p h d", d=HD)
                    c.op(c.dve, lambda: nc.vector.tensor_tensor(out=stmp[:].rearrange("p (h d) -> p h d", d=HD), in0=Sv,
                                                                in1=cdec[:, hs].unsqueeze(2).to_broadcast([128, 8, HD]), op=ALU.mult),
                         reads=[bS, bcdec], writes=[bstmp])
                    c.op(c.dve, lambda: nc.vector.tensor_tensor(out=S[:, gs], in0=stmp[:], in1=pSt[:, :], op=ALU.add),
                         reads=[bstmp, bpSt], writes=[bS])
                    c.op(c.act, lambda: nc.scalar.copy(out=Sb[:, gs], in_=S[:, gs]), reads=[bS], writes=[bSb])
                if not with_output or SSD_STOP <= 3:
                    continue
                def fyt():
                    for q in range(16):
                        ins = nc.tensor.transpose(pYT[:, q, 0:L], ytok[0:L, q * 128:(q + 1) * 128], self.identf[0:L, 0:L])
                    return ins
                c.op(c.pe, fyt, reads=[bytok], writes=[bpYT])
                bq = lambda col: self.vecs[:, col:col + 16].unsqueeze(2).to_broadcast([128, 16, L])
                c.op(c.dve, lambda: nc.vector.tensor_tensor(out=y1[:, :, 0:L], in0=xbc[:, 0:16, t0:t0 + L], in1=bq(V_DS), op=ALU.mult),
                     reads=[bxbc], writes=[by1])
                c.op(c.dve, lambda: nc.vector.tensor_tensor(out=y1[:, :, 0:L], in0=y1[:, :, 0:L], in1=pYT[:, :, 0:L], op=ALU.add),
                     reads=[by1, bpYT], writes=[by1])
                zt_, bz = zc[ic % 2]
                c.dma(c.sp, zt_[:, :, 0:L], zv[:, :, t0:t0 + L], writes=[bz])
                c.op(c.act, lambda: nc.scalar.activation(out=sz[:, :, 0:L], in_=zt_[:, :, 0:L], func=AF.Silu), reads=[bz], writes=[bsz])
                c.op(c.dve, lambda: nc.vector.tensor_tensor(out=y1[:, :, 0:L], in0=y1[:, :, 0:L], in1=sz[:, :, 0:L], op=ALU.mult),
                     reads=[by1, bsz], writes=[by1])
                c.op(c.dve, lambda: nc.vector.tensor_tensor(out=sq[:, :, 0:L], in0=y1[:, :, 0:L], in1=y1[:, :, 0:L], op=ALU.mult),
                     reads=[by1], writes=[bsq])

                def fss():
                    for g in range(4):
                        for q in range(4):
                            ins = nc.tensor.matmul(pR[:, g * 64:g * 64 + L], lhsT=self.onesb[:, :], rhs=sq[:, 4 * g + q, 0:L],
                                                   start=(q == 0), stop=(q == 3))
                    return ins
                c.op(c.pe, fss, reads=[bsq], writes=[bpR])
                pRv = pR[:, 0:256].rearrange("p (g l) -> p g l", l=64)
                c.op(c.act, lambda: nc.scalar.activation(out=rs[:, :, 0:L], in_=pRv[:, :, 0:L], func=AF.Sqrt, scale=1.0 / 512,
                                                         bias=self.epsc[:, :]),
                     reads=[bpR], writes=[brs])
                c.op(c.dve, lambda: nc.vector.reciprocal(out=rs[:, :, 0:L], in_=rs[:, :, 0:L]), reads=[brs], writes=[brs])
                y1v = y1[:, :, 0:L].rearrange("p (g q) l -> p g q l", q=4)
                c.op(c.dve, lambda: nc.vector.tensor_tensor(out=y1v, in0=y1v, in1=rs[:, :, 0:L].unsqueeze(2).to_broadcast([128, 4, 4, L]),
                                                            op=ALU.mult),
                     reads=[by1, brs], writes=[by1])
                o, bo = yo[ic % 2]
                c.op(c.dve, lambda: nc.vector.tensor_tensor(out=o[:, :, 0:L], in0=y1[:, :, 0:L], in1=bq(V_SN), op=ALU.mult),
                     reads=[by1], writes=[bo])
                c.dma(c.pool, brv[:, :, t0:t0 + L], o[:, :, 0:L], reads=[bo], writes=[self.B("brT")])
            if not with_output:
                c.dma(c.pool, self.L_out[:, :], S[:], reads=[bS], writes=[self.B("ssdl")])
                c.op(c.act, lambda: nc.scalar.activation(out=tacc[:], in_=tacc[:], func=AF.Exp), reads=[btacc], writes=[btacc])
                c.dma(c.pool, self.D_out[:, :], tacc[:], reads=[btacc], writes=[self.B("ssdl")])


    def gemm_B(self, ph, actT, KC, NT, blocks, tag, odt):
        nc, c = self.nc, self.c
        NW = max(b["nw"] for b in blocks)
        wst, bw = self.TB(ph, [128, KC, NW], F32, "bwst")
        wbf = [self.TB(ph, [128, KC, NW], BF16, f"bwbf{i}") for i in range(2)]
        ost = [self.TB(ph, [128, NW], odt, f"bost{i}") for i in range(3)]
        pss = [self.TB(ph, [128, 512], F32, f"bps{i}", psum=True) for i in range(3)]
        bact = self.B("actres")
        n = 0
        for ib, blk in enumerate(blocks):
            nw = blk["nw"]
            wb, bwb = wbf[ib % 2]
            Wv = blk["W"].rearrange("(kc p) m -> p kc m", p=128)
            step = max(1, KC // 4)
            for k0 in range(0, KC, step):
                c.dma(c.sp, wst[:, k0:k0 + step, 0:nw], Wv[:, k0:k0 + step, :], writes=[bw])
            half = max(1, KC // 2)
            c.op(c.dve, lambda: nc.vector.tensor_copy(out=wb[:, 0:half, 0:nw], in_=wst[:, 0:half, 0:nw]), reads=[bw], writes=[bwb])
            if half < KC:
                c.op(c.pool, lambda: nc.gpsimd.tensor_copy(out=wb[:, half:KC, 0:nw], in_=wst[:, half:KC, 0:nw]), reads=[bw], writes=[bwb])
            t0 = 0
            while t0 < NT:
                nt = min(128, NT - t0)
                p, bp = pss[n % 3]
                o, bo = ost[n % 3]
                n += 1

                def fm():
                    for k in range(KC):
                        ins = nc.tensor.matmul(p[0:nt, 0:nw], lhsT=actT[:, k, t0:t0 + nt], rhs=wb[:, k, 0:nw],
                                               start=(k == 0), stop=(k == KC - 1))
                    return ins
                c.op(c.pe, fm, reads=[bwb, bact], writes=[bp])
                c.op(c.act, lambda: nc.scalar.copy(out=o[0:nt, 0:nw], in_=p[0:nt, 0:nw]), reads=[bp], writes=[bo])
                c.dma(c.pool, blk["out"][t0:t0 + nt, :], o[0:nt, 0:nw], reads=[bo], writes=[self.B("gemmB_out")])
                t0 += nt

    def fm_rmsnorm(self, ph, row0, KC, wcol, out_dram, tag):
        nc, c, TS = self.nc, self.c, self.TS
        x, bx = self.TB(ph, [128, KC, TS], BF16, tag + "x")
        sq, bsq = self.TB(ph, [128, KC, 512], BF16, tag + "sq")
        rs, brs = self.TB(ph, [128, 512], F32, tag + "rs")
        o, bo = self.TB(ph, [128, KC, TS], BF16, tag + "o")
        p, bp = self.TB(ph, [128, 512], F32, tag + "p", psum=True)
        c.dma(c.sp, x[:], self.projT[row0:row0 + KC * 128, :].rearrange("(kc p) t -> p kc t", p=128), writes=[bx])
        for (t0, nt) in ttiles(TS, 512):
            c.op(c.dve, lambda: nc.vector.tensor_tensor(out=sq[:, :, 0:nt], in0=x[:, :, t0:t0 + nt], in1=x[:, :, t0:t0 + nt], op=ALU.mult),
                 reads=[bx], writes=[bsq])

            def fs():
                for k in range(KC):
                    ins = nc.tensor.matmul(p[:, 0:nt], lhsT=self.onesb[:, :], rhs=sq[:, k, 0:nt], start=(k == 0), stop=(k == KC - 1))
                return ins
            c.op(c.pe, fs, reads=[bsq], writes=[bp])
            c.op(c.act, lambda: nc.scalar.activation(out=rs[:, 0:nt], in_=p[:, 0:nt], func=AF.Sqrt, scale=1.0 / (KC * 128),
                                                     bias=self.epsc[:, :]), reads=[bp], writes=[brs])
            c.op(c.dve, lambda: nc.vector.reciprocal(out=rs[:, 0:nt], in_=rs[:, 0:nt]), reads=[brs], writes=[brs])
            for k in range(KC):
                c.op(c.dve, lambda: nc.vector.scalar_tensor_tensor(out=o[:, k, t0:t0 + nt], in0=x[:, k, t0:t0 + nt],
                                                                   scalar=self.vecs[:, wcol + k:wcol + k + 1], in1=rs[:, 0:nt],
                                                                   op0=ALU.mult, op1=ALU.mult),
                     reads=[bx, brs], writes=[bo])
        c.dma(c.pool, out_dram.rearrange("(kc p) t -> p kc t", p=128), o[:], reads=[bo], writes=[self.B(tag + "out")])

    def make_rope(self, ph):
        nc, c, TS = self.nc, self.c, self.TS
        ang, ba = self.TB(ph, [64, TS], F32, "ang")
        self.cos2, self.bcos = self.TB(ph, [64, TS], F32, "cos2")
        self.sin2, self.bsin = self.TB(ph, [64, TS], F32, "sin2")
        fr, bfr = self.TB(ph, [64, 1], F32, "fr")
        rmf, brm = self.TB(ph, [64, 64], F32, "rmf")
        self.rm, self.brm = self.TB(ph, [64, 64], BF16, "rm")
        pi = float(np.pi)
        negpi, bnp = self.TB(ph, [64, 1], F32, "negpi")

        kf, bkf = self.TB(ph, [64, TS], F32, "kf")
        ki, bki = self.TB(ph, [64, TS], mybir.dt.int32, "ki")
        wr, bwr = self.TB(ph, [64, TS], F32, "wr")

        c.chain(c.pool, [
            lambda: nc.gpsimd.iota(ang[:], pattern=[[1, TS]], base=0, channel_multiplier=0, allow_small_or_imprecise_dtypes=True),
            lambda: nc.gpsimd.memset(rmf[:], 0.0),
            lambda: nc.gpsimd.affine_select(out=rmf[:], in_=rmf[:], pattern=[[1, 64]], compare_op=ALU.not_equal, fill=1.0, base=-32, channel_multiplier=-1),
            lambda: nc.gpsimd.affine_select(out=rmf[:], in_=rmf[:], pattern=[[-1, 64]], compare_op=ALU.not_equal, fill=-1.0, base=-32, channel_multiplier=1),
            lambda: nc.gpsimd.tensor_copy(out=self.rm[:], in_=rmf[:]),
        ], writes=[ba, brm, self.brm])
        c.op(c.dve, lambda: nc.vector.tensor_scalar(out=ang[:], in0=ang[:], scalar1=self.cmeta[0:64, C_POS:C_POS + 1],
                                                    scalar2=self.vecs[0:64, V_IF:V_IF + 1], op0=ALU.add, op1=ALU.mult),
             reads=[ba], writes=[ba])
        C1 = 6.28125
        C2 = float(2 * np.pi - C1)
        for (dst, bdst, shift) in ((self.sin2, self.bsin, 0.0), (self.cos2, self.bcos, pi / 2)):
            c.chain(c.dve, [
                lambda: nc.vector.tensor_scalar(out=dst[:], in0=ang[:], scalar1=shift, scalar2=None, op0=ALU.add),
                lambda: nc.vector.tensor_scalar(out=kf[:], in0=dst[:], scalar1=1.0 / (2 * pi), scalar2=None, op0=ALU.mult),
                lambda: nc.vector.tensor_copy(out=ki[:], in_=kf[:]),
                lambda: nc.vector.tensor_copy(out=kf[:], in_=ki[:]),
                lambda: nc.vector.scalar_tensor_tensor(out=dst[:], in0=kf[:], scalar=-C1, in1=dst[:], op0=ALU.mult, op1=ALU.add),
                lambda: nc.vector.scalar_tensor_tensor(out=dst[:], in0=kf[:], scalar=-C2, in1=dst[:], op0=ALU.mult, op1=ALU.add),
                lambda: nc.vector.tensor_scalar(out=wr[:], in0=dst[:], scalar1=pi, scalar2=-2 * pi, op0=ALU.is_gt, op1=ALU.mult),
                lambda: nc.vector.tensor_tensor(out=dst[:], in0=dst[:], in1=wr[:], op=ALU.add),
                lambda: nc.vector.tensor_scalar(out=wr[:], in0=dst[:], scalar1=-pi, scalar2=2 * pi, op0=ALU.is_lt, op1=ALU.mult),
                lambda: nc.vector.tensor_tensor(out=dst[:], in0=dst[:], in1=wr[:], op=ALU.add),
                lambda: nc.vector.tensor_scalar(out=dst[:], in0=dst[:], scalar1=pi, scalar2=-pi, op0=ALU.min, op1=ALU.max),
            ], reads=[ba], writes=[bdst, bkf, bki, bwr])
            c.op(c.act, lambda: nc.scalar.activation(out=dst[:], in_=dst[:], func=AF.Sin), reads=[bdst], writes=[bdst])

    def rope_apply(self, ph, src, bsrc, dst, bdst, pr, bpr, t1, bt1, t2, bt2, t0, nt):
        nc, c = self.nc, self.c
        c.op(c.pe, lambda: nc.tensor.matmul(pr[0:64, 0:nt], lhsT=self.rm[:, :], rhs=src[0:64, t0:t0 + nt], start=True, stop=True),
             reads=[bsrc, self.brm], writes=[bpr])
        c.op(c.dve, lambda: nc.vector.tensor_tensor(out=t1[0:64, 0:nt], in0=src[0:64, t0:t0 + nt], in1=self.cos2[:, t0:t0 + nt], op=ALU.mult),
             reads=[bsrc, self.bcos], writes=[bt1])
        c.op(c.dve, lambda: nc.vector.tensor_tensor(out=t2[0:64, 0:nt], in0=pr[0:64, 0:nt], in1=self.sin2[:, t0:t0 + nt], op=ALU.mult),
             reads=[bpr, self.bsin], writes=[bt2])
        c.op(c.dve, lambda: nc.vector.tensor_tensor(out=dst[0:64, t0:t0 + nt], in0=t1[0:64, 0:nt], in1=t2[0:64, 0:nt], op=ALU.add),
             reads=[bt1, bt2], writes=[bdst])

    def ph_mla_prep(self, with_q):
        nc, c, TS = self.nc, self.c, self.TS
        with Phase(self, "mp") as ph:
            self.make_rope(ph)
            if with_q:
                self.S("qnT", "qnT", [QR, TS], BF16)
                self.fm_rmsnorm(ph, O_Q, 8, V_QN, self.qnT[:, :], "qn")
            self.fm_rmsnorm(ph, O_KV, 4, V_KN, self.kvx[0:KVR, :], "kn")
            kr, bkr = self.TB(ph, [64, TS], BF16, "kr")
            ko, bko = self.TB(ph, [64, TS], BF16, "ko")
            t1, bt1 = self.TB(ph, [64, 512], F32, "t1")
            t2, bt2 = self.TB(ph, [64, 512], F32, "t2")
            pr, bpr = self.TB(ph, [128, 512], F32, "pr", psum=True)
            c.dma(c.sp, kr[:], self.projT[O_KR:O_KR + 64, :], writes=[bkr])
            for (t0, nt) in ttiles(TS, 512):
                self.rope_apply(ph, kr, bkr, ko, bko, pr, bpr, t1, bt1, t2, bt2, t0, nt)
            c.dma(c.pool, self.kvx[KVR:KVR + 64, :], ko[:], reads=[bko], writes=[self.B("kvx")])

    def ph_mla_proj(self):
        nc, c, TS, SEG, NK = self.nc, self.c, self.TS, self.SEG, self.NK
        NKC, NKG = self.NKC, self.NKG
        sc = 1.0 / float(np.sqrt(NOPE + ROPE))
        self.S("qT", "qT", [MH * 192, TS], BF16)
        self.S("kT", "kT", [MH * 128, NKC], BF16)
        self.S("vTok", "vTok", [NKC, MH * 128], BF16)
        with Phase(self, "mq") as ph:
            actT = self.load_actT(ph, self.qnT, 8)
            blocks = []
            for h in range(MH):
                blocks.append(dict(W=self.I("w_q_b")[:, h * 192:h * 192 + 128], mw=128, out=self.qT[h * 192:h * 192 + 128, :], scale=sc))
                blocks.append(dict(W=self.I("w_q_b")[:, h * 192 + 128:h * 192 + 192], mw=64, out=self.qT[h * 192 + 128:h * 192 + 192, :], scale=sc))
            self.gemm_A(ph, actT, 8, blocks, "mq")
        with Phase(self, "mk") as ph:
            kvn, bk = ph.sb([128, 4, NKC], BF16, "kvnC"), self.B("actres")
            for k in range(4):
                for (c0, ncol, src) in self.kv_pieces(k):
                    c.dma(c.sp, kvn[:, k, c0:c0 + ncol], src, reads=[self.B("kvg")], writes=[bk])
            c.dma(c.sp, kvn[:, :, NKG:NKC], self.kvx[0:KVR, HALO:TS].rearrange("(kc p) t -> p kc t", p=128), writes=[bk])
            tts = []
            t = 0
            while t < NKC:
                tts.append((t, min(512, NKC - t)))
                t += 512
            blocks = [dict(W=self.I("w_kv_b")[:, h * 256:h * 256 + 128], mw=128, out=self.kT[h * 128:(h + 1) * 128, :]) for h in range(MH)]
            self.gemm_A(ph, kvn, 4, blocks, "mk", tts=tts, width=NKC)
            blocks = [dict(W=self.I("w_kv_b")[:, h * 256 + 128:h * 256 + 256], nw=128, out=self.vTok[:, h * 128:(h + 1) * 128]) for h in range(MH)]
            self.gemm_B(ph, kvn, 4, NKC, blocks, "mv", BF16)

    def ph_mla_attn(self):
        nc, c, TS, SEG, NK, NSEG = self.nc, self.c, self.TS, self.SEG, self.NK, self.NSEG
        NKC, NKG = self.NKC, self.NKG
        QT = min(512, SEG)
        with Phase(self, "at") as ph:
            self.make_rope(ph)
            kpe, bkpe = self.TB(ph, [64, NKC], BF16, "kpe")
            for (c0, ncol, src) in self.kv_pieces(4):
                c.dma(c.sp, kpe[:, c0:c0 + ncol], src, reads=[self.B("kvg")], writes=[bkpe])
            c.dma(c.sp, kpe[:, NKG:NKC], self.kvx[KVR:KVR + 64, HALO:TS], writes=[bkpe])
            nd = QT // 128
            masks = []
            for d_ in range(nd):
                m, bm = self.TB(ph, [128, QT], BF16, f"mask{d_}")

                fns = [lambda: nc.gpsimd.memset(m[:], 0.0)]
                for kh in range(2):
                    c0 = 64 * (2 * d_ + kh)
                    if c0 < QT:
                        fns.append(lambda kh=kh, c0=c0: nc.gpsimd.memset(m[64 * kh:64 * kh + 64, c0:QT], 1.0))
                c.chain(c.pool, fns, writes=[bm])
                masks.append((m, bm))
            qn = [self.TB(ph, [128, TS], BF16, f"qn{i}") for i in range(2)]
            qr = [self.TB(ph, [64, TS], BF16, f"qr{i}") for i in range(2)]
            qp, bqp = self.TB(ph, [64, TS], BF16, "qp")
            kt = [self.TB(ph, [128, NKC], BF16, f"kt{i}") for i in range(2)]
            NKT = (NKC - HALO) // 128
            vv = [self.TB(ph, [128, NKT, 128], BF16, f"vv{i}") for i in range(2)]
            vm = [self.TB(ph, [16, 128], BF16, f"vm{i}") for i in range(2)]
            gt = [self.TB(ph, [128, TS], BF16, f"gt{i}") for i in range(2)]
            gf, bgf = self.TB(ph, [128, 512], F32, "gf")
            t1, bt1 = self.TB(ph, [64, 512], F32, "t1")
            t2, bt2 = self.TB(ph, [64, 512], F32, "t2")
            pt = [self.TB(ph, [128, 512], BF16, f"pt{i}") for i in range(3)]
            rden, brden = self.TB(ph, [128, 512], F32, "rden")
            accD, baccD = self.TB(ph, [128, 512], F32, "accD")
            accP, baccP = self.TB(ph, [128, 512], F32, "accP")
            of, bof = self.TB(ph, [128, 512], F32, "of")
            ob = [self.TB(ph, [128, TS], BF16, f"ob{i}") for i in range(2)]
            pS = [self.TB(ph, [128, 512], F32, f"pS{i}", psum=True) for i in range(3)]
            pO, bpO = self.TB(ph, [128, 512], F32, "pO", psum=True)
            pD, bpD = self.TB(ph, [128, 512], F32, "pD", psum=True)
            pr, bpr = self.TB(ph, [128, 512], F32, "pr", psum=True)
            nS = 0
            for h in range(MH):
                i = h % 2
                (qn_, bqn), (qr_, bqr), (kt_, bkt), (vv_, bvv), (vm_, bvm), (gt_, bgt), (ob_, bob) = qn[i], qr[i], kt[i], vv[i], vm[i], gt[i], ob[i]
                c.dma(c.sp, qn_[:], self.qT[h * 192:h * 192 + 128, :], writes=[bqn])
                c.dma(c.sp, qr_[:], self.qT[h * 192 + 128:h * 192 + 192, :], writes=[bqr])
                c.dma(c.sp, kt_[:], self.kT[h * 128:(h + 1) * 128, :], writes=[bkt])
                c.dma(c.sp, vm_[:], self.vTok[0:HALO, h * 128:(h + 1) * 128], writes=[bvm])
                c.dma(c.sp, vv_[:], self.vTok[HALO:NKC, h * 128:(h + 1) * 128].rearrange("(kt p) v -> p kt v", p=128), writes=[bvv])
                c.dma(c.sp, gt_[:], self.projT[O_MG + h * 128:O_MG + (h + 1) * 128, :], writes=[bgt])
                for (t0, nt) in ttiles(TS, QT):
                    self.rope_apply(ph, qr_, bqr, qp, bqp, pr, bpr, t1, bt1, t2, bt2, t0, nt)
                for iq, (t0, nt) in enumerate(ttiles(TS, QT)):
                    kts = [(0, HALO, None, None, vm_[0:HALO, :])]
                    if iq > 0:
                        for j in range(NSEG - 1):
                            for ii in range(SEG // 128):
                                kti = j * (SEG // 128) + ii
                                kts.append((HALO + kti * 128, 128, self.cmeta[:, C_VIS + j:C_VIS + j + 1], None, vv_[:, kti, :]))
                        a = iq - 1
                        for ii in range(SEG // 128):
                            d_ = ii - a * nd
                            if d_ >= nd:
                                continue
                            kti = (NSEG - 1) * (SEG // 128) + ii
                            kts.append((NKG + ii * 128, 128, None, masks[d_] if d_ >= 0 else None, vv_[:, kti, :]))
                    c.op(c.dve, lambda: nc.vector.memset(accD[:, 0:nt], 0.0), writes=[baccD])
                    c.op(c.pool, lambda: nc.gpsimd.memset(accP[:, 0:nt], 0.0), writes=[baccP])
                    pendq = []
                    for ik, (k0, nk, bias, mask, vl) in enumerate(kts):
                        ps_, bps = pS[nS % 3]
                        pt_, bpt = pt[nS % 3]
                        nS += 1

                        def fs():
                            nc.tensor.matmul(ps_[0:nk, 0:nt], lhsT=kt_[:, k0:k0 + nk], rhs=qn_[:, t0:t0 + nt], start=True, stop=False)
                            return nc.tensor.matmul(ps_[0:nk, 0:nt], lhsT=kpe[:, k0:k0 + nk], rhs=qp[:, t0:t0 + nt], start=False, stop=True)
                        c.op(c.pe, fs, reads=[bkt, bqn, bkpe, bqp], writes=[bps])
                        kw = {} if bias is None else {"bias": bias[0:nk, :]}
                        c.op(c.act, lambda: nc.scalar.activation(out=pt_[0:nk, 0:nt], in_=ps_[0:nk, 0:nt], func=AF.Exp, **kw),
                             reads=[bps], writes=[bpt])
                        if mask is not None:
                            c.op(c.dve, lambda: nc.vector.tensor_tensor(out=pt_[0:nk, 0:nt], in0=pt_[0:nk, 0:nt], in1=mask[0][0:nk, 0:nt], op=ALU.mult),
                                 reads=[bpt, mask[1]], writes=[bpt])
                        if ik % 2 == 0:
                            c.op(c.dve, lambda: nc.vector.tensor_tensor(out=accD[0:nk, 0:nt], in0=accD[0:nk, 0:nt], in1=pt_[0:nk, 0:nt], op=ALU.add),
                                 reads=[bpt, baccD], writes=[baccD])
                        else:
                            c.op(c.pool, lambda: nc.gpsimd.tensor_tensor(out=accP[0:nk, 0:nt], in0=accP[0:nk, 0:nt], in1=pt_[0:nk, 0:nt], op=ALU.add),
                                 reads=[bpt, baccP], writes=[baccP])

                        def mk_fo(pt_=pt_, bpt=bpt, nk=nk, vl=vl, first=(ik == 0), last=(ik == len(kts) - 1)):
                            def fo():
                                return nc.tensor.matmul(pO[:, 0:nt], lhsT=vl, rhs=pt_[0:nk, 0:nt], start=first, stop=last)
                            return lambda: c.op(c.pe, fo, reads=[bpt, bvv, bvm], writes=[bpO])
                        pendq.append(mk_fo())
                        if len(pendq) > 2:
                            pendq.pop(0)()
                    for f_ in pendq:
                        f_()

                    def fden():
                        nc.tensor.matmul(pD[:, 0:nt], lhsT=self.onesf[:, :], rhs=accD[:, 0:nt], start=True, stop=False)
                        return nc.tensor.matmul(pD[:, 0:nt], lhsT=self.onesf[:, :], rhs=accP[:, 0:nt], start=False, stop=True)
                    c.op(c.pe, fden, reads=[baccD, baccP], writes=[bpD])
                    c.op(c.dve, lambda: nc.vector.reciprocal(out=rden[:, 0:nt], in_=pD[:, 0:nt]), reads=[bpD], writes=[brden])
                    c.op(c.dve, lambda: nc.vector.tensor_tensor(out=of[:, 0:nt], in0=pO[:, 0:nt], in1=rden[:, 0:nt], op=ALU.mult),
                         reads=[bpO, brden], writes=[bof])
                    c.op(c.act, lambda: nc.scalar.activation(out=gf[:, 0:nt], in_=gt_[:, t0:t0 + nt], func=AF.Silu), reads=[bgt], writes=[bgf])
                    c.op(c.dve, lambda: nc.vector.tensor_tensor(out=ob_[:, t0:t0 + nt], in0=of[:, 0:nt], in1=gf[:, 0:nt], op=ALU.mult),
                         reads=[bof, bgf], writes=[bob])
                c.dma(c.pool, self.brT[BW + h * 128:BW + (h + 1) * 128, :], ob_[:], reads=[bob], writes=[self.B("brT")])


    def ph_pool(self):
        nc, c, TS = self.nc, self.c, self.TS
        with Phase(self, "pl") as ph:
            mixed = ph.sb([128, 16, TS], BF16, "mixed")
            bmx = self.B("actres")
            ic, bic = self.TB(ph, [128, 4, HALO], F32, "ic")

            fns = []
            for g in range(4):
                w = 2 ** (g + 1)
                fns.append(lambda g=g, w=w: nc.gpsimd.memset(ic[:, g, :], 1.0 / w))
                for t in range(w - 1):
                    fns.append(lambda g=g, t=t: nc.gpsimd.memset(ic[:, g, t:t + 1], 1.0 / (t + 1)))
            c.chain(c.pool, fns, writes=[bic])
            ub = [self.TB(ph, [128, TS], BF16, f"ub{i}") for i in range(2)]
            uf, buf_ = self.TB(ph, [128, TS], F32, "uf")
            s0, bs0 = self.TB(ph, [128, TS], F32, "s0")
            s1, bs1 = self.TB(ph, [128, TS], F32, "s1")
            th, bth = self.TB(ph, [128, HALO], F32, "th")
            for q in range(16):
                g = q // 4
                w = 2 ** (g + 1)
                u, bu = ub[q % 2]
                c.dma(c.sp, u[:], self.projT[O_PU + q * 128:O_PU + (q + 1) * 128, :], writes=[bu])
                c.op(c.act, lambda: nc.scalar.copy(out=uf[:], in_=u[:]), reads=[bu], writes=[buf_])
                cur, bcur, nxt, bnxt = uf, buf_, s0, bs0
                step = 1
                while step < w:
                    def fw():
                        nc.vector.tensor_copy(out=nxt[:, 0:step], in_=cur[:, 0:step])
                        return nc.vector.tensor_tensor(out=nxt[:, step:TS], in0=cur[:, step:TS], in1=cur[:, 0:TS - step], op=ALU.add)
                    c.op(c.dve, fw, reads=[bcur], writes=[bnxt])
                    if nxt is s0:
                        cur, bcur, nxt, bnxt = s0, bs0, s1, bs1
                    else:
                        cur, bcur, nxt, bnxt = s1, bs1, s0, bs0
                    step *= 2

                c.op(c.dve, lambda: nc.vector.scalar_tensor_tensor(out=mixed[:, q, HALO:TS], in0=cur[:, HALO:TS], scalar=1.0 / w,
                                                                   in1=uf[:, HALO:TS], op0=ALU.mult, op1=ALU.subtract),
                     reads=[bcur, buf_], writes=[bmx])
                c.op(c.dve, lambda: nc.vector.tensor_tensor(out=th[:], in0=cur[:, 0:HALO], in1=ic[:, g, :], op=ALU.mult),
                     reads=[bcur, bic], writes=[bth])
                c.op(c.dve, lambda: nc.vector.tensor_tensor(out=mixed[:, q, 0:HALO], in0=th[:], in1=uf[:, 0:HALO], op=ALU.subtract),
                     reads=[bth, buf_], writes=[bmx])
            for g in range(4):
                with Phase(self, f"pg{g}") as ph2:
                    blocks = []
                    for m in range(4):
                        r0 = g * 512 + m * 128
                        blocks.append(dict(W=self.I("w_pool")[g, :, m * 128:(m + 1) * 128], mw=128, out=self.brT[2 * BW + r0:2 * BW + r0 + 128, :],
                                           func=AF.Identity, scale=self.vecs[:, V_PS + 4 * g + m:V_PS + 4 * g + m + 1],
                                           pm=(self.projT[O_PG + r0:O_PG + r0 + 128, :], AF.Silu)))
                    self.gemm_A(ph2, mixed[:, 4 * g:4 * g + 4, :], 4, blocks, f"pg{g}")

    def ph_branch(self):
        nc, c, TS = self.nc, self.c, self.TS
        self.S("brW", "brW", [3 * D, TS], BF16)
        for i in range(3):
            with Phase(self, f"bw{i}") as ph:
                actT = self.load_actT(ph, self.brT[i * BW:(i + 1) * BW, :], 16)
                blocks = [dict(W=self.I("w_branch")[i, :, m * 128:(m + 1) * 128], mw=128,
                               out=self.brW[i * D + m * 128:i * D + (m + 1) * 128, :],
                               pm=(self.sigT[i * D + m * 128:i * D + (m + 1) * 128, :], AF.Copy)) for m in range(32)]
                self.gemm_A(ph, actT, 16, blocks, f"bw{i}")
        with Phase(self, "mg") as ph:
            tl = [[self.TB(ph, [128, TS], BF16, f"m{i}_{j}") for j in range(3)] for i in range(2)]
            acc = [self.TB(ph, [128, TS], F32, f"macc{i}") for i in range(2)]
            mo = [self.TB(ph, [128, TS], BF16, f"mo{i}") for i in range(2)]
            for kc in range(32):
                i = kc % 2
                for j in range(3):
                    c.dma(c.sp, tl[i][j][0][:], self.brW[j * D + kc * 128:j * D + (kc + 1) * 128, :], writes=[tl[i][j][1]])
                a, ba = acc[i]
                o, bo = mo[i]
                c.op(c.dve, lambda: nc.vector.tensor_tensor(out=a[:], in0=tl[i][0][0][:], in1=tl[i][1][0][:], op=ALU.add),
                     reads=[tl[i][0][1], tl[i][1][1]], writes=[ba])
                c.op(c.dve, lambda: nc.vector.tensor_tensor(out=o[:], in0=a[:], in1=tl[i][2][0][:], op=ALU.add),
                     reads=[ba, tl[i][2][1]], writes=[bo])
                c.dma(c.pool, self.mergedT[kc * 128:(kc + 1) * 128, :], o[:], reads=[bo], writes=[self.B("mergedT")])

    def ph_out(self):
        nc, c, TS = self.nc, self.c, self.TS
        with Phase(self, "op") as ph:
            actT = self.load_actT(ph, self.mergedT, 32)
            blocks = [dict(W=self.I("w_out")[:, cb * 256:(cb + 1) * 256], nw=256, out=self.outF[:, cb * 256:(cb + 1) * 256]) for cb in range(16)]
            self.gemm_B(ph, actT, 32, TS, blocks, "op", F32)
        with Phase(self, "fn") as ph:
            postw, bpw = self.TB(ph, [128, D], F32, "postw")
            c.dma(c.sp, postw[:], self.I("rows")[0:1, D:2 * D].partition_broadcast(128), writes=[bpw])
            ot = [self.TB(ph, [128, D], F32, f"fo{i}") for i in range(2)]
            ht = [self.TB(ph, [128, D], F32, f"fh{i}") for i in range(2)]
            junk, bj = self.TB(ph, [128, D], BF16, "fjunk")
            ss = [self.TB(ph, [128, 1], F32, f"fss{i}") for i in range(2)]
            for it, (t0, nt) in enumerate(ttiles(TS, 128)):
                i = it % 2
                (o, bo), (hh, bh), (s_, bs) = ot[i], ht[i], ss[i]
                c.dma(c.sp, o[0:nt, :], self.outF[t0:t0 + nt, :], writes=[bo])
                c.dma(c.sp, hh[0:nt, :], self.hsrc()[t0:t0 + nt, :], reads=[self.B("hres")], writes=[bh])
                c.op(c.act, lambda: nc.scalar.activation(out=junk[0:nt, :], in_=o[0:nt, :], func=AF.Square, accum_out=s_[0:nt, :]),
                     reads=[bo], writes=[bj, bs])
                c.op(c.act, lambda: nc.scalar.activation(out=s_[0:nt, :], in_=s_[0:nt, :], func=AF.Sqrt, scale=1.0 / D, bias=self.epsc[0:nt, :]),
                     reads=[bs], writes=[bs])
                c.op(c.dve, lambda: nc.vector.reciprocal(out=s_[0:nt, :], in_=s_[0:nt, :]), reads=[bs], writes=[bs])
                c.op(c.dve, lambda: nc.vector.scalar_tensor_tensor(out=o[0:nt, :], in0=o[0:nt, :], scalar=s_[0:nt, 0:1], in1=postw[0:nt, :],
                                                                   op0=ALU.mult, op1=ALU.mult),
                     reads=[bo, bs, bpw], writes=[bo])
                c.op(c.dve, lambda: nc.vector.tensor_tensor(out=o[0:nt, :], in0=o[0:nt, :], in1=hh[0:nt, :], op=ALU.add),
                     reads=[bo, bh], writes=[bo])
                c.dma(c.pool, self.hdst()[t0:t0 + nt, :], o[0:nt, :], reads=[bo], writes=[self.B("hdst")])


    def exchange_mid(self):
        c = self.c
        c.barrier()
        for k in range(5):
            r0 = k * 128
            nr = 128 if k < 4 else ROPE
            c.dma(c.sp, self.kvc[k][:, :], self.kvx[r0:r0 + nr, :], writes=[self.B(f"kvc{k}")])
        c.barrier()
        for k in range(5):
            c.coll("AllGather", self.groups, self.kvc[k], self.kvc_g[k], writes=[self.B("kvg")])
        c.coll("AllGather", self.groups, self.L_out, self.L_g, writes=[self.B("ssdg")])
        c.coll("AllGather", self.groups, self.D_out, self.D_g, writes=[self.B("ssdg")])
        c.barrier()

    def ph_halo_exchange(self):
        nc, c, TS, NSEG = self.nc, self.c, self.TS, self.NSEG
        with Phase(self, "hx") as ph:
            c.dma(c.sp, self.tail_loc[:, :], self.h1[TS - HALO:TS, :], writes=[self.B("tail")])
            c.barrier()
            c.coll("AllGather", self.groups, self.tail_loc, self.tails_g, writes=[self.B("tailg")])
            c.barrier()
            own, bown = self.TB(ph, [HALO, D], F32, "own")
            tl, btl = self.TB(ph, [HALO, NSEG, D], F32, "tl")
            c.dma(c.sp, own[:], self.h1[0:HALO, :], writes=[bown])
            c.dma(c.sp, tl[:], self.tails_g.rearrange("(j r) d -> r j d", r=HALO), writes=[btl])
            c.op(c.dve, lambda: nc.vector.tensor_scalar(out=own[:], in0=own[:], scalar1=self.cmeta[0:HALO, C_M0:C_M0 + 1], scalar2=None,
                                                        op0=ALU.mult), reads=[bown], writes=[bown])
            for j in range(NSEG - 1):
                c.op(c.dve, lambda: nc.vector.scalar_tensor_tensor(out=own[:], in0=tl[:, j, :], scalar=self.cmeta[0:HALO, C_OH + j:C_OH + j + 1],
                                                                   in1=own[:], op0=ALU.mult, op1=ALU.add),
                     reads=[btl, bown], writes=[bown])
            c.dma(c.pool, self.h1[0:HALO, :], own[:], reads=[bown], writes=[self.B("hdst")])


def build_program(SEG, NSEG, mode, dbg=False, phases=None):
    P = Prog(SEG, NSEG, mode, dbg)
    c = P.c
    with contextlib.ExitStack() as es:
        class _G:
            pass
        gph = Phase(P, "glob")
        gph.__enter__()
        P.load_consts(gph)
        phases = phases or (["norm", "inproj", "conv", "ssd", "mla", "pool", "out"] if mode == "B" else ["norm", "inproj", "conv", "ssd", "mla"])
        if "norm" in phases:
            P.ph_norm()
        if "inproj" in phases:
            if mode == "A":
                P.ph_inproj([(O_XBC, O_Q), (O_KV, O_MG)], False)
            else:
                P.ph_inproj([(0, IN_DIM)], True)
        if "conv" in phases:
            P.ph_conv()
        if "ssd" in phases:
            P.ph_ssd(mode == "B")
        if "mla" in phases:
            P.ph_mla_prep(mode == "B")
            if mode == "B":
                P.ph_mla_proj()
                P.ph_mla_attn()
        if "pool" in phases:
            P.ph_pool()
        if "out" in phases:
            P.ph_branch()
            P.ph_out()
        gph.__exit__(None, None, None)
    c.barrier()
    c.close()
    return P


def host_layer_inputs(inp, L):
    f = lambda a: np.ascontiguousarray(a, dtype=np.float32)
    vecs = np.zeros((128, NV), np.float32)
    cw = inp["conv_w"][L][:, 0, :]
    vecs[:, V_CW:V_CW + 96] = cw.T.reshape(24, 128, 4).transpose(1, 0, 2).reshape(128, 96)
    vecs[:, V_CB:V_CB + 24] = inp["conv_b"][L].reshape(24, 128).T
    vecs[:, V_DS:V_DS + 16] = np.repeat(inp["d_skip"][L], HD).reshape(16, 128).T
    vecs[:, V_SN:V_SN + 16] = inp["ssd_norm_w"][L].reshape(16, 128).T
    vecs[:, V_QN:V_QN + 8] = inp["q_norm_w"][L].reshape(8, 128).T
    vecs[:, V_KN:V_KN + 4] = inp["kv_norm_w"][L].reshape(4, 128).T
    vecs[:, V_PS:V_PS + 16] = inp["pool_scale"][L].reshape(16, 128).T
    vecs[0:32, V_DTB] = inp["dt_bias"][L]
    inv_freq = np.power(np.float32(10000.0), -np.arange(0, ROPE, 2, dtype=np.float32) / np.float32(ROPE)).astype(np.float32)
    vecs[0:32, V_IF] = inv_freq
    vecs[32:64, V_IF] = inv_freq
    rows = np.concatenate([inp["pre_norm_w"][L], inp["post_norm_w"][L], inp["a_log"][L]])[None, :]
    return {
        "vecs": vecs, "rows": f(rows), "w_in": f(inp["w_in"][L]), "w_gate": f(inp["w_gate"][L]),
        "w_branch": f(inp["w_branch"][L]), "w_out": f(inp["w_out"][L]), "w_q_b": f(inp["w_q_b"][L]),
        "w_kv_b": f(inp["w_kv_b"][L]), "w_pool": f(inp["w_pool"][L]),
    }


def host_cmeta(seg, NSEG, SEG):
    cm = np.zeros((128, 16), np.float32)
    cm[:, C_M0] = 1.0 if seg == 0 else 0.0
    for j in range(4):
        cm[:, C_VIS + j] = 0.0 if j < seg else NEG
        cm[:, C_OH + j] = 1.0 if (j + 1) == seg else 0.0
    cm[:, C_POS] = float(seg * SEG)
    return cm


SEG_FULL, NSEG_FULL, NBATCH = 2048, 4, 2
_PROGS = {}


def _prog(mode):
    if mode not in _PROGS:
        _PROGS[mode] = build_program(SEG_FULL, NSEG_FULL, mode)
    return _PROGS[mode]


def kernel_unfused(**inputs):
    x = np.asarray(inputs["x"], dtype=np.float32)
    meta = np.asarray(inputs["meta_tokens"], dtype=np.float32)
    params = {k: np.asarray(v) for k, v in inputs.items() if k not in ("x", "meta_tokens")}
    SEG, NSEG = SEG_FULL, NSEG_FULL
    TS = HALO + SEG
    ncores = NBATCH * NSEG
    hfull = np.concatenate([np.broadcast_to(meta[None], (NBATCH, HALO, D)), x], axis=1).astype(np.float32)
    cmetas = [host_cmeta(cid % NSEG, NSEG, SEG) for cid in range(ncores)]
    depth = params["w_in"].shape[0]
    for L in range(depth):
        lay = host_layer_inputs(params, L)
        hs = [np.ascontiguousarray(hfull[cid // NSEG, (cid % NSEG) * SEG:(cid % NSEG) * SEG + TS]) for cid in range(ncores)]
        PA = _prog("A")
        in_maps = []
        for cid in range(ncores):
            m = {}
            for name in PA.inputs:
                m[name] = hs[cid] if name == "h" else cmetas[cid] if name == "cmeta" else lay[name]
            in_maps.append(m)
        ra = run_bass_kernel_spmd(PA.nc, in_maps, core_ids=list(range(ncores))).results
        kv_all, L_all, D_all = [], [], []
        for b in range(NBATCH):
            rs = [ra[b * NSEG + s] for s in range(NSEG)]
            kv_all.append(np.ascontiguousarray(np.concatenate([np.asarray(rs[0]["kvx"])[:, :HALO]] +
                                                              [np.asarray(r_["kvx"])[:, HALO:] for r_ in rs], axis=1)))
            L_all.append(np.ascontiguousarray(np.stack([np.asarray(r_["L_out"]) for r_ in rs], axis=0)))
            D_all.append(np.ascontiguousarray(np.concatenate([np.asarray(r_["D_out"])[0] for r_ in rs])[None, :]))
        PB = _prog("B")
        in_maps = []
        for cid in range(ncores):
            b = cid // NSEG
            m = {}
            for name in PB.inputs:
                if name == "h":
                    m[name] = hs[cid]
                elif name == "cmeta":
                    m[name] = cmetas[cid]
                elif name == "kv_all":
                    m[name] = kv_all[b]
                elif name == "L_all":
                    m[name] = L_all[b]
                elif name == "D_all":
                    m[name] = D_all[b]
                else:
                    m[name] = lay[name]
            in_maps.append(m)
        rb = run_bass_kernel_spmd(PB.nc, in_maps, core_ids=list(range(ncores))).results
        for cid in range(ncores):
            b, s = cid // NSEG, cid % NSEG
            ho = np.asarray(rb[cid]["h_out"])
            hfull[b, HALO + s * SEG:HALO + (s + 1) * SEG] = ho[HALO:]
            if s == 0:
                hfull[b, 0:HALO] = ho[0:HALO]
    return np.ascontiguousarray(hfull[:, HALO:]).astype(np.float32)


def build_fused(SEG, NSEG, depth):
    P = Prog(SEG, NSEG, "F")
    gph = Phase(P, "glob")
    gph.__enter__()
    for L in range(depth):
        P.L = L
        P.h_src = None if L == 0 else P.h1
        P.h_dst = P.h1 if L < depth - 1 else P.h_out
        P.load_consts(gph)
        P.ph_norm()
        P.ph_inproj([(0, IN_DIM)], True)
        P.ph_conv()
        P.ph_ssd(False)
        P.ph_mla_prep(True)
        P.exchange_mid()
        P.ph_ssd(True)
        P.ph_mla_proj()
        P.ph_mla_attn()
        P.ph_pool()
        P.ph_branch()
        P.ph_out()
        if L < depth - 1:
            P.ph_halo_exchange()
    gph.__exit__(None, None, None)
    P.c.barrier()
    P.c.close()
    return P


def kernel(**inputs):
    x = np.asarray(inputs["x"], dtype=np.float32)
    meta = np.asarray(inputs["meta_tokens"], dtype=np.float32)
    params = {k: np.asarray(v) for k, v in inputs.items() if k not in ("x", "meta_tokens")}
    SEG, NSEG = SEG_FULL, NSEG_FULL
    TS = HALO + SEG
    ncores = NBATCH * NSEG
    depth = params["w_in"].shape[0]
    if "F" not in _PROGS:
        _PROGS["F"] = build_fused(SEG, NSEG, depth)
    P = _PROGS["F"]
    hfull = np.concatenate([np.broadcast_to(meta[None], (NBATCH, HALO, D)), x], axis=1).astype(np.float32)
    lays = [host_layer_inputs(params, L) for L in range(depth)]
    in_maps = []
    for cid in range(ncores):
        b, s = cid // NSEG, cid % NSEG
        m = {}
        for key in P.inputs:
            if key == "h":
                m[key] = np.ascontiguousarray(hfull[b, s * SEG:s * SEG + TS])
            elif key == "cmeta":
                m[key] = host_cmeta(s, NSEG, SEG)
            else:
                name, L = key.rsplit("_L", 1)
                m[key] = lays[int(L)][name]
        in_maps.append(m)
    res = run_bass_kernel_spmd(P.nc, in_maps, core_ids=list(range(ncores))).results
    out = np.empty((NBATCH, NSEG * SEG, D), np.float32)
    for cid in range(ncores):
        b, s = cid // NSEG, cid % NSEG
        out[b, s * SEG:(s + 1) * SEG] = np.asarray(res[cid]["h_out"])[HALO:]
    return out
```

```python
import contextlib
import numpy as np
import ml_dtypes
import concourse.bass as bass
import concourse.mybir as mybir
from concourse.bass_utils import run_bass_kernel_spmd

F32 = mybir.dt.float32
BF16 = mybir.dt.bfloat16
AF = mybir.ActivationFunctionType
ALU = mybir.AluOpType

D = 4096
BW = 2048
EPS = 1e-6
NH, HD, NG, NST, XBC = 32, 64, 4, 128, 3072
MH, NOPE, ROPE, VD, QR, KVR = 16, 128, 64, 128, 1024, 512
IN_DIM = 12896
O_Z, O_XBC, O_DT, O_Q, O_KV, O_KR, O_MG, O_PU, O_PG = 0, 2048, 5120, 5152, 6176, 6688, 6752, 8800, 10848
HALO = 16
NEG = -30000.0

SSD_STOP = 9.0
V_CW, V_CB, V_DS, V_SN, V_QN, V_KN, V_PS, V_DTB, V_IF, NV = 0, 96, 120, 136, 152, 160, 164, 180, 181, 182
C_M0, C_VIS, C_OH, C_POS = 0, 1, 5, 9


class Buf:
    __slots__ = ("name", "lw", "rd", "excl")

    def __init__(self, name="", excl=False):
        self.name = name
        self.lw = None
        self.rd = []
        self.excl = excl


class Eng:
    def __init__(self, name, h, sem):
        self.name, self.h, self.sem = name, h, sem
        self.cnt = 0
        self.known = {}

    def wait_ev(self, ev):
        sem, val, _ = ev
        if self.known.get(id(sem), 0) < val:
            self.h.wait_ge(sem, val)
            self.known[id(sem)] = val


class Ctx:
    def __init__(self, nc, n_dma_sems=8):
        self.nc = nc
        self.es = contextlib.ExitStack()
        mk = lambda n: self.es.enter_context(nc.semaphore(n))
        self.pe = Eng("pe", nc.tensor, mk("s_pe"))
        self.act = Eng("act", nc.scalar, mk("s_act"))
        self.dve = Eng("dve", nc.vector, mk("s_dve"))
        self.pool = Eng("pool", nc.gpsimd, mk("s_pool"))
        self.sp = Eng("sp", nc.sync, mk("s_sp"))
        self.engs = [self.pe, self.act, self.dve, self.pool, self.sp]
        self.dsems = {}
        for e in (self.sp, self.pool, self.act):
            self.dsems[e.name] = [[mk(f"d_{e.name}{i}"), 0] for i in range(n_dma_sems)]
        self.drr = {e.name: 0 for e in (self.sp, self.pool, self.act)}
        self.n_ops = 0
        self.cc_sem = mk("s_cc")
        self.cc_cnt = 0

    def close(self):
        self.es.close()

    def _deps(self, eng, reads, writes, same_raw):
        for r in reads:
            if r.lw is not None and (r.lw[2] != eng.name or same_raw):
                eng.wait_ev(r.lw)
        for w in writes:
            if w.lw is not None and (w.lw[2] != eng.name or same_raw):
                eng.wait_ev(w.lw)
            for ev in w.rd:
                if ev[2] != eng.name:
                    eng.wait_ev(ev)

    def _commit(self, ev, reads, writes):
        for r in reads:
            r.rd.append(ev)
            if len(r.rd) > 48:
                last = {}
                for e in r.rd:
                    if id(e[0]) not in last or last[id(e[0])][1] < e[1]:
                        last[id(e[0])] = e
                r.rd = list(last.values())
        for w in writes:
            w.lw = ev
            w.rd = []

    def op(self, eng, fn, reads=(), writes=()):
        ex = [r for r in reads if r.excl]
        if ex:
            writes = list(writes) + ex
            reads = [r for r in reads if not r.excl]
        self._deps(eng, reads, writes, same_raw=(eng is not self.pe))
        ins = fn()
        ins.then_inc(eng.sem, 1)
        eng.cnt += 1
        ev = (eng.sem, eng.cnt, eng.name)
        self._commit(ev, reads, writes)
        self.n_ops += 1
        return ev

    def chain(self, eng, fns, reads=(), writes=()):
        ev = None
        for fn in fns:
            ev = self.op(eng, fn, reads=reads, writes=writes)
        return ev

    def dma(self, eng, out, in_, reads=(), writes=(), **kw):
        pool = self.dsems[eng.name]
        i = self.drr[eng.name]
        self.drr[eng.name] = (i + 1) % len(pool)
        slot = pool[i]
        if slot[1] > 0:
            eng.wait_ev((slot[0], slot[1], "dma"))
        self._deps(eng, reads, writes, same_raw=True)
        eng.h.dma_start(out=out, in_=in_, **kw).then_inc(slot[0], 16)
        slot[1] += 16
        ev = (slot[0], slot[1], "dma_" + eng.name + str(i))
        self._commit(ev, reads, writes)
        self.n_ops += 1
        return ev

    def coll(self, kind, groups, src, dst, reads=(), writes=()):
        eng = self.pool
        if self.cc_cnt > 0:
            eng.wait_ev((self.cc_sem, self.cc_cnt, "coll"))
        self._deps(eng, reads, writes, same_raw=True)
        self.nc.gpsimd.collective_compute(kind, ALU.bypass, replica_groups=groups, ins=[src.opt()], outs=[dst.opt()]).then_inc(self.cc_sem)
        self.cc_cnt += 1
        ev = (self.cc_sem, self.cc_cnt, "coll")
        self._commit(ev, reads, writes)
        return ev

    def barrier(self):
        evs = []
        for e in self.engs:
            if e.cnt > 0:
                evs.append((e.sem, e.cnt, e.name))
        for name, pool in self.dsems.items():
            for s in pool:
                if s[1] > 0:
                    evs.append((s[0], s[1], "dma"))
        if self.cc_cnt > 0:
            evs.append((self.cc_sem, self.cc_cnt, "coll"))
        for e in self.engs:
            for ev in evs:
                if ev[0] is not e.sem:
                    e.wait_ev(ev)


class Phase:
    _seq = [0]

    def __init__(self, P, name):
        Phase._seq[0] += 1
        self.P, self.name = P, f"{name}x{Phase._seq[0]}"
        self.es = contextlib.ExitStack()
        self.n = 0

    def __enter__(self):
        self.P.c.barrier()
        return self

    def __exit__(self, *a):
        self.P.c.barrier()
        self.es.close()
        return False

    def sb(self, shape, dt, name=None):
        self.n += 1
        return self.es.enter_context(self.P.nc.sbuf_tensor(f"{self.name}_{name or 's'}{self.n}", list(shape), dt))

    def ps(self, shape, dt, name=None):
        self.n += 1
        return self.es.enter_context(self.P.nc.psum_tensor(f"{self.name}_{name or 'p'}{self.n}", list(shape), dt))


def ttiles(TS, n):
    out = [(0, HALO)]
    t = HALO
    while t < TS:
        m = min(n, TS - t)
        out.append((t, m))
        t += m
    return out


class Prog:
    def __init__(self, SEG, NSEG, mode="B", dbg=False):
        self.SEG, self.NSEG, self.mode, self.dbg = SEG, NSEG, mode, dbg
        self.TS = TS = HALO + SEG
        self.NK = HALO + NSEG * SEG
        self.NKG = HALO + (NSEG - 1) * SEG
        self.NKC = self.NKG + SEG
        nc = self.nc = bass.Bass("TRN2", target_bir_lowering=False)
        self.c = Ctx(nc)
        skind = "ExternalOutput" if dbg else "Internal"
        dt_ = lambda n, s, t, k: nc.dram_tensor(n, list(s), t, kind=k).ap()
        self._dt = dt_
        NK = self.NK
        self.ispec = {
            "h": ([TS, D], F32), "cmeta": ([128, 16], F32), "vecs": ([128, NV], F32), "rows": ([1, 2 * D + 32], F32),
            "w_in": ([D, IN_DIM], F32), "w_gate": ([3, D, D], F32), "w_branch": ([3, BW, D], F32),
            "w_out": ([D, D], F32), "w_q_b": ([QR, MH * (NOPE + ROPE)], F32), "w_kv_b": ([KVR, MH * (NOPE + VD)], F32),
            "w_pool": ([4, 512, 512], F32), "kv_all": ([KVR + ROPE, NK], BF16), "L_all": ([NSEG, 128, BW], F32),
            "D_all": ([1, NSEG * NH], F32),
        }
        self.inputs = {}
        self.L = 0
        self.per_layer = {"vecs", "rows", "w_in", "w_gate", "w_branch", "w_out", "w_q_b", "w_kv_b", "w_pool"}
        if mode == "F":
            self.h_out = dt_("h_out", [TS, D], F32, "ExternalOutput")
            self.h1 = dt_("h1", [TS, D], F32, "Internal")
            self.L_out = dt_("L_loc", [128, BW], F32, "Internal")
            self.D_out = dt_("D_loc", [128, NH], F32, "Internal")
            self.kvc = [dt_(f"kvc{k}", [128 if k < 4 else ROPE, TS], BF16, "Internal") for k in range(5)]
            self.kvc_g = [dt_(f"kvcg{k}", [NSEG * (128 if k < 4 else ROPE), TS], BF16, "Internal") for k in range(5)]
            self.L_g = dt_("L_g", [NSEG * 128, BW], F32, "Internal")
            self.D_g = dt_("D_g", [NSEG * 128, NH], F32, "Internal")
            self.tail_loc = dt_("tail_loc", [HALO, D], F32, "Internal")
            self.tails_g = dt_("tails_g", [NSEG * HALO, D], F32, "Internal")
            self.groups = [list(range(b * NSEG, (b + 1) * NSEG)) for b in range(2)]
        elif mode == "B":
            self.h_out = dt_("h_out", [TS, D], F32, "ExternalOutput")
        else:
            self.kvx_out = dt_("kvx", [KVR + ROPE, TS], BF16, "ExternalOutput")
            self.L_out = dt_("L_out", [128, BW], F32, "ExternalOutput")
            self.D_out = dt_("D_out", [128, NH], F32, "ExternalOutput")
        self.xnT = dt_("xnT", [D, TS], BF16, skind)
        self.projT = dt_("projT", [IN_DIM, TS], BF16, skind)
        self.dtT = dt_("dtT", [NH, TS], F32, skind)
        self.xbcT = dt_("xbcT", [XBC, TS], BF16, skind)
        self.kvx = dt_("kvx_s", [KVR + ROPE, TS], BF16, skind) if mode in ("B", "F") else self.kvx_out
        self.h_src = None
        self.h_dst = None
        if mode in ("B", "F"):
            self.sigT = dt_("sigT", [3 * D, TS], BF16, skind)
            self.brT = dt_("brT", [3 * BW, TS], BF16, skind)
            self.mergedT = dt_("mergedT", [D, TS], BF16, skind)
            self.outF = dt_("outF", [TS, D], F32, skind)
        self.bufs = {}

    def I(self, name):
        key = f"{name}_L{self.L}" if (self.mode == "F" and name in self.per_layer) else name
        if key not in self.inputs:
            shp, t = self.ispec[name]
            self.inputs[key] = self._dt(key, shp, t, "ExternalInput")
        return self.inputs[key]

    def S(self, attr, name, shape, dt):
        if not hasattr(self, attr) or getattr(self, attr) is None:
            setattr(self, attr, self._dt(name, shape, dt, "Internal"))
        return getattr(self, attr)

    def hsrc(self):
        return self.h_src if self.h_src is not None else self.I("h")

    def hdst(self):
        return self.h_dst if self.h_dst is not None else self.h_out

    def kv_pieces(self, k):
        SEG, NSEG, TS = self.SEG, self.NSEG, self.TS
        r0 = k * 128
        nr = 128 if k < 4 else ROPE
        if self.mode != "F":
            return [(0, self.NKG, self.I("kv_all")[r0:r0 + nr, 0:self.NKG])]
        g = self.kvc_g[k]
        out = [(0, HALO, g[0:nr, 0:HALO])]
        for j in range(NSEG - 1):
            out.append((HALO + j * SEG, SEG, g[j * nr:(j + 1) * nr, HALO:TS]))
        return out

    def B(self, name):
        if name not in self.bufs:
            self.bufs[name] = Buf(name)
        return self.bufs[name]

    def load_consts(self, ph):
        nc, c = self.nc, self.c
        self.cmeta = ph.sb([128, 16], F32, "cmeta")
        self.vecs = ph.sb([128, NV], F32, "vecs")
        bc = self.B("consts")
        c.dma(c.sp, self.cmeta[:], self.I("cmeta")[:, :], writes=[bc])
        c.dma(c.sp, self.vecs[:], self.I("vecs")[:, :], writes=[bc])
        self.identf = ph.sb([128, 128], F32, "identf")
        self.ident = ph.sb([128, 128], BF16, "ident")
        self.onesf = ph.sb([128, 128], F32, "onesf")
        self.onesb = ph.sb([128, 128], BF16, "onesb")
        self.epsc = ph.sb([128, 1], F32, "epsc")

        c.chain(c.pool, [
            lambda: nc.gpsimd.memset(self.identf[:], 0.0),
            lambda: nc.gpsimd.memset(self.onesf[:], 1.0),
            lambda: nc.gpsimd.memset(self.onesb[:], 1.0),
            lambda: nc.gpsimd.memset(self.epsc[:], EPS),
            lambda: nc.gpsimd.affine_select(out=self.identf[:], in_=self.identf[:], pattern=[[-1, 128]],
                                            compare_op=ALU.not_equal, fill=1.0, base=0, channel_multiplier=1),
            lambda: nc.gpsimd.tensor_copy(out=self.ident[:], in_=self.identf[:]),
        ], writes=[bc])
        c.barrier()

    def ph_norm(self):
        nc, c, TS = self.nc, self.c, self.TS
        with Phase(self, "n1") as ph:
            prew = ph.sb([128, D], F32, "prew")
            bpw = self.B("prew")
            c.dma(c.sp, prew[:], self.I("rows")[0:1, 0:D].partition_broadcast(128), writes=[bpw])
            ht = [ph.sb([128, D], F32, f"ht{i}") for i in range(2)]
            junk = ph.sb([128, D], BF16, "junk")
            xnb = [ph.sb([128, D], BF16, f"xnb{i}") for i in range(2)]
            ss = [ph.sb([128, 1], F32, f"ss{i}") for i in range(2)]
            xT = [ph.sb([128, 32, 128], BF16, f"xT{i}") for i in range(2)]
            pT = [ph.ps([128, 8, 128], BF16, f"pT{i}") for i in range(4)]
            bht = [self.B(f"n1ht{i}") for i in range(2)]
            bxn = [self.B(f"n1xn{i}") for i in range(2)]
            bss = [self.B(f"n1ss{i}") for i in range(2)]
            bxT = [self.B(f"n1xT{i}") for i in range(2)]
            bpT = [self.B(f"n1pT{i}") for i in range(4)]
            bj = self.B("n1junk")
            xnT_v = self.xnT.rearrange("(kc p) t -> p kc t", p=128)
            npt = 0
            for it, (t0, nt) in enumerate(ttiles(TS, 128)):
                i = it % 2
                c.dma(c.sp, ht[i][0:nt, :], self.hsrc()[t0:t0 + nt, :], reads=[self.B("hres")], writes=[bht[i]])
                c.op(c.act, lambda: nc.scalar.activation(out=junk[0:nt, :], in_=ht[i][0:nt, :], func=AF.Square,
                                                         accum_out=ss[i][0:nt, :]),
                     reads=[bht[i]], writes=[bj, bss[i]])

                c.op(c.act, lambda: nc.scalar.activation(out=ss[i][0:nt, :], in_=ss[i][0:nt, :], func=AF.Sqrt,
                                                         scale=1.0 / D, bias=self.epsc[0:nt, :]),
                     reads=[bss[i]], writes=[bss[i]])
                c.op(c.dve, lambda: nc.vector.reciprocal(out=ss[i][0:nt, :], in_=ss[i][0:nt, :]),
                     reads=[bss[i]], writes=[bss[i]])
                c.op(c.dve, lambda: nc.vector.scalar_tensor_tensor(out=xnb[i][0:nt, :], in0=ht[i][0:nt, :],
                                                                   scalar=ss[i][0:nt, 0:1], in1=prew[0:nt, :],
                                                                   op0=ALU.mult, op1=ALU.mult),
                     reads=[bht[i], bss[i], bpw], writes=[bxn[i]])
                for g in range(4):
                    j = npt % 4
                    npt += 1

                    def ft():
                        for k in range(8):
                            ins = nc.tensor.transpose(pT[j][:, k, 0:nt], xnb[i][0:nt, (g * 8 + k) * 128:(g * 8 + k + 1) * 128],
                                                      self.ident[0:nt, 0:nt])
                        return ins
                    c.op(c.pe, ft, reads=[bxn[i]], writes=[bpT[j]])
                    if g % 2 == 0:
                        c.op(c.act, lambda: nc.scalar.copy(out=xT[i][:, g * 8:(g + 1) * 8, 0:nt], in_=pT[j][:, :, 0:nt]),
                             reads=[bpT[j]], writes=[bxT[i]])
                    else:
                        c.op(c.dve, lambda: nc.vector.tensor_copy(out=xT[i][:, g * 8:(g + 1) * 8, 0:nt], in_=pT[j][:, :, 0:nt]),
                             reads=[bpT[j]], writes=[bxT[i]])
                c.dma(c.pool, xnT_v[:, :, t0:t0 + nt], xT[i][:, :, 0:nt], reads=[bxT[i]], writes=[self.B("xnT")])

    def gemm_A(self, ph, actT, KC, blocks, tag, n_tile=512, tts=None, width=None):
        nc, c, TS = self.nc, self.c, (width or self.TS)
        wst = [ph.sb([128, KC, 128], F32, f"wst{i}") for i in range(2)]
        wbf = [ph.sb([128, KC, 128], BF16, f"wbf{i}") for i in range(2)]
        ost_b = [ph.sb([128, TS], BF16, f"ostb{i}") for i in range(2)]
        ost_f = [ph.sb([128, TS], F32, f"ostf{i}") for i in range(2)] if any(b.get("odt", BF16) == F32 for b in blocks) else None
        pss = [ph.ps([128, 512], F32, f"ps{i}") for i in range(4)]
        bw = [self.B(f"{tag}wst{i}") for i in range(2)]
        bwb = [self.B(f"{tag}wbf{i}") for i in range(2)]
        bo = [self.B(f"{tag}ost{i}") for i in range(2)]
        bp = [self.B(f"{tag}ps{i}") for i in range(4)]
        bact = self.B("actres")
        if any(b.get("pm") is not None for b in blocks):
            pmb = [ph.sb([128, TS], BF16, f"pmb{i}") for i in range(2)]
            pmf = [ph.sb([128, TS], F32, f"pmf{i}") for i in range(2)]
            evf = [ph.sb([128, 512], F32, f"evf{i}") for i in range(2)]
            bpmb = [self.B(f"{tag}pmb{i}") for i in range(2)]
            bpm = [self.B(f"{tag}pm{i}") for i in range(2)]
            bev = [self.B(f"{tag}ev{i}") for i in range(2)]
        tts = tts or ttiles(TS, n_tile)
        npp = 0
        for ib, blk in enumerate(blocks):
            i = ib % 2
            mw = blk["mw"]
            Wv = blk["W"].rearrange("(kc p) m -> p kc m", p=128)
            c.dma(c.sp, wst[i][:, :, 0:mw], Wv, writes=[bw[i]])
            half = KC // 2 if KC >= 2 else KC
            c.op(c.dve, lambda: nc.vector.tensor_copy(out=wbf[i][:, 0:half, 0:mw], in_=wst[i][:, 0:half, 0:mw]),
                 reads=[bw[i]], writes=[bwb[i]])
            if half < KC:
                c.op(c.pool, lambda: nc.gpsimd.tensor_copy(out=wbf[i][:, half:KC, 0:mw], in_=wst[i][:, half:KC, 0:mw]),
                     reads=[bw[i]], writes=[bwb[i]])
            odt = blk.get("odt", BF16)
            ost = ost_b[i] if odt == BF16 else ost_f[i]
            if blk.get("pm") is not None:
                src, pfunc = blk["pm"]
                c.dma(c.sp, pmb[i][0:mw, :], src, writes=[bpmb[i]])
                c.op(c.act, lambda: nc.scalar.activation(out=pmf[i][0:mw, :], in_=pmb[i][0:mw, :], func=pfunc),
                     reads=[bpmb[i]], writes=[bpm[i]])
            for (t0, nt) in tts:
                j = npp % 4
                npp += 1

                def fm():
                    for k in range(KC):
                        ins = nc.tensor.matmul(pss[j][0:mw, 0:nt], lhsT=wbf[i][:, k, 0:mw], rhs=actT[:, k, t0:t0 + nt],
                                               start=(k == 0), stop=(k == KC - 1))
                    return ins
                c.op(c.pe, fm, reads=[bwb[i], bact], writes=[bp[j]])
                kw = {}
                if blk.get("bias") is not None:
                    kw["bias"] = blk["bias"]
                if blk.get("pm") is None:
                    c.op(c.act, lambda: nc.scalar.activation(out=ost[0:mw, t0:t0 + nt], in_=pss[j][0:mw, 0:nt],
                                                             func=blk.get("func", AF.Copy), scale=blk.get("scale", 1.0), **kw),
                         reads=[bp[j]], writes=[bo[i]])
                else:
                    jj = npp % 2
                    c.op(c.act, lambda: nc.scalar.activation(out=evf[jj][0:mw, 0:nt], in_=pss[j][0:mw, 0:nt],
                                                             func=blk.get("func", AF.Copy), scale=blk.get("scale", 1.0), **kw),
                         reads=[bp[j]], writes=[bev[jj]])
                    c.op(c.dve, lambda: nc.vector.tensor_tensor(out=ost[0:mw, t0:t0 + nt], in0=evf[jj][0:mw, 0:nt],
                                                                in1=pmf[i][0:mw, t0:t0 + nt], op=ALU.mult),
                         reads=[bev[jj], bpm[i]], writes=[bo[i]])
            c.dma(c.pool, blk["out"], ost[0:mw, :], reads=[bo[i]], writes=[self.B(blk.get("obuf", "gemm_out"))])

    def load_actT(self, ph, src, KC, name="actT"):
        c = self.c
        t = ph.sb([128, KC, self.TS], BF16, name)
        v = src.rearrange("(kc p) t -> p kc t", p=128)
        step = max(1, KC // 4)
        for k0 in range(0, KC, step):
            c.dma(c.sp, t[:, k0:k0 + step, :], v[:, k0:k0 + step, :], writes=[self.B("actres")])
        return t

    def ph_inproj(self, col_ranges, with_gates):
        nc, c = self.nc, self.c
        with Phase(self, "g2") as ph:
            actT = self.load_actT(ph, self.xnT, 32)
            blocks = []
            for (c0, c1) in col_ranges:
                cc = c0
                while cc < c1:
                    lim = c1
                    for b in (O_DT, O_Q, O_KR, O_MG):
                        if cc < b < lim:
                            lim = b
                    mw = min(128, lim - cc)
                    if cc == O_DT:
                        blocks.append(dict(W=self.I("w_in")[:, cc:cc + mw], mw=mw, out=self.dtT[:, :], odt=F32))
                    else:
                        blocks.append(dict(W=self.I("w_in")[:, cc:cc + mw], mw=mw, out=self.projT[cc:cc + mw, :]))
                    cc += mw
            if with_gates:
                for i in range(3):
                    for m in range(32):
                        blocks.append(dict(W=self.I("w_gate")[i, :, m * 128:(m + 1) * 128], mw=128,
                                           out=self.sigT[i * D + m * 128:i * D + (m + 1) * 128, :], func=AF.Sigmoid))
            self.gemm_A(ph, actT, 32, blocks, "g2")


    def TB(self, ph, shape, dt, name, psum=False):
        t = ph.ps(shape, dt, name) if psum else ph.sb(shape, dt, name)
        b = self.B(f"{ph.name}_{name}_{ph.n}")
        b.excl = psum
        return t, b

    def ph_conv(self):
        nc, c, TS = self.nc, self.c, self.TS
        with Phase(self, "cv") as ph:
            xin = [self.TB(ph, [128, TS + 3], BF16, f"xin{i}") for i in range(2)]
            acc = [self.TB(ph, [128, TS], F32, f"acc{i}") for i in range(2)]
            ot = [self.TB(ph, [128, TS], BF16, f"ot{i}") for i in range(2)]
            for i in range(2):
                c.op(c.pool, lambda: nc.gpsimd.memset(xin[i][0][:, 0:3], 0.0), writes=[xin[i][1]])
            for kc in range(24):
                i = kc % 2
                x, bx = xin[i]
                a, ba = acc[i]
                o, bo = ot[i]
                c.dma(c.sp, x[:, 3:3 + TS], self.projT[O_XBC + kc * 128:O_XBC + (kc + 1) * 128, :], writes=[bx])
                w = lambda k: self.vecs[:, V_CW + kc * 4 + k:V_CW + kc * 4 + k + 1]

                fns = [lambda: nc.vector.tensor_scalar(out=a[:], in0=x[:, 0:TS], scalar1=w(0), scalar2=None, op0=ALU.mult)]
                for k in range(1, 4):
                    fns.append(lambda k=k: nc.vector.scalar_tensor_tensor(out=a[:], in0=x[:, k:k + TS], scalar=w(k), in1=a[:],
                                                                          op0=ALU.mult, op1=ALU.add))
                c.chain(c.dve, fns, reads=[bx], writes=[ba])
                c.op(c.act, lambda: nc.scalar.activation(out=o[:], in_=a[:], func=AF.Silu,
                                                         bias=self.vecs[:, V_CB + kc:V_CB + kc + 1]),
                     reads=[ba], writes=[bo])
                c.dma(c.pool, self.xbcT[kc * 128:(kc + 1) * 128, :], o[:], reads=[bo], writes=[self.B("xbcT")])

    def ph_ssd(self, with_output):
        nc, c, TS, SEG, NSEG = self.nc, self.c, self.TS, self.SEG, self.NSEG
        with Phase(self, "sd") as ph:
            xbc = self.load_actT(ph, self.xbcT, 24, "xbc")
            bxbc = self.B("actres")
            if with_output:
                zc = [self.TB(ph, [128, 16, 64], BF16, f"zc{i}") for i in range(2)]
                zv = self.projT[0:BW, :].rearrange("(kc p) t -> p kc t", p=128)
            dts, bdts = self.TB(ph, [32, TS], F32, "dts")
            negA, bnA = self.TB(ph, [128, 32], F32, "negA")
            c.dma(c.sp, dts[:], self.dtT[:, :], writes=[bdts])
            c.dma(c.sp, negA[:], self.I("rows")[0:1, 2 * D:2 * D + 32].partition_broadcast(128), writes=[bnA])

            c.chain(c.act, [
                lambda: nc.scalar.activation(out=dts[:], in_=dts[:], func=AF.Exp, bias=self.vecs[0:32, V_DTB:V_DTB + 1]),
                lambda: nc.scalar.activation(out=dts[:], in_=dts[:], func=AF.Ln, bias=self.onesf[0:32, 0:1]),
            ], reads=[bdts], writes=[bdts])
            c.op(c.act, lambda: nc.scalar.activation(out=negA[:], in_=negA[:], func=AF.Exp), reads=[bnA], writes=[bnA])
            c.op(c.dve, lambda: nc.vector.tensor_scalar(out=negA[:], in0=negA[:], scalar1=-1.0, scalar2=None, op0=ALU.mult),
                 reads=[bnA], writes=[bnA])
            c.op(c.dve, lambda: nc.vector.tensor_scalar(out=dts[:, 0:HALO], in0=dts[:, 0:HALO],
                                                        scalar1=self.cmeta[0:32, C_M0:C_M0 + 1], scalar2=None, op0=ALU.mult),
                 reads=[bdts], writes=[bdts])
            tri, btri = self.TB(ph, [64, 64], F32, "tri")
            t2, bt2 = self.TB(ph, [64, 64], F32, "t2")
            ones64 = self.onesf

            c.chain(c.pool, [
                lambda: nc.gpsimd.memset(tri[:], 1.0),
                lambda: nc.gpsimd.memset(t2[:], 1.0),
                lambda: nc.gpsimd.affine_select(out=tri[:], in_=tri[:], pattern=[[1, 64]], compare_op=ALU.is_ge, fill=0.0,
                                                base=0, channel_multiplier=-1),
                lambda: nc.gpsimd.affine_select(out=t2[:], in_=t2[:], pattern=[[-1, 64]], compare_op=ALU.is_gt, fill=0.0,
                                                base=0, channel_multiplier=1),
            ], writes=[btri, bt2])
            S, bS = self.TB(ph, [128, BW], F32, "S")
            Sb, bSb = self.TB(ph, [128, BW], BF16, "Sb")
            tacc, btacc = self.TB(ph, [128, 32], F32, "tacc")
            stmp, bstmp = self.TB(ph, [128, 512], F32, "stmp")
            c.op(c.dve, lambda: nc.vector.memset(S[:], 0.0), writes=[bS])
            c.op(c.dve, lambda: nc.vector.memset(tacc[:], 0.0), writes=[btacc])
            if with_output and NSEG > 1:
                Dall, bD = self.TB(ph, [128, NSEG * NH], F32, "Dall")
                if self.mode == "F":
                    for j in range(NSEG):
                        c.dma(c.sp, Dall[:, j * NH:(j + 1) * NH], self.D_g[j * 128:j * 128 + 1, :].partition_broadcast(128),
                              reads=[self.B("ssdg")], writes=[bD])
                else:
                    c.dma(c.sp, Dall[:], self.I("D_all")[0:1, :].partition_broadcast(128), writes=[bD])
                T, bT = self.TB(ph, [128, BW], F32, "T")
                Lj, bLj = self.TB(ph, [128, BW], F32, "Lj")
                c.op(c.dve, lambda: nc.vector.memset(T[:], 0.0), writes=[bT])
                for j in range(NSEG - 1):
                    Lsrc = self.L_g[j * 128:(j + 1) * 128, :] if self.mode == "F" else self.I("L_all")[j, :, :]
                    c.dma(c.sp, Lj[:], Lsrc, reads=[self.B("ssdg")], writes=[bLj])
                    Tv = T[:].rearrange("p (h d) -> p h d", d=HD)
                    c.op(c.dve, lambda: nc.vector.tensor_tensor(
                        out=Tv, in0=Tv, in1=Dall[:, j * NH:(j + 1) * NH].unsqueeze(2).to_broadcast([128, NH, HD]), op=ALU.mult),
                        reads=[bT, bD], writes=[bT])
                    c.op(c.dve, lambda: nc.vector.tensor_tensor(out=T[:], in0=T[:], in1=Lj[:], op=ALU.add),
                         reads=[bT, bLj], writes=[bT])
                    c.op(c.dve, lambda: nc.vector.scalar_tensor_tensor(out=S[:], in0=T[:], scalar=self.cmeta[:, C_OH + j:C_OH + j + 1],
                                                                       in1=S[:], op0=ALU.mult, op1=ALU.add),
                         reads=[bT, bS], writes=[bS])
            c.op(c.act, lambda: nc.scalar.copy(out=Sb[:], in_=S[:]), reads=[bS], writes=[bSb])
            pA, bpA = self.TB(ph, [128, 512], F32, "pA", psum=True)
            pX, bpX = self.TB(ph, [128, 1024], BF16, "pX", psum=True)
            pR, bpR = self.TB(ph, [128, 512], F32, "pR", psum=True)
            pY, bpY = self.TB(ph, [128, 512], F32, "pY", psum=True)
            pYo, bpYo = self.TB(ph, [128, 512], F32, "pYo", psum=True)
            pSt, bpSt = self.TB(ph, [128, 512], F32, "pSt", psum=True)
            pYT, bpYT = self.TB(ph, [128, 16, 64], F32, "pYT", psum=True)
            dtk, bdtk = self.TB(ph, [64, 32], F32, "dtk")
            dak, bdak = self.TB(ph, [64, 32], F32, "dak")
            acs, bacs = self.TB(ph, [64, 32], F32, "acs")
            ea, bea = self.TB(ph, [64, 32], F32, "ea")
            dte, bdte = self.TB(ph, [64, 32], F32, "dte")
            cdec, bcdec = self.TB(ph, [128, 32], F32, "cdec")
            xdt, bxdt = self.TB(ph, [64, 512], BF16, "xdt")
            xdtw, bxdtw = self.TB(ph, [64, 512], BF16, "xdtw")
            btok, bbtok = self.TB(ph, [64, 128], BF16, "btok")
            Xg, bXg = self.TB(ph, [64, 512], F32, "Xg")
            dec, bdec = self.TB(ph, [64, 512], F32, "dec")
            gm, bgm = self.TB(ph, [64, 64], F32, "gm")
            mt, bmt = self.TB(ph, [64, 512], BF16, "mt")
            ytmp, bytmp = self.TB(ph, [64, 512], F32, "ytmp")
            ytok, bytok = self.TB(ph, [64, BW], F32, "ytok")
            y1, by1 = self.TB(ph, [128, 16, 64], F32, "y1")
            sz, bsz = self.TB(ph, [128, 16, 64], F32, "sz")
            sq, bsq = self.TB(ph, [128, 16, 64], BF16, "sq")
            rs, brs = self.TB(ph, [128, 4, 64], F32, "rs")
            yo = [self.TB(ph, [128, 16, 64], BF16, f"yo{i}") for i in range(2)]
            brv = self.brT[0:BW, :].rearrange("(kc p) t -> p kc t", p=128) if with_output else None
            chunks = [(0, HALO)] + [(HALO + 64 * i, 64) for i in range(SEG // 64)]
            if SSD_STOP == 0:
                chunks = []
            for ic, (t0, L) in enumerate(chunks):
                c.op(c.pe, lambda: nc.tensor.transpose(pA[0:L, 0:32], dts[0:32, t0:t0 + L], self.identf[0:32, 0:32]),
                     reads=[bdts], writes=[bpA])
                c.op(c.act, lambda: nc.scalar.copy(out=dtk[0:L, :], in_=pA[0:L, 0:32]), reads=[bpA], writes=[bdtk])
                c.op(c.dve, lambda: nc.vector.tensor_tensor(out=dak[0:L, :], in0=dtk[0:L, :], in1=negA[0:L, :], op=ALU.mult),
                     reads=[bdtk, bnA], writes=[bdak])

                def fcs():
                    nc.tensor.matmul(pA[0:L, 32:64], lhsT=tri[0:L, 0:L], rhs=dak[0:L, :], start=True, stop=True)
                    nc.tensor.matmul(pA[0:L, 64:96], lhsT=ones64[0:L, 0:L], rhs=dak[0:L, :], start=True, stop=True)
                    return nc.tensor.matmul(pA[0:128, 96:128], lhsT=ones64[0:L, 0:128], rhs=dak[0:L, :], start=True, stop=True)
                c.op(c.pe, fcs, reads=[bdak, btri], writes=[bpA])
                c.op(c.act, lambda: nc.scalar.copy(out=acs[0:L, :], in_=pA[0:L, 32:64]), reads=[bpA], writes=[bacs])
                c.op(c.act, lambda: nc.scalar.activation(out=ea[0:L, :], in_=pA[0:L, 32:64], func=AF.Exp), reads=[bpA], writes=[bea])
                c.op(c.dve, lambda: nc.vector.tensor_tensor(out=dte[0:L, :], in0=pA[0:L, 64:96], in1=acs[0:L, :], op=ALU.subtract),
                     reads=[bpA, bacs], writes=[bdte])
                c.op(c.act, lambda: nc.scalar.activation(out=dte[0:L, :], in_=dte[0:L, :], func=AF.Exp), reads=[bdte], writes=[bdte])
                c.op(c.act, lambda: nc.scalar.activation(out=cdec[:], in_=pA[0:128, 96:128], func=AF.Exp), reads=[bpA], writes=[bcdec])
                c.op(c.dve, lambda: nc.vector.tensor_tensor(out=tacc[:], in0=tacc[:], in1=pA[0:128, 96:128], op=ALU.add),
                     reads=[bpA, btacc], writes=[btacc])
                if SSD_STOP <= 1:
                    continue
                for g in range(NG):
                    hs = slice(8 * g, 8 * g + 8)
                    gs = slice(512 * g, 512 * (g + 1))
                    def ftr():
                        for q in range(4):
                            nc.tensor.transpose(pX[0:L, q * 128:(q + 1) * 128], xbc[:, 4 * g + q, t0:t0 + L], self.ident[:, :])
                        return nc.tensor.transpose(pX[0:L, 512:640], xbc[:, 16 + g, t0:t0 + L], self.ident[:, :])
                    c.op(c.pe, ftr, reads=[bxbc], writes=[bpX])
                    if SSD_STOP <= 1.1:
                        continue
                    bc8 = lambda t: t[0:L, hs].unsqueeze(2).to_broadcast([L, 8, HD])
                    v3 = lambda t: t[0:L, 0:512].rearrange("p (h d) -> p h d", d=HD)
                    c.op(c.dve, lambda: nc.vector.tensor_tensor(out=v3(xdt), in0=v3(pX), in1=bc8(dtk), op=ALU.mult),
                         reads=[bpX, bdtk], writes=[bxdt])
                    c.op(c.act, lambda: nc.scalar.copy(out=btok[0:L, :], in_=pX[0:L, 512:640]), reads=[bpX], writes=[bbtok])
                    c.op(c.dve, lambda: nc.vector.tensor_tensor(out=v3(xdtw), in0=v3(xdt), in1=bc8(dte), op=ALU.mult),
                         reads=[bxdt, bdte], writes=[bxdtw])
                    if with_output and SSD_STOP > 2:
                        vL = lambda t: t[0:L, 0:8 * L].rearrange("p (h l) -> p h l", l=L)
                        c.op(c.dve, lambda: nc.vector.tensor_tensor(
                            out=vL(Xg), in0=dak[0:L, hs].unsqueeze(2).to_broadcast([L, 8, L]),
                            in1=tri[0:L, 0:L].unsqueeze(1).to_broadcast([L, 8, L]), op=ALU.mult),
                            reads=[bdak, btri], writes=[bXg])
                        c.op(c.pe, lambda: nc.tensor.matmul(pR[0:L, 0:8 * L], lhsT=t2[0:L, 0:L], rhs=Xg[0:L, 0:8 * L], start=True, stop=True),
                             reads=[bXg, bt2], writes=[bpR])
                        c.op(c.act, lambda: nc.scalar.activation(out=dec[0:L, 0:8 * L], in_=pR[0:L, 0:8 * L], func=AF.Exp),
                             reads=[bpR], writes=[bdec])
                        c.op(c.pe, lambda: nc.tensor.matmul(pA[0:L, 128:128 + L], lhsT=xbc[:, 16 + g, t0:t0 + L], rhs=xbc[:, 20 + g, t0:t0 + L],
                                                            start=True, stop=True),
                             reads=[bxbc], writes=[bpA])
                        c.op(c.dve, lambda: nc.vector.tensor_tensor(out=gm[0:L, 0:L], in0=pA[0:L, 128:128 + L], in1=tri[0:L, 0:L], op=ALU.mult),
                             reads=[bpA, btri], writes=[bgm])
                        c.op(c.dve, lambda: nc.vector.tensor_tensor(out=vL(mt), in0=vL(dec),
                                                                    in1=gm[0:L, 0:L].unsqueeze(1).to_broadcast([L, 8, L]), op=ALU.mult),
                             reads=[bdec, bgm], writes=[bmt])

                        def fy():
                            for h8 in range(8):
                                ins = nc.tensor.matmul(pY[0:L, h8 * HD:(h8 + 1) * HD], lhsT=mt[0:L, h8 * L:(h8 + 1) * L],
                                                       rhs=xdt[0:L, h8 * HD:(h8 + 1) * HD], start=True, stop=True)
                            return ins
                        c.op(c.pe, fy, reads=[bmt, bxdt], writes=[bpY])
                        c.op(c.pe, lambda: nc.tensor.matmul(pYo[0:L, :], lhsT=xbc[:, 20 + g, t0:t0 + L], rhs=Sb[:, gs], start=True, stop=True),
                             reads=[bxbc, bSb], writes=[bpYo])
                        c.op(c.dve, lambda: nc.vector.tensor_tensor(out=v3(ytmp), in0=v3(pYo), in1=bc8(ea), op=ALU.mult),
                             reads=[bpYo, bea], writes=[bytmp])
                        c.op(c.dve, lambda: nc.vector.tensor_tensor(out=ytok[0:L, gs], in0=ytmp[0:L, :], in1=pY[0:L, :], op=ALU.add),
                             reads=[bytmp, bpY], writes=[bytok])
                    if SSD_STOP <= 1.2:
                        continue
                    c.op(c.pe, lambda: nc.tensor.matmul(pSt[:, :], lhsT=btok[0:L, :], rhs=xdtw[0:L, :], start=True, stop=True),
                         reads=[bbtok, bxdtw], writes=[bpSt])
                    Sv = S[:, gs].rearrange("p (h d) -> p h d", d=HD)
                    c.op(c.dve, lambda: nc.vector.tensor_tensor(out=stmp[:].rearrange("p (h d) -> p h d", d=HD), in0=Sv,
                                                                in1=cdec[:, hs].unsqueeze(2).to_broadcast([128, 8, HD]), op=ALU.mult),
                         reads=[bS, bcdec], writes=[bstmp])
                    c.op(c.dve, lambda: nc.vector.tensor_tensor(out=S[:, gs], in0=stmp[:], in1=pSt[:, :], op=ALU.add),
                         reads=[bstmp, bpSt], writes=[bS])
                    c.op(c.act, lambda: nc.scalar.copy(out=Sb[:, gs], in_=S[:, gs]), reads=[bS], writes=[bSb])
                if not with_output or SSD_STOP <= 3:
                    continue
                def fyt():
                    for q in range(16):
                        ins = nc.tensor.transpose(pYT[:, q, 0:L], ytok[0:L, q * 128:(q + 1) * 128], self.identf[0:L, 0:L])
                    return ins
                c.op(c.pe, fyt, reads=[bytok], writes=[bpYT])
                bq = lambda col: self.vecs[:, col:col + 16].unsqueeze(2).to_broadcast([128, 16, L])
                c.op(c.dve, lambda: nc.vector.tensor_tensor(out=y1[:, :, 0:L], in0=xbc[:, 0:16, t0:t0 + L], in1=bq(V_DS), op=ALU.mult),
                     reads=[bxbc], writes=[by1])
                c.op(c.dve, lambda: nc.vector.tensor_tensor(out=y1[:, :, 0:L], in0=y1[:, :, 0:L], in1=pYT[:, :, 0:L], op=ALU.add),
                     reads=[by1, bpYT], writes=[by1])
                zt_, bz = zc[ic % 2]
                c.dma(c.sp, zt_[:, :, 0:L], zv[:, :, t0:t0 + L], writes=[bz])
                c.op(c.act, lambda: nc.scalar.activation(out=sz[:, :, 0:L], in_=zt_[:, :, 0:L], func=AF.Silu), reads=[bz], writes=[bsz])
                c.op(c.dve, lambda: nc.vector.tensor_tensor(out=y1[:, :, 0:L], in0=y1[:, :, 0:L], in1=sz[:, :, 0:L], op=ALU.mult),
                     reads=[by1, bsz], writes=[by1])
                c.op(c.dve, lambda: nc.vector.tensor_tensor(out=sq[:, :, 0:L], in0=y1[:, :, 0:L], in1=y1[:, :, 0:L], op=ALU.mult),
                     reads=[by1], writes=[bsq])

                def fss():
                    for g in range(4):
                        for q in range(4):
                            ins = nc.tensor.matmul(pR[:, g * 64:g * 64 + L], lhsT=self.onesb[:, :], rhs=sq[:, 4 * g + q, 0:L],
                                                   start=(q == 0), stop=(q == 3))
                    return ins
                c.op(c.pe, fss, reads=[bsq], writes=[bpR])
                pRv = pR[:, 0:256].rearrange("p (g l) -> p g l", l=64)
                c.op(c.act, lambda: nc.scalar.activation(out=rs[:, :, 0:L], in_=pRv[:, :, 0:L], func=AF.Sqrt, scale=1.0 / 512,
                                                         bias=self.epsc[:, :]),
                     reads=[bpR], writes=[brs])
                c.op(c.dve, lambda: nc.vector.reciprocal(out=rs[:, :, 0:L], in_=rs[:, :, 0:L]), reads=[brs], writes=[brs])
                y1v = y1[:, :, 0:L].rearrange("p (g q) l -> p g q l", q=4)
                c.op(c.dve, lambda: nc.vector.tensor_tensor(out=y1v, in0=y1v, in1=rs[:, :, 0:L].unsqueeze(2).to_broadcast([128, 4, 4, L]),
                                                            op=ALU.mult),
                     reads=[by1, brs], writes=[by1])
                o, bo = yo[ic % 2]
                c.op(c.dve, lambda: nc.vector.tensor_tensor(out=o[:, :, 0:L], in0=y1[:, :, 0:L], in1=bq(V_SN), op=ALU.mult),
                     reads=[by1], writes=[bo])
                c.dma(c.pool, brv[:, :, t0:t0 + L], o[:, :, 0:L], reads=[bo], writes=[self.B("brT")])
            if not with_output:
                c.dma(c.pool, self.L_out[:, :], S[:], reads=[bS], writes=[self.B("ssdl")])
                c.op(c.act, lambda: nc.scalar.activation(out=tacc[:], in_=tacc[:], func=AF.Exp), reads=[btacc], writes=[btacc])
                c.dma(c.pool, self.D_out[:, :], tacc[:], reads=[btacc], writes=[self.B("ssdl")])


    def gemm_B(self, ph, actT, KC, NT, blocks, tag, odt):
        nc, c = self.nc, self.c
        NW = max(b["nw"] for b in blocks)
        wst, bw = self.TB(ph, [128, KC, NW], F32, "bwst")
        wbf = [self.TB(ph, [128, KC, NW], BF16, f"bwbf{i}") for i in range(2)]
        ost = [self.TB(ph, [128, NW], odt, f"bost{i}") for i in range(3)]
        pss = [self.TB(ph, [128, 512], F32, f"bps{i}", psum=True) for i in range(3)]
        bact = self.B("actres")
        n = 0
        for ib, blk in enumerate(blocks):
            nw = blk["nw"]
            wb, bwb = wbf[ib % 2]
            Wv = blk["W"].rearrange("(kc p) m -> p kc m", p=128)
            step = max(1, KC // 4)
            for k0 in range(0, KC, step):
                c.dma(c.sp, wst[:, k0:k0 + step, 0:nw], Wv[:, k0:k0 + step, :], writes=[bw])
            half = max(1, KC // 2)
            c.op(c.dve, lambda: nc.vector.tensor_copy(out=wb[:, 0:half, 0:nw], in_=wst[:, 0:half, 0:nw]), reads=[bw], writes=[bwb])
            if half < KC:
                c.op(c.pool, lambda: nc.gpsimd.tensor_copy(out=wb[:, half:KC, 0:nw], in_=wst[:, half:KC, 0:nw]), reads=[bw], writes=[bwb])
            t0 = 0
            while t0 < NT:
                nt = min(128, NT - t0)
                p, bp = pss[n % 3]
                o, bo = ost[n % 3]
                n += 1

                def fm():
                    for k in range(KC):
                        ins = nc.tensor.matmul(p[0:nt, 0:nw], lhsT=actT[:, k, t0:t0 + nt], rhs=wb[:, k, 0:nw],
                                               start=(k == 0), stop=(k == KC - 1))
                    return ins
                c.op(c.pe, fm, reads=[bwb, bact], writes=[bp])
                c.op(c.act, lambda: nc.scalar.copy(out=o[0:nt, 0:nw], in_=p[0:nt, 0:nw]), reads=[bp], writes=[bo])
                c.dma(c.pool, blk["out"][t0:t0 + nt, :], o[0:nt, 0:nw], reads=[bo], writes=[self.B("gemmB_out")])
                t0 += nt

    def fm_rmsnorm(self, ph, row0, KC, wcol, out_dram, tag):
        nc, c, TS = self.nc, self.c, self.TS
        x, bx = self.TB(ph, [128, KC, TS], BF16, tag + "x")
        sq, bsq = self.TB(ph, [128, KC, 512], BF16, tag + "sq")
        rs, brs = self.TB(ph, [128, 512], F32, tag + "rs")
        o, bo = self.TB(ph, [128, KC, TS], BF16, tag + "o")
        p, bp = self.TB(ph, [128, 512], F32, tag + "p", psum=True)
        c.dma(c.sp, x[:], self.projT[row0:row0 + KC * 128, :].rearrange("(kc p) t -> p kc t", p=128), writes=[bx])
        for (t0, nt) in ttiles(TS, 512):
            c.op(c.dve, lambda: nc.vector.tensor_tensor(out=sq[:, :, 0:nt], in0=x[:, :, t0:t0 + nt], in1=x[:, :, t0:t0 + nt], op=ALU.mult),
                 reads=[bx], writes=[bsq])

            def fs():
                for k in range(KC):
                    ins = nc.tensor.matmul(p[:, 0:nt], lhsT=self.onesb[:, :], rhs=sq[:, k, 0:nt], start=(k == 0), stop=(k == KC - 1))
                return ins
            c.op(c.pe, fs, reads=[bsq], writes=[bp])
            c.op(c.act, lambda: nc.scalar.activation(out=rs[:, 0:nt], in_=p[:, 0:nt], func=AF.Sqrt, scale=1.0 / (KC * 128),
                                                     bias=self.epsc[:, :]), reads=[bp], writes=[brs])
            c.op(c.dve, lambda: nc.vector.reciprocal(out=rs[:, 0:nt], in_=rs[:, 0:nt]), reads=[brs], writes=[brs])
            for k in range(KC):
                c.op(c.dve, lambda: nc.vector.scalar_tensor_tensor(out=o[:, k, t0:t0 + nt], in0=x[:, k, t0:t0 + nt],
                                                                   scalar=self.vecs[:, wcol + k:wcol + k + 1], in1=rs[:, 0:nt],
                                                                   op0=ALU.mult, op1=ALU.mult),
                     reads=[bx, brs], writes=[bo])
        c.dma(c.pool, out_dram.rearrange("(kc p) t -> p kc t", p=128), o[:], reads=[bo], writes=[self.B(tag + "out")])

    def make_rope(self, ph):
        nc, c, TS = self.nc, self.c, self.TS
        ang, ba = self.TB(ph, [64, TS], F32, "ang")
        self.cos2, self.bcos = self.TB(ph, [64, TS], F32, "cos2")
        self.sin2, self.bsin = self.TB(ph, [64, TS], F32, "sin2")
        fr, bfr = self.TB(ph, [64, 1], F32, "fr")
        rmf, brm = self.TB(ph, [64, 64], F32, "rmf")
        self.rm, self.brm = self.TB(ph, [64, 64], BF16, "rm")
        pi = float(np.pi)
        negpi, bnp = self.TB(ph, [64, 1], F32, "negpi")

        kf, bkf = self.TB(ph, [64, TS], F32, "kf")
        ki, bki = self.TB(ph, [64, TS], mybir.dt.int32, "ki")
        wr, bwr = self.TB(ph, [64, TS], F32, "wr")

        c.chain(c.pool, [
            lambda: nc.gpsimd.iota(ang[:], pattern=[[1, TS]], base=0, channel_multiplier=0, allow_small_or_imprecise_dtypes=True),
            lambda: nc.gpsimd.memset(rmf[:], 0.0),
            lambda: nc.gpsimd.affine_select(out=rmf[:], in_=rmf[:], pattern=[[1, 64]], compare_op=ALU.not_equal, fill=1.0, base=-32, channel_multiplier=-1),
            lambda: nc.gpsimd.affine_select(out=rmf[:], in_=rmf[:], pattern=[[-1, 64]], compare_op=ALU.not_equal, fill=-1.0, base=-32, channel_multiplier=1),
            lambda: nc.gpsimd.tensor_copy(out=self.rm[:], in_=rmf[:]),
        ], writes=[ba, brm, self.brm])
        c.op(c.dve, lambda: nc.vector.tensor_scalar(out=ang[:], in0=ang[:], scalar1=self.cmeta[0:64, C_POS:C_POS + 1],
                                                    scalar2=self.vecs[0:64, V_IF:V_IF + 1], op0=ALU.add, op1=ALU.mult),
             reads=[ba], writes=[ba])
        C1 = 6.28125
        C2 = float(2 * np.pi - C1)
        for (dst, bdst, shift) in ((self.sin2, self.bsin, 0.0), (self.cos2, self.bcos, pi / 2)):
            c.chain(c.dve, [
                lambda: nc.vector.tensor_scalar(out=dst[:], in0=ang[:], scalar1=shift, scalar2=None, op0=ALU.add),
                lambda: nc.vector.tensor_scalar(out=kf[:], in0=dst[:], scalar1=1.0 / (2 * pi), scalar2=None, op0=ALU.mult),
                lambda: nc.vector.tensor_copy(out=ki[:], in_=kf[:]),
                lambda: nc.vector.tensor_copy(out=kf[:], in_=ki[:]),
                lambda: nc.vector.scalar_tensor_tensor(out=dst[:], in0=kf[:], scalar=-C1, in1=dst[:], op0=ALU.mult, op1=ALU.add),
                lambda: nc.vector.scalar_tensor_tensor(out=dst[:], in0=kf[:], scalar=-C2, in1=dst[:], op0=ALU.mult, op1=ALU.add),
                lambda: nc.vector.tensor_scalar(out=wr[:], in0=dst[:], scalar1=pi, scalar2=-2 * pi, op0=ALU.is_gt, op1=ALU.mult),
                lambda: nc.vector.tensor_tensor(out=dst[:], in0=dst[:], in1=wr[:], op=ALU.add),
                lambda: nc.vector.tensor_scalar(out=wr[:], in0=dst[:], scalar1=-pi, scalar2=2 * pi, op0=ALU.is_lt, op1=ALU.mult),
                lambda: nc.vector.tensor_tensor(out=dst[:], in0=dst[:], in1=wr[:], op=ALU.add),
                lambda: nc.vector.tensor_scalar(out=dst[:], in0=dst[:], scalar1=pi, scalar2=-pi, op0=ALU.min, op1=ALU.max),
            ], reads=[ba], writes=[bdst, bkf, bki, bwr])
            c.op(c.act, lambda: nc.scalar.activation(out=dst[:], in_=dst[:], func=AF.Sin), reads=[bdst], writes=[bdst])

    def rope_apply(self, ph, src, bsrc, dst, bdst, pr, bpr, t1, bt1, t2, bt2, t0, nt):
        nc, c = self.nc, self.c
        c.op(c.pe, lambda: nc.tensor.matmul(pr[0:64, 0:nt], lhsT=self.rm[:, :], rhs=src[0:64, t0:t0 + nt], start=True, stop=True),
             reads=[bsrc, self.brm], writes=[bpr])
        c.op(c.dve, lambda: nc.vector.tensor_tensor(out=t1[0:64, 0:nt], in0=src[0:64, t0:t0 + nt], in1=self.cos2[:, t0:t0 + nt], op=ALU.mult),
             reads=[bsrc, self.bcos], writes=[bt1])
        c.op(c.dve, lambda: nc.vector.tensor_tensor(out=t2[0:64, 0:nt], in0=pr[0:64, 0:nt], in1=self.sin2[:, t0:t0 + nt], op=ALU.mult),
             reads=[bpr, self.bsin], writes=[bt2])
        c.op(c.dve, lambda: nc.vector.tensor_tensor(out=dst[0:64, t0:t0 + nt], in0=t1[0:64, 0:nt], in1=t2[0:64, 0:nt], op=ALU.add),
             reads=[bt1, bt2], writes=[bdst])

    def ph_mla_prep(self, with_q):
        nc, c, TS = self.nc, self.c, self.TS
        with Phase(self, "mp") as ph:
            self.make_rope(ph)
            if with_q:
                self.S("qnT", "qnT", [QR, TS], BF16)
                self.fm_rmsnorm(ph, O_Q, 8, V_QN, self.qnT[:, :], "qn")
            self.fm_rmsnorm(ph, O_KV, 4, V_KN, self.kvx[0:KVR, :], "kn")
            kr, bkr = self.TB(ph, [64, TS], BF16, "kr")
            ko, bko = self.TB(ph, [64, TS], BF16, "ko")
            t1, bt1 = self.TB(ph, [64, 512], F32, "t1")
            t2, bt2 = self.TB(ph, [64, 512], F32, "t2")
            pr, bpr = self.TB(ph, [128, 512], F32, "pr", psum=True)
            c.dma(c.sp, kr[:], self.projT[O_KR:O_KR + 64, :], writes=[bkr])
            for (t0, nt) in ttiles(TS, 512):
                self.rope_apply(ph, kr, bkr, ko, bko, pr, bpr, t1, bt1, t2, bt2, t0, nt)
            c.dma(c.pool, self.kvx[KVR:KVR + 64, :], ko[:], reads=[bko], writes=[self.B("kvx")])

    def ph_mla_proj(self):
        nc, c, TS, SEG, NK = self.nc, self.c, self.TS, self.SEG, self.NK
        NKC, NKG = self.NKC, self.NKG
        sc = 1.0 / float(np.sqrt(NOPE + ROPE))
        self.S("qT", "qT", [MH * 192, TS], BF16)
        self.S("kT", "kT", [MH * 128, NKC], BF16)
        self.S("vTok", "vTok", [NKC, MH * 128], BF16)
        with Phase(self, "mq") as ph:
            actT = self.load_actT(ph, self.qnT, 8)
            blocks = []
            for h in range(MH):
                blocks.append(dict(W=self.I("w_q_b")[:, h * 192:h * 192 + 128], mw=128, out=self.qT[h * 192:h * 192 + 128, :], scale=sc))
                blocks.append(dict(W=self.I("w_q_b")[:, h * 192 + 128:h * 192 + 192], mw=64, out=self.qT[h * 192 + 128:h * 192 + 192, :], scale=sc))
            self.gemm_A(ph, actT, 8, blocks, "mq")
        with Phase(self, "mk") as ph:
            kvn, bk = ph.sb([128, 4, NKC], BF16, "kvnC"), self.B("actres")
            for k in range(4):
                for (c0, ncol, src) in self.kv_pieces(k):
                    c.dma(c.sp, kvn[:, k, c0:c0 + ncol], src, reads=[self.B("kvg")], writes=[bk])
            c.dma(c.sp, kvn[:, :, NKG:NKC], self.kvx[0:KVR, HALO:TS].rearrange("(kc p) t -> p kc t", p=128), writes=[bk])
            tts = []
            t = 0
            while t < NKC:
                tts.append((t, min(512, NKC - t)))
                t += 512
            blocks = [dict(W=self.I("w_kv_b")[:, h * 256:h * 256 + 128], mw=128, out=self.kT[h * 128:(h + 1) * 128, :]) for h in range(MH)]
            self.gemm_A(ph, kvn, 4, blocks, "mk", tts=tts, width=NKC)
            blocks = [dict(W=self.I("w_kv_b")[:, h * 256 + 128:h * 256 + 256], nw=128, out=self.vTok[:, h * 128:(h + 1) * 128]) for h in range(MH)]
            self.gemm_B(ph, kvn, 4, NKC, blocks, "mv", BF16)

    def ph_mla_attn(self):
        nc, c, TS, SEG, NK, NSEG = self.nc, self.c, self.TS, self.SEG, self.NK, self.NSEG
        NKC, NKG = self.NKC, self.NKG
        QT = min(512, SEG)
        with Phase(self, "at") as ph:
            self.make_rope(ph)
            kpe, bkpe = self.TB(ph, [64, NKC], BF16, "kpe")
            for (c0, ncol, src) in self.kv_pieces(4):
                c.dma(c.sp, kpe[:, c0:c0 + ncol], src, reads=[self.B("kvg")], writes=[bkpe])
            c.dma(c.sp, kpe[:, NKG:NKC], self.kvx[KVR:KVR + 64, HALO:TS], writes=[bkpe])
            nd = QT // 128
            masks = []
            for d_ in range(nd):
                m, bm = self.TB(ph, [128, QT], BF16, f"mask{d_}")

                fns = [lambda: nc.gpsimd.memset(m[:], 0.0)]
                for kh in range(2):
                    c0 = 64 * (2 * d_ + kh)
                    if c0 < QT:
                        fns.append(lambda kh=kh, c0=c0: nc.gpsimd.memset(m[64 * kh:64 * kh + 64, c0:QT], 1.0))
                c.chain(c.pool, fns, writes=[bm])
                masks.append((m, bm))
            qn = [self.TB(ph, [128, TS], BF16, f"qn{i}") for i in range(2)]
            qr = [self.TB(ph, [64, TS], BF16, f"qr{i}") for i in range(2)]
            qp, bqp = self.TB(ph, [64, TS], BF16, "qp")
            kt = [self.TB(ph, [128, NKC], BF16, f"kt{i}") for i in range(2)]
            NKT = (NKC - HALO) // 128
            vv = [self.TB(ph, [128, NKT, 128], BF16, f"vv{i}") for i in range(2)]
            vm = [self.TB(ph, [16, 128], BF16, f"vm{i}") for i in range(2)]
            gt = [self.TB(ph, [128, TS], BF16, f"gt{i}") for i in range(2)]
            gf, bgf = self.TB(ph, [128, 512], F32, "gf")
            t1, bt1 = self.TB(ph, [64, 512], F32, "t1")
            t2, bt2 = self.TB(ph, [64, 512], F32, "t2")
            pt = [self.TB(ph, [128, 512], BF16, f"pt{i}") for i in range(3)]
            rden, brden = self.TB(ph, [128, 512], F32, "rden")
            of, bof = self.TB(ph, [128, 512], F32, "of")
            ob = [self.TB(ph, [128, TS], BF16, f"ob{i}") for i in range(2)]
            pS = [self.TB(ph, [128, 512], F32, f"pS{i}", psum=True) for i in range(3)]
            pO, bpO = self.TB(ph, [128, 512], F32, "pO", psum=True)
            pD, bpD = self.TB(ph, [128, 512], F32, "pD", psum=True)
            pr, bpr = self.TB(ph, [128, 512], F32, "pr", psum=True)
            nS = 0
            for h in range(MH):
                i = h % 2
                (qn_, bqn), (qr_, bqr), (kt_, bkt), (vv_, bvv), (vm_, bvm), (gt_, bgt), (ob_, bob) = qn[i], qr[i], kt[i], vv[i], vm[i], gt[i], ob[i]
                c.dma(c.sp, qn_[:], self.qT[h * 192:h * 192 + 128, :], writes=[bqn])
                c.dma(c.sp, qr_[:], self.qT[h * 192 + 128:h * 192 + 192, :], writes=[bqr])
                c.dma(c.sp, kt_[:], self.kT[h * 128:(h + 1) * 128, :], writes=[bkt])
                c.dma(c.sp, vm_[:], self.vTok[0:HALO, h * 128:(h + 1) * 128], writes=[bvm])
                c.dma(c.sp, vv_[:], self.vTok[HALO:NKC, h * 128:(h + 1) * 128].rearrange("(kt p) v -> p kt v", p=128), writes=[bvv])
                c.dma(c.sp, gt_[:], self.projT[O_MG + h * 128:O_MG + (h + 1) * 128, :], writes=[bgt])
                for (t0, nt) in ttiles(TS, QT):
                    self.rope_apply(ph, qr_, bqr, qp, bqp, pr, bpr, t1, bt1, t2, bt2, t0, nt)
                for iq, (t0, nt) in enumerate(ttiles(TS, QT)):
                    kts = [(0, HALO, None, None, vm_[0:HALO, :])]
                    if iq > 0:
                        for j in range(NSEG - 1):
                            for ii in range(SEG // 128):
                                kti = j * (SEG // 128) + ii
                                kts.append((HALO + kti * 128, 128, self.cmeta[:, C_VIS + j:C_VIS + j + 1], None, vv_[:, kti, :]))
                        a = iq - 1
                        for ii in range(SEG // 128):
                            d_ = ii - a * nd
                            if d_ >= nd:
                                continue
                            kti = (NSEG - 1) * (SEG // 128) + ii
                            kts.append((NKG + ii * 128, 128, None, masks[d_] if d_ >= 0 else None, vv_[:, kti, :]))
                    pendq = []
                    for ik, (k0, nk, bias, mask, vl) in enumerate(kts):
                        ps_, bps = pS[nS % 3]
                        pt_, bpt = pt[nS % 3]
                        nS += 1

                        def fs():
                            nc.tensor.matmul(ps_[0:nk, 0:nt], lhsT=kt_[:, k0:k0 + nk], rhs=qn_[:, t0:t0 + nt], start=True, stop=False)
                            return nc.tensor.matmul(ps_[0:nk, 0:nt], lhsT=kpe[:, k0:k0 + nk], rhs=qp[:, t0:t0 + nt], start=False, stop=True)
                        c.op(c.pe, fs, reads=[bkt, bqn, bkpe, bqp], writes=[bps])
                        kw = {} if bias is None else {"bias": bias[0:nk, :]}
                        c.op(c.act, lambda: nc.scalar.activation(out=pt_[0:nk, 0:nt], in_=ps_[0:nk, 0:nt], func=AF.Exp, **kw),
                             reads=[bps], writes=[bpt])
                        if mask is not None:
                            c.op(c.dve, lambda: nc.vector.tensor_tensor(out=pt_[0:nk, 0:nt], in0=pt_[0:nk, 0:nt], in1=mask[0][0:nk, 0:nt], op=ALU.mult),
                                 reads=[bpt, mask[1]], writes=[bpt])
                        def mk_fo(pt_=pt_, bpt=bpt, nk=nk, vl=vl, first=(ik == 0), last=(ik == len(kts) - 1)):
                            def fo():
                                nc.tensor.matmul(pO[:, 0:nt], lhsT=vl, rhs=pt_[0:nk, 0:nt], start=first, stop=last)
                                return nc.tensor.matmul(pD[:, 0:nt], lhsT=self.onesb[0:nk, :], rhs=pt_[0:nk, 0:nt], start=first, stop=last)
                            return lambda: c.op(c.pe, fo, reads=[bpt, bvv, bvm], writes=[bpO, bpD])
                        pendq.append(mk_fo())
                        if len(pendq) > 2:
                            pendq.pop(0)()
                    for f_ in pendq:
                        f_()

                    c.op(c.dve, lambda: nc.vector.reciprocal(out=rden[:, 0:nt], in_=pD[:, 0:nt]), reads=[bpD], writes=[brden])
                    c.op(c.dve, lambda: nc.vector.tensor_tensor(out=of[:, 0:nt], in0=pO[:, 0:nt], in1=rden[:, 0:nt], op=ALU.mult),
                         reads=[bpO, brden], writes=[bof])
                    c.op(c.act, lambda: nc.scalar.activation(out=gf[:, 0:nt], in_=gt_[:, t0:t0 + nt], func=AF.Silu), reads=[bgt], writes=[bgf])
                    c.op(c.dve, lambda: nc.vector.tensor_tensor(out=ob_[:, t0:t0 + nt], in0=of[:, 0:nt], in1=gf[:, 0:nt], op=ALU.mult),
                         reads=[bof, bgf], writes=[bob])
                c.dma(c.pool, self.brT[BW + h * 128:BW + (h + 1) * 128, :], ob_[:], reads=[bob], writes=[self.B("brT")])


    def ph_pool(self):
        nc, c, TS = self.nc, self.c, self.TS
        with Phase(self, "pl") as ph:
            mixed = ph.sb([128, 16, TS], BF16, "mixed")
            bmx = self.B("actres")
            ic, bic = self.TB(ph, [128, 4, HALO], F32, "ic")

            fns = []
            for g in range(4):
                w = 2 ** (g + 1)
                fns.append(lambda g=g, w=w: nc.gpsimd.memset(ic[:, g, :], 1.0 / w))
                for t in range(w - 1):
                    fns.append(lambda g=g, t=t: nc.gpsimd.memset(ic[:, g, t:t + 1], 1.0 / (t + 1)))
            c.chain(c.pool, fns, writes=[bic])
            ub = [self.TB(ph, [128, TS], BF16, f"ub{i}") for i in range(2)]
            uf, buf_ = self.TB(ph, [128, TS], F32, "uf")
            s0, bs0 = self.TB(ph, [128, TS], F32, "s0")
            s1, bs1 = self.TB(ph, [128, TS], F32, "s1")
            th, bth = self.TB(ph, [128, HALO], F32, "th")
            for q in range(16):
                g = q // 4
                w = 2 ** (g + 1)
                u, bu = ub[q % 2]
                c.dma(c.sp, u[:], self.projT[O_PU + q * 128:O_PU + (q + 1) * 128, :], writes=[bu])
                c.op(c.act, lambda: nc.scalar.copy(out=uf[:], in_=u[:]), reads=[bu], writes=[buf_])
                cur, bcur, nxt, bnxt = uf, buf_, s0, bs0
                step = 1
                while step < w:
                    def fw():
                        nc.vector.tensor_copy(out=nxt[:, 0:step], in_=cur[:, 0:step])
                        return nc.vector.tensor_tensor(out=nxt[:, step:TS], in0=cur[:, step:TS], in1=cur[:, 0:TS - step], op=ALU.add)
                    c.op(c.dve, fw, reads=[bcur], writes=[bnxt])
                    if nxt is s0:
                        cur, bcur, nxt, bnxt = s0, bs0, s1, bs1
                    else:
                        cur, bcur, nxt, bnxt = s1, bs1, s0, bs0
                    step *= 2

                c.op(c.dve, lambda: nc.vector.scalar_tensor_tensor(out=mixed[:, q, HALO:TS], in0=cur[:, HALO:TS], scalar=1.0 / w,
                                                                   in1=uf[:, HALO:TS], op0=ALU.mult, op1=ALU.subtract),
                     reads=[bcur, buf_], writes=[bmx])
                c.op(c.dve, lambda: nc.vector.tensor_tensor(out=th[:], in0=cur[:, 0:HALO], in1=ic[:, g, :], op=ALU.mult),
                     reads=[bcur, bic], writes=[bth])
                c.op(c.dve, lambda: nc.vector.tensor_tensor(out=mixed[:, q, 0:HALO], in0=th[:], in1=uf[:, 0:HALO], op=ALU.subtract),
                     reads=[bth, buf_], writes=[bmx])
            for g in range(4):
                with Phase(self, f"pg{g}") as ph2:
                    blocks = []
                    for m in range(4):
                        r0 = g * 512 + m * 128
                        blocks.append(dict(W=self.I("w_pool")[g, :, m * 128:(m + 1) * 128], mw=128, out=self.brT[2 * BW + r0:2 * BW + r0 + 128, :],
                                           func=AF.Identity, scale=self.vecs[:, V_PS + 4 * g + m:V_PS + 4 * g + m + 1],
                                           pm=(self.projT[O_PG + r0:O_PG + r0 + 128, :], AF.Silu)))
                    self.gemm_A(ph2, mixed[:, 4 * g:4 * g + 4, :], 4, blocks, f"pg{g}")

    def ph_branch(self):
        nc, c, TS = self.nc, self.c, self.TS
        self.S("brW", "brW", [3 * D, TS], BF16)
        for i in range(3):
            with Phase(self, f"bw{i}") as ph:
                actT = self.load_actT(ph, self.brT[i * BW:(i + 1) * BW, :], 16)
                blocks = [dict(W=self.I("w_branch")[i, :, m * 128:(m + 1) * 128], mw=128,
                               out=self.brW[i * D + m * 128:i * D + (m + 1) * 128, :],
                               pm=(self.sigT[i * D + m * 128:i * D + (m + 1) * 128, :], AF.Copy)) for m in range(32)]
                self.gemm_A(ph, actT, 16, blocks, f"bw{i}")
        with Phase(self, "mg") as ph:
            tl = [[self.TB(ph, [128, TS], BF16, f"m{i}_{j}") for j in range(3)] for i in range(2)]
            acc = [self.TB(ph, [128, TS], F32, f"macc{i}") for i in range(2)]
            mo = [self.TB(ph, [128, TS], BF16, f"mo{i}") for i in range(2)]
            for kc in range(32):
                i = kc % 2
                for j in range(3):
                    c.dma(c.sp, tl[i][j][0][:], self.brW[j * D + kc * 128:j * D + (kc + 1) * 128, :], writes=[tl[i][j][1]])
                a, ba = acc[i]
                o, bo = mo[i]
                c.op(c.dve, lambda: nc.vector.tensor_tensor(out=a[:], in0=tl[i][0][0][:], in1=tl[i][1][0][:], op=ALU.add),
                     reads=[tl[i][0][1], tl[i][1][1]], writes=[ba])
                c.op(c.dve, lambda: nc.vector.tensor_tensor(out=o[:], in0=a[:], in1=tl[i][2][0][:], op=ALU.add),
                     reads=[ba, tl[i][2][1]], writes=[bo])
                c.dma(c.pool, self.mergedT[kc * 128:(kc + 1) * 128, :], o[:], reads=[bo], writes=[self.B("mergedT")])

    def ph_out(self):
        nc, c, TS = self.nc, self.c, self.TS
        with Phase(self, "op") as ph:
            actT = self.load_actT(ph, self.mergedT, 32)
            blocks = [dict(W=self.I("w_out")[:, cb * 256:(cb + 1) * 256], nw=256, out=self.outF[:, cb * 256:(cb + 1) * 256]) for cb in range(16)]
            self.gemm_B(ph, actT, 32, TS, blocks, "op", F32)
        with Phase(self, "fn") as ph:
            postw, bpw = self.TB(ph, [128, D], F32, "postw")
            c.dma(c.sp, postw[:], self.I("rows")[0:1, D:2 * D].partition_broadcast(128), writes=[bpw])
            ot = [self.TB(ph, [128, D], F32, f"fo{i}") for i in range(2)]
            ht = [self.TB(ph, [128, D], F32, f"fh{i}") for i in range(2)]
            junk, bj = self.TB(ph, [128, D], BF16, "fjunk")
            ss = [self.TB(ph, [128, 1], F32, f"fss{i}") for i in range(2)]
            for it, (t0, nt) in enumerate(ttiles(TS, 128)):
                i = it % 2
                (o, bo), (hh, bh), (s_, bs) = ot[i], ht[i], ss[i]
                c.dma(c.sp, o[0:nt, :], self.outF[t0:t0 + nt, :], writes=[bo])
                c.dma(c.sp, hh[0:nt, :], self.hsrc()[t0:t0 + nt, :], reads=[self.B("hres")], writes=[bh])
                c.op(c.act, lambda: nc.scalar.activation(out=junk[0:nt, :], in_=o[0:nt, :], func=AF.Square, accum_out=s_[0:nt, :]),
                     reads=[bo], writes=[bj, bs])
                c.op(c.act, lambda: nc.scalar.activation(out=s_[0:nt, :], in_=s_[0:nt, :], func=AF.Sqrt, scale=1.0 / D, bias=self.epsc[0:nt, :]),
                     reads=[bs], writes=[bs])
                c.op(c.dve, lambda: nc.vector.reciprocal(out=s_[0:nt, :], in_=s_[0:nt, :]), reads=[bs], writes=[bs])
                c.op(c.dve, lambda: nc.vector.scalar_tensor_tensor(out=o[0:nt, :], in0=o[0:nt, :], scalar=s_[0:nt, 0:1], in1=postw[0:nt, :],
                                                                   op0=ALU.mult, op1=ALU.mult),
                     reads=[bo, bs, bpw], writes=[bo])
                c.op(c.dve, lambda: nc.vector.tensor_tensor(out=o[0:nt, :], in0=o[0:nt, :], in1=hh[0:nt, :], op=ALU.add),
                     reads=[bo, bh], writes=[bo])
                c.dma(c.pool, self.hdst()[t0:t0 + nt, :], o[0:nt, :], reads=[bo], writes=[self.B("hdst")])


    def exchange_mid(self):
        c = self.c
        c.barrier()
        for k in range(5):
            r0 = k * 128
            nr = 128 if k < 4 else ROPE
            c.dma(c.sp, self.kvc[k][:, :], self.kvx[r0:r0 + nr, :], writes=[self.B(f"kvc{k}")])
        c.barrier()
        for k in range(5):
            c.coll("AllGather", self.groups, self.kvc[k], self.kvc_g[k], writes=[self.B("kvg")])
        c.coll("AllGather", self.groups, self.L_out, self.L_g, writes=[self.B("ssdg")])
        c.coll("AllGather", self.groups, self.D_out, self.D_g, writes=[self.B("ssdg")])
        c.barrier()

    def ph_halo_exchange(self):
        nc, c, TS, NSEG = self.nc, self.c, self.TS, self.NSEG
        with Phase(self, "hx") as ph:
            c.dma(c.sp, self.tail_loc[:, :], self.h1[TS - HALO:TS, :], writes=[self.B("tail")])
            c.barrier()
            c.coll("AllGather", self.groups, self.tail_loc, self.tails_g, writes=[self.B("tailg")])
            c.barrier()
            own, bown = self.TB(ph, [HALO, D], F32, "own")
            tl, btl = self.TB(ph, [HALO, NSEG, D], F32, "tl")
            c.dma(c.sp, own[:], self.h1[0:HALO, :], writes=[bown])
            c.dma(c.sp, tl[:], self.tails_g.rearrange("(j r) d -> r j d", r=HALO), writes=[btl])
            c.op(c.dve, lambda: nc.vector.tensor_scalar(out=own[:], in0=own[:], scalar1=self.cmeta[0:HALO, C_M0:C_M0 + 1], scalar2=None,
                                                        op0=ALU.mult), reads=[bown], writes=[bown])
            for j in range(NSEG - 1):
                c.op(c.dve, lambda: nc.vector.scalar_tensor_tensor(out=own[:], in0=tl[:, j, :], scalar=self.cmeta[0:HALO, C_OH + j:C_OH + j + 1],
                                                                   in1=own[:], op0=ALU.mult, op1=ALU.add),
                     reads=[btl, bown], writes=[bown])
            c.dma(c.pool, self.h1[0:HALO, :], own[:], reads=[bown], writes=[self.B("hdst")])


def build_program(SEG, NSEG, mode, dbg=False, phases=None):
    P = Prog(SEG, NSEG, mode, dbg)
    c = P.c
    with contextlib.ExitStack() as es:
        class _G:
            pass
        gph = Phase(P, "glob")
        gph.__enter__()
        P.load_consts(gph)
        phases = phases or (["norm", "inproj", "conv", "ssd", "mla", "pool", "out"] if mode == "B" else ["norm", "inproj", "conv", "ssd", "mla"])
        if "norm" in phases:
            P.ph_norm()
        if "inproj" in phases:
            if mode == "A":
                P.ph_inproj([(O_XBC, O_Q), (O_KV, O_MG)], False)
            else:
                P.ph_inproj([(0, IN_DIM)], True)
        if "conv" in phases:
            P.ph_conv()
        if "ssd" in phases:
            P.ph_ssd(mode == "B")
        if "mla" in phases:
            P.ph_mla_prep(mode == "B")
            if mode == "B":
                P.ph_mla_proj()
                P.ph_mla_attn()
        if "pool" in phases:
            P.ph_pool()
        if "out" in phases:
            P.ph_branch()
            P.ph_out()
        gph.__exit__(None, None, None)
    c.barrier()
    c.close()
    return P


def host_layer_inputs(inp, L):
    f = lambda a: np.ascontiguousarray(a, dtype=np.float32)
    vecs = np.zeros((128, NV), np.float32)
    cw = inp["conv_w"][L][:, 0, :]
    vecs[:, V_CW:V_CW + 96] = cw.T.reshape(24, 128, 4).transpose(1, 0, 2).reshape(128, 96)
    vecs[:, V_CB:V_CB + 24] = inp["conv_b"][L].reshape(24, 128).T
    vecs[:, V_DS:V_DS + 16] = np.repeat(inp["d_skip"][L], HD).reshape(16, 128).T
    vecs[:, V_SN:V_SN + 16] = inp["ssd_norm_w"][L].reshape(16, 128).T
    vecs[:, V_QN:V_QN + 8] = inp["q_norm_w"][L].reshape(8, 128).T
    vecs[:, V_KN:V_KN + 4] = inp["kv_norm_w"][L].reshape(4, 128).T
    vecs[:, V_PS:V_PS + 16] = inp["pool_scale"][L].reshape(16, 128).T
    vecs[0:32, V_DTB] = inp["dt_bias"][L]
    inv_freq = np.power(np.float32(10000.0), -np.arange(0, ROPE, 2, dtype=np.float32) / np.float32(ROPE)).astype(np.float32)
    vecs[0:32, V_IF] = inv_freq
    vecs[32:64, V_IF] = inv_freq
    rows = np.concatenate([inp["pre_norm_w"][L], inp["post_norm_w"][L], inp["a_log"][L]])[None, :]
    return {
        "vecs": vecs, "rows": f(rows), "w_in": f(inp["w_in"][L]), "w_gate": f(inp["w_gate"][L]),
        "w_branch": f(inp["w_branch"][L]), "w_out": f(inp["w_out"][L]), "w_q_b": f(inp["w_q_b"][L]),
        "w_kv_b": f(inp["w_kv_b"][L]), "w_pool": f(inp["w_pool"][L]),
    }


def host_cmeta(seg, NSEG, SEG):
    cm = np.zeros((128, 16), np.float32)
    cm[:, C_M0] = 1.0 if seg == 0 else 0.0
    for j in range(4):
        cm[:, C_VIS + j] = 0.0 if j < seg else NEG
        cm[:, C_OH + j] = 1.0 if (j + 1) == seg else 0.0
    cm[:, C_POS] = float(seg * SEG)
    return cm


SEG_FULL, NSEG_FULL, NBATCH = 2048, 4, 2
_PROGS = {}


def _prog(mode):
    if mode not in _PROGS:
        _PROGS[mode] = build_program(SEG_FULL, NSEG_FULL, mode)
    return _PROGS[mode]


def kernel_unfused(**inputs):
    x = np.asarray(inputs["x"], dtype=np.float32)
    meta = np.asarray(inputs["meta_tokens"], dtype=np.float32)
    params = {k: np.asarray(v) for k, v in inputs.items() if k not in ("x", "meta_tokens")}
    SEG, NSEG = SEG_FULL, NSEG_FULL
    TS = HALO + SEG
    ncores = NBATCH * NSEG
    hfull = np.concatenate([np.broadcast_to(meta[None], (NBATCH, HALO, D)), x], axis=1).astype(np.float32)
    cmetas = [host_cmeta(cid % NSEG, NSEG, SEG) for cid in range(ncores)]
    depth = params["w_in"].shape[0]
    for L in range(depth):
        lay = host_layer_inputs(params, L)
        hs = [np.ascontiguousarray(hfull[cid // NSEG, (cid % NSEG) * SEG:(cid % NSEG) * SEG + TS]) for cid in range(ncores)]
        PA = _prog("A")
        in_maps = []
        for cid in range(ncores):
            m = {}
            for name in PA.inputs:
                m[name] = hs[cid] if name == "h" else cmetas[cid] if name == "cmeta" else lay[name]
            in_maps.append(m)
        ra = run_bass_kernel_spmd(PA.nc, in_maps, core_ids=list(range(ncores))).results
        kv_all, L_all, D_all = [], [], []
        for b in range(NBATCH):
            rs = [ra[b * NSEG + s] for s in range(NSEG)]
            kv_all.append(np.ascontiguousarray(np.concatenate([np.asarray(rs[0]["kvx"])[:, :HALO]] +
                                                              [np.asarray(r_["kvx"])[:, HALO:] for r_ in rs], axis=1)))
            L_all.append(np.ascontiguousarray(np.stack([np.asarray(r_["L_out"]) for r_ in rs], axis=0)))
            D_all.append(np.ascontiguousarray(np.concatenate([np.asarray(r_["D_out"])[0] for r_ in rs])[None, :]))
        PB = _prog("B")
        in_maps = []
        for cid in range(ncores):
            b = cid // NSEG
            m = {}
            for name in PB.inputs:
                if name == "h":
                    m[name] = hs[cid]
                elif name == "cmeta":
                    m[name] = cmetas[cid]
                elif name == "kv_all":
                    m[name] = kv_all[b]
                elif name == "L_all":
                    m[name] = L_all[b]
                elif name == "D_all":
                    m[name] = D_all[b]
                else:
                    m[name] = lay[name]
            in_maps.append(m)
        rb = run_bass_kernel_spmd(PB.nc, in_maps, core_ids=list(range(ncores))).results
        for cid in range(ncores):
            b, s = cid // NSEG, cid % NSEG
            ho = np.asarray(rb[cid]["h_out"])
            hfull[b, HALO + s * SEG:HALO + (s + 1) * SEG] = ho[HALO:]
            if s == 0:
                hfull[b, 0:HALO] = ho[0:HALO]
    return np.ascontiguousarray(hfull[:, HALO:]).astype(np.float32)


def build_fused(SEG, NSEG, depth):
    P = Prog(SEG, NSEG, "F")
    gph = Phase(P, "glob")
    gph.__enter__()
    for L in range(depth):
        P.L = L
        P.h_src = None if L == 0 else P.h1
        P.h_dst = P.h1 if L < depth - 1 else P.h_out
        P.load_consts(gph)
        P.ph_norm()
        P.ph_inproj([(0, IN_DIM)], True)
        P.ph_conv()
        P.ph_ssd(False)
        P.ph_mla_prep(True)
        P.exchange_mid()
        P.ph_ssd(True)
        P.ph_mla_proj()
        P.ph_mla_attn()
        P.ph_pool()
        P.ph_branch()
        P.ph_out()
        if L < depth - 1:
            P.ph_halo_exchange()
    gph.__exit__(None, None, None)
    P.c.barrier()
    P.c.close()
    return P


def kernel(**inputs):
    x = np.asarray(inputs["x"], dtype=np.float32)
    meta = np.asarray(inputs["meta_tokens"], dtype=np.float32)
    params = {k: np.asarray(v) for k, v in inputs.items() if k not in ("x", "meta_tokens")}
    SEG, NSEG = SEG_FULL, NSEG_FULL
    TS = HALO + SEG
    ncores = NBATCH * NSEG
    depth = params["w_in"].shape[0]
    if "F" not in _PROGS:
        _PROGS["F"] = build_fused(SEG, NSEG, depth)
    P = _PROGS["F"]
    hfull = np.concatenate([np.broadcast_to(meta[None], (NBATCH, HALO, D)), x], axis=1).astype(np.float32)
    lays = [host_layer_inputs(params, L) for L in range(depth)]
    in_maps = []
    for cid in range(ncores):
        b, s = cid // NSEG, cid % NSEG
        m = {}
        for key in P.inputs:
            if key == "h":
                m[key] = np.ascontiguousarray(hfull[b, s * SEG:s * SEG + TS])
            elif key == "cmeta":
                m[key] = host_cmeta(s, NSEG, SEG)
            else:
                name, L = key.rsplit("_L", 1)
                m[key] = lays[int(L)][name]
        in_maps.append(m)
    res = run_bass_kernel_spmd(P.nc, in_maps, core_ids=list(range(ncores))).results
    out = np.empty((NBATCH, NSEG * SEG, D), np.float32)
    for cid in range(ncores):
        b, s = cid // NSEG, cid % NSEG
        out[b, s * SEG:(s + 1) * SEG] = np.asarray(res[cid]["h_out"])[HALO:]
    return out
```

```python
import contextlib
import numpy as np
import ml_dtypes
import concourse.bass as bass
import concourse.mybir as mybir
from concourse.bass_utils import run_bass_kernel_spmd

F32 = mybir.dt.float32
BF16 = mybir.dt.bfloat16
AF = mybir.ActivationFunctionType
ALU = mybir.AluOpType

D = 4096
BW = 2048
EPS = 1e-6
NH, HD, NG, NST, XBC = 32, 64, 4, 128, 3072
MH, NOPE, ROPE, VD, QR, KVR = 16, 128, 64, 128, 1024, 512
IN_DIM = 12896
O_Z, O_XBC, O_DT, O_Q, O_KV, O_KR, O_MG, O_PU, O_PG = 0, 2048, 5120, 5152, 6176, 6688, 6752, 8800, 10848
HALO = 16
NEG = -30000.0

SSD_STOP = 9.0
V_CW, V_CB, V_DS, V_SN, V_QN, V_KN, V_PS, V_DTB, V_IF, NV = 0, 96, 120, 136, 152, 160, 164, 180, 181, 182
C_M0, C_VIS, C_OH, C_POS = 0, 1, 5, 9


class Buf:
    __slots__ = ("name", "lw", "rd", "excl")

    def __init__(self, name="", excl=False):
        self.name = name
        self.lw = None
        self.rd = []
        self.excl = excl


class Eng:
    def __init__(self, name, h, sem):
        self.name, self.h, self.sem = name, h, sem
        self.cnt = 0
        self.known = {}

    def wait_ev(self, ev):
        sem, val, _ = ev
        if self.known.get(id(sem), 0) < val:
            self.h.wait_ge(sem, val)
            self.known[id(sem)] = val


class Ctx:
    def __init__(self, nc, n_dma_sems=8):
        self.nc = nc
        self.es = contextlib.ExitStack()
        mk = lambda n: self.es.enter_context(nc.semaphore(n))
        self.pe = Eng("pe", nc.tensor, mk("s_pe"))
        self.act = Eng("act", nc.scalar, mk("s_act"))
        self.dve = Eng("dve", nc.vector, mk("s_dve"))
        self.pool = Eng("pool", nc.gpsimd, mk("s_pool"))
        self.sp = Eng("sp", nc.sync, mk("s_sp"))
        self.engs = [self.pe, self.act, self.dve, self.pool, self.sp]
        self.dsems = {}
        for e in (self.sp, self.pool, self.act):
            self.dsems[e.name] = [[mk(f"d_{e.name}{i}"), 0] for i in range(n_dma_sems)]
        self.drr = {e.name: 0 for e in (self.sp, self.pool, self.act)}
        self.n_ops = 0
        self.cc_sem = mk("s_cc")
        self.cc_cnt = 0

    def close(self):
        self.es.close()

    def _deps(self, eng, reads, writes, same_raw):
        for r in reads:
            if r.lw is not None and (r.lw[2] != eng.name or same_raw):
                eng.wait_ev(r.lw)
        for w in writes:
            if w.lw is not None and (w.lw[2] != eng.name or same_raw):
                eng.wait_ev(w.lw)
            for ev in w.rd:
                if ev[2] != eng.name:
                    eng.wait_ev(ev)

    def _commit(self, ev, reads, writes):
        for r in reads:
            r.rd.append(ev)
            if len(r.rd) > 48:
                last = {}
                for e in r.rd:
                    if id(e[0]) not in last or last[id(e[0])][1] < e[1]:
                        last[id(e[0])] = e
                r.rd = list(last.values())
        for w in writes:
            w.lw = ev
            w.rd = []

    def op(self, eng, fn, reads=(), writes=()):
        ex = [r for r in reads if r.excl]
        if ex:
            writes = list(writes) + ex
            reads = [r for r in reads if not r.excl]
        self._deps(eng, reads, writes, same_raw=(eng is not self.pe))
        ins = fn()
        ins.then_inc(eng.sem, 1)
        eng.cnt += 1
        ev = (eng.sem, eng.cnt, eng.name)
        self._commit(ev, reads, writes)
        self.n_ops += 1
        return ev

    def chain(self, eng, fns, reads=(), writes=()):
        ev = None
        for fn in fns:
            ev = self.op(eng, fn, reads=reads, writes=writes)
        return ev

    def dma(self, eng, out, in_, reads=(), writes=(), **kw):
        pool = self.dsems[eng.name]
        i = self.drr[eng.name]
        self.drr[eng.name] = (i + 1) % len(pool)
        slot = pool[i]
        if slot[1] > 0:
            eng.wait_ev((slot[0], slot[1], "dma"))
        self._deps(eng, reads, writes, same_raw=True)
        eng.h.dma_start(out=out, in_=in_, **kw).then_inc(slot[0], 16)
        slot[1] += 16
        ev = (slot[0], slot[1], "dma_" + eng.name + str(i))
        self._commit(ev, reads, writes)
        self.n_ops += 1
        return ev

    def coll(self, kind, groups, src, dst, reads=(), writes=()):
        eng = self.pool
        if self.cc_cnt > 0:
            eng.wait_ev((self.cc_sem, self.cc_cnt, "coll"))
        self._deps(eng, reads, writes, same_raw=True)
        self.nc.gpsimd.collective_compute(kind, ALU.bypass, replica_groups=groups, ins=[src.opt()], outs=[dst.opt()]).then_inc(self.cc_sem)
        self.cc_cnt += 1
        ev = (self.cc_sem, self.cc_cnt, "coll")
        self._commit(ev, reads, writes)
        return ev

    def barrier(self):
        evs = []
        for e in self.engs:
            if e.cnt > 0:
                evs.append((e.sem, e.cnt, e.name))
        for name, pool in self.dsems.items():
            for s in pool:
                if s[1] > 0:
                    evs.append((s[0], s[1], "dma"))
        if self.cc_cnt > 0:
            evs.append((self.cc_sem, self.cc_cnt, "coll"))
        for e in self.engs:
            for ev in evs:
                if ev[0] is not e.sem:
                    e.wait_ev(ev)


class Phase:
    _seq = [0]

    def __init__(self, P, name):
        Phase._seq[0] += 1
        self.P, self.name = P, f"{name}x{Phase._seq[0]}"
        self.es = contextlib.ExitStack()
        self.n = 0

    def __enter__(self):
        self.P.c.barrier()
        return self

    def __exit__(self, *a):
        self.P.c.barrier()
        self.es.close()
        return False

    def sb(self, shape, dt, name=None):
        self.n += 1
        return self.es.enter_context(self.P.nc.sbuf_tensor(f"{self.name}_{name or 's'}{self.n}", list(shape), dt))

    def ps(self, shape, dt, name=None):
        self.n += 1
        return self.es.enter_context(self.P.nc.psum_tensor(f"{self.name}_{name or 'p'}{self.n}", list(shape), dt))


def ttiles(TS, n):
    out = [(0, HALO)]
    t = HALO
    while t < TS:
        m = min(n, TS - t)
        out.append((t, m))
        t += m
    return out


class Prog:
    def __init__(self, SEG, NSEG, mode="B", dbg=False):
        self.SEG, self.NSEG, self.mode, self.dbg = SEG, NSEG, mode, dbg
        self.TS = TS = HALO + SEG
        self.NK = HALO + NSEG * SEG
        self.NKG = HALO + (NSEG - 1) * SEG
        self.NKC = self.NKG + SEG
        nc = self.nc = bass.Bass("TRN2", target_bir_lowering=False)
        self.c = Ctx(nc)
        skind = "ExternalOutput" if dbg else "Internal"
        dt_ = lambda n, s, t, k: nc.dram_tensor(n, list(s), t, kind=k).ap()
        self._dt = dt_
        NK = self.NK
        self.ispec = {
            "h": ([TS, D], F32), "cmeta": ([128, 16], F32), "vecs": ([128, NV], F32), "rows": ([1, 2 * D + 32], F32),
            "w_in": ([D, IN_DIM], F32), "w_gate": ([3, D, D], F32), "w_branch": ([3, BW, D], F32),
            "w_out": ([D, D], F32), "w_q_b": ([QR, MH * (NOPE + ROPE)], F32), "w_kv_b": ([KVR, MH * (NOPE + VD)], F32),
            "w_pool": ([4, 512, 512], F32), "kv_all": ([KVR + ROPE, NK], BF16), "L_all": ([NSEG, 128, BW], F32),
            "D_all": ([1, NSEG * NH], F32),
        }
        self.inputs = {}
        self.L = 0
        self.per_layer = {"vecs", "rows", "w_in", "w_gate", "w_branch", "w_out", "w_q_b", "w_kv_b", "w_pool"}
        if mode == "F":
            self.h_out = dt_("h_out", [TS, D], F32, "ExternalOutput")
            self.h1 = dt_("h1", [TS, D], F32, "Internal")
            self.L_out = dt_("L_loc", [128, BW], F32, "Internal")
            self.D_out = dt_("D_loc", [128, NH], F32, "Internal")
            self.kvc = [dt_(f"kvc{k}", [128 if k < 4 else ROPE, TS], BF16, "Internal") for k in range(5)]
            self.kvc_g = [dt_(f"kvcg{k}", [NSEG * (128 if k < 4 else ROPE), TS], BF16, "Internal") for k in range(5)]
            self.L_g = dt_("L_g", [NSEG * 128, BW], F32, "Internal")
            self.D_g = dt_("D_g", [NSEG * 128, NH], F32, "Internal")
            self.tail_loc = dt_("tail_loc", [HALO, D], F32, "Internal")
            self.tails_g = dt_("tails_g", [NSEG * HALO, D], F32, "Internal")
            self.groups = [list(range(b * NSEG, (b + 1) * NSEG)) for b in range(2)]
        elif mode == "B":
            self.h_out = dt_("h_out", [TS, D], F32, "ExternalOutput")
        else:
            self.kvx_out = dt_("kvx", [KVR + ROPE, TS], BF16, "ExternalOutput")
            self.L_out = dt_("L_out", [128, BW], F32, "ExternalOutput")
            self.D_out = dt_("D_out", [128, NH], F32, "ExternalOutput")
        self.xnT = dt_("xnT", [D, TS], BF16, skind)
        self.projT = dt_("projT", [IN_DIM, TS], BF16, skind)
        self.dtT = dt_("dtT", [NH, TS], F32, skind)
        self.xbcT = dt_("xbcT", [XBC, TS], BF16, skind)
        self.kvx = dt_("kvx_s", [KVR + ROPE, TS], BF16, skind) if mode in ("B", "F") else self.kvx_out
        self.h_src = None
        self.h_dst = None
        if mode in ("B", "F"):
            self.sigT = dt_("sigT", [3 * D, TS], BF16, skind)
            self.brT = dt_("brT", [3 * BW, TS], BF16, skind)
            self.mergedT = dt_("mergedT", [D, TS], BF16, skind)
            self.outF = dt_("outF", [TS, D], F32, skind)
        self.bufs = {}

    def I(self, name):
        key = f"{name}_L{self.L}" if (self.mode == "F" and name in self.per_layer) else name
        if key not in self.inputs:
            shp, t = self.ispec[name]
            self.inputs[key] = self._dt(key, shp, t, "ExternalInput")
        return self.inputs[key]

    def S(self, attr, name, shape, dt):
        if not hasattr(self, attr) or getattr(self, attr) is None:
            setattr(self, attr, self._dt(name, shape, dt, "Internal"))
        return getattr(self, attr)

    def hsrc(self):
        return self.h_src if self.h_src is not None else self.I("h")

    def hdst(self):
        return self.h_dst if self.h_dst is not None else self.h_out

    def kv_pieces(self, k):
        SEG, NSEG, TS = self.SEG, self.NSEG, self.TS
        r0 = k * 128
        nr = 128 if k < 4 else ROPE
        if self.mode != "F":
            return [(0, self.NKG, self.I("kv_all")[r0:r0 + nr, 0:self.NKG])]
        g = self.kvc_g[k]
        out = [(0, HALO, g[0:nr, 0:HALO])]
        for j in range(NSEG - 1):
            out.append((HALO + j * SEG, SEG, g[j * nr:(j + 1) * nr, HALO:TS]))
        return out

    def B(self, name):
        if name not in self.bufs:
            self.bufs[name] = Buf(name)
        return self.bufs[name]

    def load_consts(self, ph):
        nc, c = self.nc, self.c
        self.cmeta = ph.sb([128, 16], F32, "cmeta")
        self.vecs = ph.sb([128, NV], F32, "vecs")
        bc = self.B("consts")
        c.dma(c.sp, self.cmeta[:], self.I("cmeta")[:, :], writes=[bc])
        c.dma(c.sp, self.vecs[:], self.I("vecs")[:, :], writes=[bc])
        self.identf = ph.sb([128, 128], F32, "identf")
        self.ident = ph.sb([128, 128], BF16, "ident")
        self.onesf = ph.sb([128, 128], F32, "onesf")
        self.onesb = ph.sb([128, 128], BF16, "onesb")
        self.epsc = ph.sb([128, 1], F32, "epsc")

        c.chain(c.pool, [
            lambda: nc.gpsimd.memset(self.identf[:], 0.0),
            lambda: nc.gpsimd.memset(self.onesf[:], 1.0),
            lambda: nc.gpsimd.memset(self.onesb[:], 1.0),
            lambda: nc.gpsimd.memset(self.epsc[:], EPS),
            lambda: nc.gpsimd.affine_select(out=self.identf[:], in_=self.identf[:], pattern=[[-1, 128]],
                                            compare_op=ALU.not_equal, fill=1.0, base=0, channel_multiplier=1),
            lambda: nc.gpsimd.tensor_copy(out=self.ident[:], in_=self.identf[:]),
        ], writes=[bc])
        c.barrier()

    def ph_norm(self):
        nc, c, TS = self.nc, self.c, self.TS
        with Phase(self, "n1") as ph:
            prew = ph.sb([128, D], F32, "prew")
            bpw = self.B("prew")
            c.dma(c.sp, prew[:], self.I("rows")[0:1, 0:D].partition_broadcast(128), writes=[bpw])
            ht = [ph.sb([128, D], F32, f"ht{i}") for i in range(2)]
            junk = ph.sb([128, D], BF16, "junk")
            xnb = [ph.sb([128, D], BF16, f"xnb{i}") for i in range(2)]
            ss = [ph.sb([128, 1], F32, f"ss{i}") for i in range(2)]
            xT = [ph.sb([128, 32, 128], BF16, f"xT{i}") for i in range(2)]
            pT = [ph.ps([128, 8, 128], BF16, f"pT{i}") for i in range(4)]
            bht = [self.B(f"n1ht{i}") for i in range(2)]
            bxn = [self.B(f"n1xn{i}") for i in range(2)]
            bss = [self.B(f"n1ss{i}") for i in range(2)]
            bxT = [self.B(f"n1xT{i}") for i in range(2)]
            bpT = [self.B(f"n1pT{i}") for i in range(4)]
            bj = self.B("n1junk")
            xnT_v = self.xnT.rearrange("(kc p) t -> p kc t", p=128)
            npt = 0
            for it, (t0, nt) in enumerate(ttiles(TS, 128)):
                i = it % 2
                c.dma(c.sp, ht[i][0:nt, :], self.hsrc()[t0:t0 + nt, :], reads=[self.B("hres")], writes=[bht[i]])
                c.op(c.act, lambda: nc.scalar.activation(out=junk[0:nt, :], in_=ht[i][0:nt, :], func=AF.Square,
                                                         accum_out=ss[i][0:nt, :]),
                     reads=[bht[i]], writes=[bj, bss[i]])

                c.op(c.act, lambda: nc.scalar.activation(out=ss[i][0:nt, :], in_=ss[i][0:nt, :], func=AF.Sqrt,
                                                         scale=1.0 / D, bias=self.epsc[0:nt, :]),
                     reads=[bss[i]], writes=[bss[i]])
                c.op(c.dve, lambda: nc.vector.reciprocal(out=ss[i][0:nt, :], in_=ss[i][0:nt, :]),
                     reads=[bss[i]], writes=[bss[i]])
                c.op(c.dve, lambda: nc.vector.scalar_tensor_tensor(out=xnb[i][0:nt, :], in0=ht[i][0:nt, :],
                                                                   scalar=ss[i][0:nt, 0:1], in1=prew[0:nt, :],
                                                                   op0=ALU.mult, op1=ALU.mult),
                     reads=[bht[i], bss[i], bpw], writes=[bxn[i]])
                for g in range(4):
                    j = npt % 4
                    npt += 1

                    def ft():
                        for k in range(8):
                            ins = nc.tensor.transpose(pT[j][:, k, 0:nt], xnb[i][0:nt, (g * 8 + k) * 128:(g * 8 + k + 1) * 128],
                                                      self.ident[0:nt, 0:nt])
                        return ins
                    c.op(c.pe, ft, reads=[bxn[i]], writes=[bpT[j]])
                    if g % 2 == 0:
                        c.op(c.act, lambda: nc.scalar.copy(out=xT[i][:, g * 8:(g + 1) * 8, 0:nt], in_=pT[j][:, :, 0:nt]),
                             reads=[bpT[j]], writes=[bxT[i]])
                    else:
                        c.op(c.dve, lambda: nc.vector.tensor_copy(out=xT[i][:, g * 8:(g + 1) * 8, 0:nt], in_=pT[j][:, :, 0:nt]),
                             reads=[bpT[j]], writes=[bxT[i]])
                c.dma(c.pool, xnT_v[:, :, t0:t0 + nt], xT[i][:, :, 0:nt], reads=[bxT[i]], writes=[self.B("xnT")])

    def gemm_A(self, ph, actT, KC, blocks, tag, n_tile=512, tts=None, width=None):
        nc, c, TS = self.nc, self.c, (width or self.TS)
        wst = [ph.sb([128, KC, 128], F32, f"wst{i}") for i in range(2)]
        wbf = [ph.sb([128, KC, 128], BF16, f"wbf{i}") for i in range(2)]
        ost_b = [ph.sb([128, TS], BF16, f"ostb{i}") for i in range(2)]
        ost_f = [ph.sb([128, TS], F32, f"ostf{i}") for i in range(2)] if any(b.get("odt", BF16) == F32 for b in blocks) else None
        pss = [ph.ps([128, 512], F32, f"ps{i}") for i in range(4)]
        bw = [self.B(f"{tag}wst{i}") for i in range(2)]
        bwb = [self.B(f"{tag}wbf{i}") for i in range(2)]
        bo = [self.B(f"{tag}ost{i}") for i in range(2)]
        bp = [self.B(f"{tag}ps{i}") for i in range(4)]
        bact = self.B("actres")
        if any(b.get("pm") is not None for b in blocks):
            pmb = [ph.sb([128, TS], BF16, f"pmb{i}") for i in range(2)]
            pmf = [ph.sb([128, TS], F32, f"pmf{i}") for i in range(2)]
            evf = [ph.sb([128, 512], F32, f"evf{i}") for i in range(2)]
            bpmb = [self.B(f"{tag}pmb{i}") for i in range(2)]
            bpm = [self.B(f"{tag}pm{i}") for i in range(2)]
            bev = [self.B(f"{tag}ev{i}") for i in range(2)]
        tts = tts or ttiles(TS, n_tile)
        npp = 0
        for ib, blk in enumerate(blocks):
            i = ib % 2
            mw = blk["mw"]
            Wv = blk["W"].rearrange("(kc p) m -> p kc m", p=128)
            c.dma(c.sp, wst[i][:, :, 0:mw], Wv, writes=[bw[i]])
            half = KC // 2 if KC >= 2 else KC
            c.op(c.dve, lambda: nc.vector.tensor_copy(out=wbf[i][:, 0:half, 0:mw], in_=wst[i][:, 0:half, 0:mw]),
                 reads=[bw[i]], writes=[bwb[i]])
            if half < KC:
                c.op(c.pool, lambda: nc.gpsimd.tensor_copy(out=wbf[i][:, half:KC, 0:mw], in_=wst[i][:, half:KC, 0:mw]),
                     reads=[bw[i]], writes=[bwb[i]])
            odt = blk.get("odt", BF16)
            ost = ost_b[i] if odt == BF16 else ost_f[i]
            if blk.get("pm") is not None:
                src, pfunc = blk["pm"]
                c.dma(c.sp, pmb[i][0:mw, :], src, writes=[bpmb[i]])
                c.op(c.act, lambda: nc.scalar.activation(out=pmf[i][0:mw, :], in_=pmb[i][0:mw, :], func=pfunc),
                     reads=[bpmb[i]], writes=[bpm[i]])
            for (t0, nt) in tts:
                j = npp % 4
                npp += 1

                def fm():
                    for k in range(KC):
                        ins = nc.tensor.matmul(pss[j][0:mw, 0:nt], lhsT=wbf[i][:, k, 0:mw], rhs=actT[:, k, t0:t0 + nt],
                                               start=(k == 0), stop=(k == KC - 1))
                    return ins
                c.op(c.pe, fm, reads=[bwb[i], bact], writes=[bp[j]])
                kw = {}
                if blk.get("bias") is not None:
                    kw["bias"] = blk["bias"]
                if blk.get("pm") is None:
                    c.op(c.act, lambda: nc.scalar.activation(out=ost[0:mw, t0:t0 + nt], in_=pss[j][0:mw, 0:nt],
                                                             func=blk.get("func", AF.Copy), scale=blk.get("scale", 1.0), **kw),
                         reads=[bp[j]], writes=[bo[i]])
                else:
                    jj = npp % 2
                    c.op(c.act, lambda: nc.scalar.activation(out=evf[jj][0:mw, 0:nt], in_=pss[j][0:mw, 0:nt],
                                                             func=blk.get("func", AF.Copy), scale=blk.get("scale", 1.0), **kw),
                         reads=[bp[j]], writes=[bev[jj]])
                    c.op(c.dve, lambda: nc.vector.tensor_tensor(out=ost[0:mw, t0:t0 + nt], in0=evf[jj][0:mw, 0:nt],
                                                                in1=pmf[i][0:mw, t0:t0 + nt], op=ALU.mult),
                         reads=[bev[jj], bpm[i]], writes=[bo[i]])
            c.dma(c.pool, blk["out"], ost[0:mw, :], reads=[bo[i]], writes=[self.B(blk.get("obuf", "gemm_out"))])

    def load_actT(self, ph, src, KC, name="actT"):
        c = self.c
        t = ph.sb([128, KC, self.TS], BF16, name)
        v = src.rearrange("(kc p) t -> p kc t", p=128)
        step = max(1, KC // 4)
        for k0 in range(0, KC, step):
            c.dma(c.sp, t[:, k0:k0 + step, :], v[:, k0:k0 + step, :], writes=[self.B("actres")])
        return t

    def ph_inproj(self, col_ranges, with_gates):
        nc, c = self.nc, self.c
        with Phase(self, "g2") as ph:
            actT = self.load_actT(ph, self.xnT, 32)
            blocks = []
            for (c0, c1) in col_ranges:
                cc = c0
                while cc < c1:
                    lim = c1
                    for b in (O_DT, O_Q, O_KR, O_MG):
                        if cc < b < lim:
                            lim = b
                    mw = min(128, lim - cc)
                    if cc == O_DT:
                        blocks.append(dict(W=self.I("w_in")[:, cc:cc + mw], mw=mw, out=self.dtT[:, :], odt=F32))
                    else:
                        blocks.append(dict(W=self.I("w_in")[:, cc:cc + mw], mw=mw, out=self.projT[cc:cc + mw, :]))
                    cc += mw
            if with_gates:
                for i in range(3):
                    for m in range(32):
                        blocks.append(dict(W=self.I("w_gate")[i, :, m * 128:(m + 1) * 128], mw=128,
                                           out=self.sigT[i * D + m * 128:i * D + (m + 1) * 128, :], func=AF.Sigmoid))
            self.gemm_A(ph, actT, 32, blocks, "g2")


    def TB(self, ph, shape, dt, name, psum=False):
        t = ph.ps(shape, dt, name) if psum else ph.sb(shape, dt, name)
        b = self.B(f"{ph.name}_{name}_{ph.n}")
        b.excl = psum
        return t, b

    def ph_conv(self):
        nc, c, TS = self.nc, self.c, self.TS
        with Phase(self, "cv") as ph:
            xin = [self.TB(ph, [128, TS + 3], BF16, f"xin{i}") for i in range(2)]
            acc = [self.TB(ph, [128, TS], F32, f"acc{i}") for i in range(2)]
            ot = [self.TB(ph, [128, TS], BF16, f"ot{i}") for i in range(2)]
            for i in range(2):
                c.op(c.pool, lambda: nc.gpsimd.memset(xin[i][0][:, 0:3], 0.0), writes=[xin[i][1]])
            for kc in range(24):
                i = kc % 2
                x, bx = xin[i]
                a, ba = acc[i]
                o, bo = ot[i]
                c.dma(c.sp, x[:, 3:3 + TS], self.projT[O_XBC + kc * 128:O_XBC + (kc + 1) * 128, :], writes=[bx])
                w = lambda k: self.vecs[:, V_CW + kc * 4 + k:V_CW + kc * 4 + k + 1]

                fns = [lambda: nc.vector.tensor_scalar(out=a[:], in0=x[:, 0:TS], scalar1=w(0), scalar2=None, op0=ALU.mult)]
                for k in range(1, 4):
                    fns.append(lambda k=k: nc.vector.scalar_tensor_tensor(out=a[:], in0=x[:, k:k + TS], scalar=w(k), in1=a[:],
                                                                          op0=ALU.mult, op1=ALU.add))
                c.chain(c.dve, fns, reads=[bx], writes=[ba])
                c.op(c.act, lambda: nc.scalar.activation(out=o[:], in_=a[:], func=AF.Silu,
                                                         bias=self.vecs[:, V_CB + kc:V_CB + kc + 1]),
                     reads=[ba], writes=[bo])
                c.dma(c.pool, self.xbcT[kc * 128:(kc + 1) * 128, :], o[:], reads=[bo], writes=[self.B("xbcT")])

    def ph_ssd(self, with_output):
        nc, c, TS, SEG, NSEG = self.nc, self.c, self.TS, self.SEG, self.NSEG
        with Phase(self, "sd") as ph:
            xbc = self.load_actT(ph, self.xbcT, 24, "xbc")
            bxbc = self.B("actres")
            if with_output:
                zc = [self.TB(ph, [128, 16, 64], BF16, f"zc{i}") for i in range(2)]
                zv = self.projT[0:BW, :].rearrange("(kc p) t -> p kc t", p=128)
            dts, bdts = self.TB(ph, [32, TS], F32, "dts")
            negA, bnA = self.TB(ph, [128, 32], F32, "negA")
            c.dma(c.sp, dts[:], self.dtT[:, :], writes=[bdts])
            c.dma(c.sp, negA[:], self.I("rows")[0:1, 2 * D:2 * D + 32].partition_broadcast(128), writes=[bnA])

            c.chain(c.act, [
                lambda: nc.scalar.activation(out=dts[:], in_=dts[:], func=AF.Exp, bias=self.vecs[0:32, V_DTB:V_DTB + 1]),
                lambda: nc.scalar.activation(out=dts[:], in_=dts[:], func=AF.Ln, bias=self.onesf[0:32, 0:1]),
            ], reads=[bdts], writes=[bdts])
            c.op(c.act, lambda: nc.scalar.activation(out=negA[:], in_=negA[:], func=AF.Exp), reads=[bnA], writes=[bnA])
            c.op(c.dve, lambda: nc.vector.tensor_scalar(out=negA[:], in0=negA[:], scalar1=-1.0, scalar2=None, op0=ALU.mult),
                 reads=[bnA], writes=[bnA])
            c.op(c.dve, lambda: nc.vector.tensor_scalar(out=dts[:, 0:HALO], in0=dts[:, 0:HALO],
                                                        scalar1=self.cmeta[0:32, C_M0:C_M0 + 1], scalar2=None, op0=ALU.mult),
                 reads=[bdts], writes=[bdts])
            tri, btri = self.TB(ph, [64, 64], F32, "tri")
            t2, bt2 = self.TB(ph, [64, 64], F32, "t2")
            ones64 = self.onesf

            c.chain(c.pool, [
                lambda: nc.gpsimd.memset(tri[:], 1.0),
                lambda: nc.gpsimd.memset(t2[:], 1.0),
                lambda: nc.gpsimd.affine_select(out=tri[:], in_=tri[:], pattern=[[1, 64]], compare_op=ALU.is_ge, fill=0.0,
                                                base=0, channel_multiplier=-1),
                lambda: nc.gpsimd.affine_select(out=t2[:], in_=t2[:], pattern=[[-1, 64]], compare_op=ALU.is_gt, fill=0.0,
                                                base=0, channel_multiplier=1),
            ], writes=[btri, bt2])
            S, bS = self.TB(ph, [128, BW], F32, "S")
            Sb, bSb = self.TB(ph, [128, BW], BF16, "Sb")
            tacc, btacc = self.TB(ph, [128, 32], F32, "tacc")
            stmp, bstmp = self.TB(ph, [128, 512], F32, "stmp")
            c.op(c.dve, lambda: nc.vector.memset(S[:], 0.0), writes=[bS])
            c.op(c.dve, lambda: nc.vector.memset(tacc[:], 0.0), writes=[btacc])
            if with_output and NSEG > 1:
                Dall, bD = self.TB(ph, [128, NSEG * NH], F32, "Dall")
                if self.mode == "F":
                    for j in range(NSEG):
                        c.dma(c.sp, Dall[:, j * NH:(j + 1) * NH], self.D_g[j * 128:j * 128 + 1, :].partition_broadcast(128),
                              reads=[self.B("ssdg")], writes=[bD])
                else:
                    c.dma(c.sp, Dall[:], self.I("D_all")[0:1, :].partition_broadcast(128), writes=[bD])
                T, bT = self.TB(ph, [128, BW], F32, "T")
                Lj, bLj = self.TB(ph, [128, BW], F32, "Lj")
                c.op(c.dve, lambda: nc.vector.memset(T[:], 0.0), writes=[bT])
                for j in range(NSEG - 1):
                    Lsrc = self.L_g[j * 128:(j + 1) * 128, :] if self.mode == "F" else self.I("L_all")[j, :, :]
                    c.dma(c.sp, Lj[:], Lsrc, reads=[self.B("ssdg")], writes=[bLj])
                    Tv = T[:].rearrange("p (h d) -> p h d", d=HD)
                    c.op(c.dve, lambda: nc.vector.tensor_tensor(
                        out=Tv, in0=Tv, in1=Dall[:, j * NH:(j + 1) * NH].unsqueeze(2).to_broadcast([128, NH, HD]), op=ALU.mult),
                        reads=[bT, bD], writes=[bT])
                    c.op(c.dve, lambda: nc.vector.tensor_tensor(out=T[:], in0=T[:], in1=Lj[:], op=ALU.add),
                         reads=[bT, bLj], writes=[bT])
                    c.op(c.dve, lambda: nc.vector.scalar_tensor_tensor(out=S[:], in0=T[:], scalar=self.cmeta[:, C_OH + j:C_OH + j + 1],
                                                                       in1=S[:], op0=ALU.mult, op1=ALU.add),
                         reads=[bT, bS], writes=[bS])
            c.op(c.act, lambda: nc.scalar.copy(out=Sb[:], in_=S[:]), reads=[bS], writes=[bSb])
            pA, bpA = self.TB(ph, [128, 512], F32, "pA", psum=True)
            pX, bpX = self.TB(ph, [128, 1024], BF16, "pX", psum=True)
            pR, bpR = self.TB(ph, [128, 512], F32, "pR", psum=True)
            pY, bpY = self.TB(ph, [128, 512], F32, "pY", psum=True)
            pYo, bpYo = self.TB(ph, [128, 512], F32, "pYo", psum=True)
            pSt, bpSt = self.TB(ph, [128, 512], F32, "pSt", psum=True)
            pYT, bpYT = self.TB(ph, [128, 16, 64], F32, "pYT", psum=True)
            dtk, bdtk = self.TB(ph, [64, 32], F32, "dtk")
            dak, bdak = self.TB(ph, [64, 32], F32, "dak")
            acs, bacs = self.TB(ph, [64, 32], F32, "acs")
            ea, bea = self.TB(ph, [64, 32], F32, "ea")
            dte, bdte = self.TB(ph, [64, 32], F32, "dte")
            cdec, bcdec = self.TB(ph, [128, 32], F32, "cdec")
            xdt, bxdt = self.TB(ph, [64, 512], BF16, "xdt")
            xdtw, bxdtw = self.TB(ph, [64, 512], BF16, "xdtw")
            btok, bbtok = self.TB(ph, [64, 128], BF16, "btok")
            Xg, bXg = self.TB(ph, [64, 512], F32, "Xg")
            dec, bdec = self.TB(ph, [64, 512], F32, "dec")
            gm, bgm = self.TB(ph, [64, 64], F32, "gm")
            mt, bmt = self.TB(ph, [64, 512], BF16, "mt")
            ytmp, bytmp = self.TB(ph, [64, 512], F32, "ytmp")
            ytok, bytok = self.TB(ph, [64, BW], F32, "ytok")
            y1, by1 = self.TB(ph, [128, 16, 64], F32, "y1")
            sz, bsz = self.TB(ph, [128, 16, 64], F32, "sz")
            sq, bsq = self.TB(ph, [128, 16, 64], BF16, "sq")
            rs, brs = self.TB(ph, [128, 4, 64], F32, "rs")
            yo = [self.TB(ph, [128, 16, 64], BF16, f"yo{i}") for i in range(2)]
            brv = self.brT[0:BW, :].rearrange("(kc p) t -> p kc t", p=128) if with_output else None
            chunks = [(0, HALO)] + [(HALO + 64 * i, 64) for i in range(SEG // 64)]
            if SSD_STOP == 0:
                chunks = []
            for ic, (t0, L) in enumerate(chunks):
                c.op(c.pe, lambda: nc.tensor.transpose(pA[0:L, 0:32], dts[0:32, t0:t0 + L], self.identf[0:32, 0:32]),
                     reads=[bdts], writes=[bpA])
                c.op(c.act, lambda: nc.scalar.copy(out=dtk[0:L, :], in_=pA[0:L, 0:32]), reads=[bpA], writes=[bdtk])
                c.op(c.dve, lambda: nc.vector.tensor_tensor(out=dak[0:L, :], in0=dtk[0:L, :], in1=negA[0:L, :], op=ALU.mult),
                     reads=[bdtk, bnA], writes=[bdak])

                def fcs():
                    nc.tensor.matmul(pA[0:L, 32:64], lhsT=tri[0:L, 0:L], rhs=dak[0:L, :], start=True, stop=True)
                    nc.tensor.matmul(pA[0:L, 64:96], lhsT=ones64[0:L, 0:L], rhs=dak[0:L, :], start=True, stop=True)
                    return nc.tensor.matmul(pA[0:128, 96:128], lhsT=ones64[0:L, 0:128], rhs=dak[0:L, :], start=True, stop=True)
                c.op(c.pe, fcs, reads=[bdak, btri], writes=[bpA])
                c.op(c.act, lambda: nc.scalar.copy(out=acs[0:L, :], in_=pA[0:L, 32:64]), reads=[bpA], writes=[bacs])
                c.op(c.act, lambda: nc.scalar.activation(out=ea[0:L, :], in_=pA[0:L, 32:64], func=AF.Exp), reads=[bpA], writes=[bea])
                c.op(c.dve, lambda: nc.vector.tensor_tensor(out=dte[0:L, :], in0=pA[0:L, 64:96], in1=acs[0:L, :], op=ALU.subtract),
                     reads=[bpA, bacs], writes=[bdte])
                c.op(c.act, lambda: nc.scalar.activation(out=dte[0:L, :], in_=dte[0:L, :], func=AF.Exp), reads=[bdte], writes=[bdte])
                c.op(c.act, lambda: nc.scalar.activation(out=cdec[:], in_=pA[0:128, 96:128], func=AF.Exp), reads=[bpA], writes=[bcdec])
                c.op(c.dve, lambda: nc.vector.tensor_tensor(out=tacc[:], in0=tacc[:], in1=pA[0:128, 96:128], op=ALU.add),
                     reads=[bpA, btacc], writes=[btacc])
                if SSD_STOP <= 1:
                    continue
                for g in range(NG):
                    hs = slice(8 * g, 8 * g + 8)
                    gs = slice(512 * g, 512 * (g + 1))
                    def ftr():
                        for q in range(4):
                            nc.tensor.transpose(pX[0:L, q * 128:(q + 1) * 128], xbc[:, 4 * g + q, t0:t0 + L], self.ident[:, :])
                        return nc.tensor.transpose(pX[0:L, 512:640], xbc[:, 16 + g, t0:t0 + L], self.ident[:, :])
                    c.op(c.pe, ftr, reads=[bxbc], writes=[bpX])
                    if SSD_STOP <= 1.1:
                        continue
                    bc8 = lambda t: t[0:L, hs].unsqueeze(2).to_broadcast([L, 8, HD])
                    v3 = lambda t: t[0:L, 0:512].rearrange("p (h d) -> p h d", d=HD)
                    c.op(c.dve, lambda: nc.vector.tensor_tensor(out=v3(xdt), in0=v3(pX), in1=bc8(dtk), op=ALU.mult),
                         reads=[bpX, bdtk], writes=[bxdt])
                    c.op(c.act, lambda: nc.scalar.copy(out=btok[0:L, :], in_=pX[0:L, 512:640]), reads=[bpX], writes=[bbtok])
                    c.op(c.dve, lambda: nc.vector.tensor_tensor(out=v3(xdtw), in0=v3(xdt), in1=bc8(dte), op=ALU.mult),
                         reads=[bxdt, bdte], writes=[bxdtw])
                    if with_output and SSD_STOP > 2:
                        vL = lambda t: t[0:L, 0:8 * L].rearrange("p (h l) -> p h l", l=L)
                        c.op(c.dve, lambda: nc.vector.tensor_tensor(
                            out=vL(Xg), in0=dak[0:L, hs].unsqueeze(2).to_broadcast([L, 8, L]),
                            in1=tri[0:L, 0:L].unsqueeze(1).to_broadcast([L, 8, L]), op=ALU.mult),
                            reads=[bdak, btri], writes=[bXg])
                        c.op(c.pe, lambda: nc.tensor.matmul(pR[0:L, 0:8 * L], lhsT=t2[0:L, 0:L], rhs=Xg[0:L, 0:8 * L], start=True, stop=True),
                             reads=[bXg, bt2], writes=[bpR])
                        c.op(c.act, lambda: nc.scalar.activation(out=dec[0:L, 0:8 * L], in_=pR[0:L, 0:8 * L], func=AF.Exp),
                             reads=[bpR], writes=[bdec])
                        c.op(c.pe, lambda: nc.tensor.matmul(pA[0:L, 128:128 + L], lhsT=xbc[:, 16 + g, t0:t0 + L], rhs=xbc[:, 20 + g, t0:t0 + L],
                                                            start=True, stop=True),
                             reads=[bxbc], writes=[bpA])
                        c.op(c.dve, lambda: nc.vector.tensor_tensor(out=gm[0:L, 0:L], in0=pA[0:L, 128:128 + L], in1=tri[0:L, 0:L], op=ALU.mult),
                             reads=[bpA, btri], writes=[bgm])
                        c.op(c.dve, lambda: nc.vector.tensor_tensor(out=vL(mt), in0=vL(dec),
                                                                    in1=gm[0:L, 0:L].unsqueeze(1).to_broadcast([L, 8, L]), op=ALU.mult),
                             reads=[bdec, bgm], writes=[bmt])

                        def fy():
                            for h8 in range(8):
                                ins = nc.tensor.matmul(pY[0:L, h8 * HD:(h8 + 1) * HD], lhsT=mt[0:L, h8 * L:(h8 + 1) * L],
                                                       rhs=xdt[0:L, h8 * HD:(h8 + 1) * HD], start=True, stop=True)
                            return ins
                        c.op(c.pe, fy, reads=[bmt, bxdt], writes=[bpY])
                        c.op(c.pe, lambda: nc.tensor.matmul(pYo[0:L, :], lhsT=xbc[:, 20 + g, t0:t0 + L], rhs=Sb[:, gs], start=True, stop=True),
                             reads=[bxbc, bSb], writes=[bpYo])
                        c.op(c.dve, lambda: nc.vector.tensor_tensor(out=v3(ytmp), in0=v3(pYo), in1=bc8(ea), op=ALU.mult),
                             reads=[bpYo, bea], writes=[bytmp])
                        c.op(c.dve, lambda: nc.vector.tensor_tensor(out=ytok[0:L, gs], in0=ytmp[0:L, :], in1=pY[0:L, :], op=ALU.add),
                             reads=[bytmp, bpY], writes=[bytok])
                    if SSD_STOP <= 1.2:
                        continue
                    c.op(c.pe, lambda: nc.tensor.matmul(pSt[:, :], lhsT=btok[0:L, :], rhs=xdtw[0:L, :], start=True, stop=True),
                         reads=[bbtok, bxdtw], writes=[bpSt])
                    Sv = S[:, gs].rearrange("p (h d) -> p h d", d=HD)
                    c.op(c.dve, lambda: nc.vector.tensor_tensor(out=stmp[:].rearrange("p (h d) -> p h d", d=HD), in0=Sv,
                                                                in1=cdec[:, hs].unsqueeze(2).to_broadcast([128, 8, HD]), op=ALU.mult),
                         reads=[bS, bcdec], writes=[bstmp])
                    c.op(c.dve, lambda: nc.vector.tensor_tensor(out=S[:, gs], in0=stmp[:], in1=pSt[:, :], op=ALU.add),
                         reads=[bstmp, bpSt], writes=[bS])
                    c.op(c.act, lambda: nc.scalar.copy(out=Sb[:, gs], in_=S[:, gs]), reads=[bS], writes=[bSb])
                if not with_output or SSD_STOP <= 3:
                    continue
                def fyt():
                    for q in range(16):
                        ins = nc.tensor.transpose(pYT[:, q, 0:L], ytok[0:L, q * 128:(q + 1) * 128], self.identf[0:L, 0:L])
                    return ins
                c.op(c.pe, fyt, reads=[bytok], writes=[bpYT])
                bq = lambda col: self.vecs[:, col:col + 16].unsqueeze(2).to_broadcast([128, 16, L])
                c.op(c.dve, lambda: nc.vector.tensor_tensor(out=y1[:, :, 0:L], in0=xbc[:, 0:16, t0:t0 + L], in1=bq(V_DS), op=ALU.mult),
                     reads=[bxbc], writes=[by1])
                c.op(c.dve, lambda: nc.vector.tensor_tensor(out=y1[:, :, 0:L], in0=y1[:, :, 0:L], in1=pYT[:, :, 0:L], op=ALU.add),
                     reads=[by1, bpYT], writes=[by1])
                zt_, bz = zc[ic % 2]
                c.dma(c.sp, zt_[:, :, 0:L], zv[:, :, t0:t0 + L], writes=[bz])
                c.op(c.act, lambda: nc.scalar.activation(out=sz[:, :, 0:L], in_=zt_[:, :, 0:L], func=AF.Silu), reads=[bz], writes=[bsz])
                c.op(c.dve, lambda: nc.vector.tensor_tensor(out=y1[:, :, 0:L], in0=y1[:, :, 0:L], in1=sz[:, :, 0:L], op=ALU.mult),
                     reads=[by1, bsz], writes=[by1])
                c.op(c.dve, lambda: nc.vector.tensor_tensor(out=sq[:, :, 0:L], in0=y1[:, :, 0:L], in1=y1[:, :, 0:L], op=ALU.mult),
                     reads=[by1], writes=[bsq])

                def fss():
                    for g in range(4):
                        for q in range(4):
                            ins = nc.tensor.matmul(pR[:, g * 64:g * 64 + L], lhsT=self.onesb[:, :], rhs=sq[:, 4 * g + q, 0:L],
                                                   start=(q == 0), stop=(q == 3))
                    return ins
                c.op(c.pe, fss, reads=[bsq], writes=[bpR])
                pRv = pR[:, 0:256].rearrange("p (g l) -> p g l", l=64)
                c.op(c.act, lambda: nc.scalar.activation(out=rs[:, :, 0:L], in_=pRv[:, :, 0:L], func=AF.Sqrt, scale=1.0 / 512,
                                                         bias=self.epsc[:, :]),
                     reads=[bpR], writes=[brs])
                c.op(c.dve, lambda: nc.vector.reciprocal(out=rs[:, :, 0:L], in_=rs[:, :, 0:L]), reads=[brs], writes=[brs])
                y1v = y1[:, :, 0:L].rearrange("p (g q) l -> p g q l", q=4)
                c.op(c.dve, lambda: nc.vector.tensor_tensor(out=y1v, in0=y1v, in1=rs[:, :, 0:L].unsqueeze(2).to_broadcast([128, 4, 4, L]),
                                                            op=ALU.mult),
                     reads=[by1, brs], writes=[by1])
                o, bo = yo[ic % 2]
                c.op(c.dve, lambda: nc.vector.tensor_tensor(out=o[:, :, 0:L], in0=y1[:, :, 0:L], in1=bq(V_SN), op=ALU.mult),
                     reads=[by1], writes=[bo])
                c.dma(c.pool, brv[:, :, t0:t0 + L], o[:, :, 0:L], reads=[bo], writes=[self.B("brT")])
            if not with_output:
                c.dma(c.pool, self.L_out[:, :], S[:], reads=[bS], writes=[self.B("ssdl")])
                c.op(c.act, lambda: nc.scalar.activation(out=tacc[:], in_=tacc[:], func=AF.Exp), reads=[btacc], writes=[btacc])
                c.dma(c.pool, self.D_out[:, :], tacc[:], reads=[btacc], writes=[self.B("ssdl")])


    def gemm_B(self, ph, actT, KC, NT, blocks, tag, odt):
        nc, c = self.nc, self.c
        NW = max(b["nw"] for b in blocks)
        wst, bw = self.TB(ph, [128, KC, NW], F32, "bwst")
        wbf = [self.TB(ph, [128, KC, NW], BF16, f"bwbf{i}") for i in range(2)]
        ost = [self.TB(ph, [128, NW], odt, f"bost{i}") for i in range(3)]
        pss = [self.TB(ph, [128, 512], F32, f"bps{i}", psum=True) for i in range(3)]
        bact = self.B("actres")
        n = 0
        for ib, blk in enumerate(blocks):
            nw = blk["nw"]
            wb, bwb = wbf[ib % 2]
            step = max(1, KC // 4)
            if blk.get("Wv") is not None:
                Wv = blk["Wv"]
                for k0 in range(0, KC, step):
                    c.dma(c.sp, wst[:, k0:k0 + step, 0:nw].rearrange("p k (a b) -> p k a b", b=Wv.shape[3]), Wv[:, k0:k0 + step, :, :],
                          writes=[bw])
            else:
                Wv = blk["W"].rearrange("(kc p) m -> p kc m", p=128)
                for k0 in range(0, KC, step):
                    c.dma(c.sp, wst[:, k0:k0 + step, 0:nw], Wv[:, k0:k0 + step, :], writes=[bw])
            half = max(1, KC // 2)
            c.op(c.dve, lambda: nc.vector.tensor_copy(out=wb[:, 0:half, 0:nw], in_=wst[:, 0:half, 0:nw]), reads=[bw], writes=[bwb])
            if half < KC:
                c.op(c.pool, lambda: nc.gpsimd.tensor_copy(out=wb[:, half:KC, 0:nw], in_=wst[:, half:KC, 0:nw]), reads=[bw], writes=[bwb])
            t0 = 0
            while t0 < NT:
                nt = min(128, NT - t0)
                p, bp = pss[n % 3]
                o, bo = ost[n % 3]
                n += 1

                def fm():
                    for k in range(KC):
                        ins = nc.tensor.matmul(p[0:nt, 0:nw], lhsT=actT[:, k, t0:t0 + nt], rhs=wb[:, k, 0:nw],
                                               start=(k == 0), stop=(k == KC - 1))
                    return ins
                c.op(c.pe, fm, reads=[bwb, bact], writes=[bp])
                c.op(c.act, lambda: nc.scalar.copy(out=o[0:nt, 0:nw], in_=p[0:nt, 0:nw]), reads=[bp], writes=[bo])
                c.dma(c.pool, blk["out"][t0:t0 + nt, :], o[0:nt, 0:nw], reads=[bo], writes=[self.B("gemmB_out")])
                t0 += nt

    def fm_rmsnorm(self, ph, row0, KC, wcol, out_dram, tag):
        nc, c, TS = self.nc, self.c, self.TS
        x, bx = self.TB(ph, [128, KC, TS], BF16, tag + "x")
        sq, bsq = self.TB(ph, [128, KC, 512], BF16, tag + "sq")
        rs, brs = self.TB(ph, [128, 512], F32, tag + "rs")
        o, bo = self.TB(ph, [128, KC, TS], BF16, tag + "o")
        p, bp = self.TB(ph, [128, 512], F32, tag + "p", psum=True)
        c.dma(c.sp, x[:], self.projT[row0:row0 + KC * 128, :].rearrange("(kc p) t -> p kc t", p=128), writes=[bx])
        for (t0, nt) in ttiles(TS, 512):
            c.op(c.dve, lambda: nc.vector.tensor_tensor(out=sq[:, :, 0:nt], in0=x[:, :, t0:t0 + nt], in1=x[:, :, t0:t0 + nt], op=ALU.mult),
                 reads=[bx], writes=[bsq])

            def fs():
                for k in range(KC):
                    ins = nc.tensor.matmul(p[:, 0:nt], lhsT=self.onesb[:, :], rhs=sq[:, k, 0:nt], start=(k == 0), stop=(k == KC - 1))
                return ins
            c.op(c.pe, fs, reads=[bsq], writes=[bp])
            c.op(c.act, lambda: nc.scalar.activation(out=rs[:, 0:nt], in_=p[:, 0:nt], func=AF.Sqrt, scale=1.0 / (KC * 128),
                                                     bias=self.epsc[:, :]), reads=[bp], writes=[brs])
            c.op(c.dve, lambda: nc.vector.reciprocal(out=rs[:, 0:nt], in_=rs[:, 0:nt]), reads=[brs], writes=[brs])
            for k in range(KC):
                c.op(c.dve, lambda: nc.vector.scalar_tensor_tensor(out=o[:, k, t0:t0 + nt], in0=x[:, k, t0:t0 + nt],
                                                                   scalar=self.vecs[:, wcol + k:wcol + k + 1], in1=rs[:, 0:nt],
                                                                   op0=ALU.mult, op1=ALU.mult),
                     reads=[bx, brs], writes=[bo])
        c.dma(c.pool, out_dram.rearrange("(kc p) t -> p kc t", p=128), o[:], reads=[bo], writes=[self.B(tag + "out")])

    def make_rope(self, ph):
        nc, c, TS = self.nc, self.c, self.TS
        ang, ba = self.TB(ph, [64, TS], F32, "ang")
        self.cos2, self.bcos = self.TB(ph, [64, TS], F32, "cos2")
        self.sin2, self.bsin = self.TB(ph, [64, TS], F32, "sin2")
        fr, bfr = self.TB(ph, [64, 1], F32, "fr")
        rmf, brm = self.TB(ph, [64, 64], F32, "rmf")
        self.rm, self.brm = self.TB(ph, [64, 64], BF16, "rm")
        pi = float(np.pi)
        negpi, bnp = self.TB(ph, [64, 1], F32, "negpi")

        kf, bkf = self.TB(ph, [64, TS], F32, "kf")
        ki, bki = self.TB(ph, [64, TS], mybir.dt.int32, "ki")
        wr, bwr = self.TB(ph, [64, TS], F32, "wr")

        c.chain(c.pool, [
            lambda: nc.gpsimd.iota(ang[:], pattern=[[1, TS]], base=0, channel_multiplier=0, allow_small_or_imprecise_dtypes=True),
            lambda: nc.gpsimd.memset(rmf[:], 0.0),
            lambda: nc.gpsimd.affine_select(out=rmf[:], in_=rmf[:], pattern=[[1, 64]], compare_op=ALU.not_equal, fill=1.0, base=-32, channel_multiplier=-1),
            lambda: nc.gpsimd.affine_select(out=rmf[:], in_=rmf[:], pattern=[[-1, 64]], compare_op=ALU.not_equal, fill=-1.0, base=-32, channel_multiplier=1),
            lambda: nc.gpsimd.tensor_copy(out=self.rm[:], in_=rmf[:]),
        ], writes=[ba, brm, self.brm])
        c.op(c.dve, lambda: nc.vector.tensor_scalar(out=ang[:], in0=ang[:], scalar1=self.cmeta[0:64, C_POS:C_POS + 1],
                                                    scalar2=self.vecs[0:64, V_IF:V_IF + 1], op0=ALU.add, op1=ALU.mult),
             reads=[ba], writes=[ba])
        C1 = 6.28125
        C2 = float(2 * np.pi - C1)
        for (dst, bdst, shift) in ((self.sin2, self.bsin, 0.0), (self.cos2, self.bcos, pi / 2)):
            c.chain(c.dve, [
                lambda: nc.vector.tensor_scalar(out=dst[:], in0=ang[:], scalar1=shift, scalar2=None, op0=ALU.add),
                lambda: nc.vector.tensor_scalar(out=kf[:], in0=dst[:], scalar1=1.0 / (2 * pi), scalar2=None, op0=ALU.mult),
                lambda: nc.vector.tensor_copy(out=ki[:], in_=kf[:]),
                lambda: nc.vector.tensor_copy(out=kf[:], in_=ki[:]),
                lambda: nc.vector.scalar_tensor_tensor(out=dst[:], in0=kf[:], scalar=-C1, in1=dst[:], op0=ALU.mult, op1=ALU.add),
                lambda: nc.vector.scalar_tensor_tensor(out=dst[:], in0=kf[:], scalar=-C2, in1=dst[:], op0=ALU.mult, op1=ALU.add),
                lambda: nc.vector.tensor_scalar(out=wr[:], in0=dst[:], scalar1=pi, scalar2=-2 * pi, op0=ALU.is_gt, op1=ALU.mult),
                lambda: nc.vector.tensor_tensor(out=dst[:], in0=dst[:], in1=wr[:], op=ALU.add),
                lambda: nc.vector.tensor_scalar(out=wr[:], in0=dst[:], scalar1=-pi, scalar2=2 * pi, op0=ALU.is_lt, op1=ALU.mult),
                lambda: nc.vector.tensor_tensor(out=dst[:], in0=dst[:], in1=wr[:], op=ALU.add),
                lambda: nc.vector.tensor_scalar(out=dst[:], in0=dst[:], scalar1=pi, scalar2=-pi, op0=ALU.min, op1=ALU.max),
            ], reads=[ba], writes=[bdst, bkf, bki, bwr])
            c.op(c.act, lambda: nc.scalar.activation(out=dst[:], in_=dst[:], func=AF.Sin), reads=[bdst], writes=[bdst])

    def rope_apply(self, ph, src, bsrc, dst, bdst, pr, bpr, t1, bt1, t2, bt2, t0, nt):
        nc, c = self.nc, self.c
        c.op(c.pe, lambda: nc.tensor.matmul(pr[0:64, 0:nt], lhsT=self.rm[:, :], rhs=src[0:64, t0:t0 + nt], start=True, stop=True),
             reads=[bsrc, self.brm], writes=[bpr])
        c.op(c.dve, lambda: nc.vector.tensor_tensor(out=t1[0:64, 0:nt], in0=src[0:64, t0:t0 + nt], in1=self.cos2[:, t0:t0 + nt], op=ALU.mult),
             reads=[bsrc, self.bcos], writes=[bt1])
        c.op(c.dve, lambda: nc.vector.tensor_tensor(out=t2[0:64, 0:nt], in0=pr[0:64, 0:nt], in1=self.sin2[:, t0:t0 + nt], op=ALU.mult),
             reads=[bpr, self.bsin], writes=[bt2])
        c.op(c.dve, lambda: nc.vector.tensor_tensor(out=dst[0:64, t0:t0 + nt], in0=t1[0:64, 0:nt], in1=t2[0:64, 0:nt], op=ALU.add),
             reads=[bt1, bt2], writes=[bdst])

    def ph_mla_prep(self, with_q):
        nc, c, TS = self.nc, self.c, self.TS
        with Phase(self, "mp") as ph:
            self.make_rope(ph)
            if with_q:
                self.S("qnT", "qnT", [QR, TS], BF16)
                self.fm_rmsnorm(ph, O_Q, 8, V_QN, self.qnT[:, :], "qn")
            self.fm_rmsnorm(ph, O_KV, 4, V_KN, self.kvx[0:KVR, :], "kn")
            kr, bkr = self.TB(ph, [64, TS], BF16, "kr")
            ko, bko = self.TB(ph, [64, TS], BF16, "ko")
            t1, bt1 = self.TB(ph, [64, 512], F32, "t1")
            t2, bt2 = self.TB(ph, [64, 512], F32, "t2")
            pr, bpr = self.TB(ph, [128, 512], F32, "pr", psum=True)
            c.dma(c.sp, kr[:], self.projT[O_KR:O_KR + 64, :], writes=[bkr])
            for (t0, nt) in ttiles(TS, 512):
                self.rope_apply(ph, kr, bkr, ko, bko, pr, bpr, t1, bt1, t2, bt2, t0, nt)
            c.dma(c.pool, self.kvx[KVR:KVR + 64, :], ko[:], reads=[bko], writes=[self.B("kvx")])

    def ph_mla_proj(self):
        nc, c, TS, SEG, NK = self.nc, self.c, self.TS, self.SEG, self.NK
        NKC, NKG = self.NKC, self.NKG
        sc = 1.0 / float(np.sqrt(NOPE + ROPE))
        self.S("qT", "qT", [MH * 192, TS], BF16)
        self.S("kT", "kT", [MH * 128, NKC], BF16)
        self.S("vTok", "vTok", [NKC, MH * 128], BF16)
        with Phase(self, "mq") as ph:
            actT = self.load_actT(ph, self.qnT, 8)
            blocks = []
            for h in range(MH):
                blocks.append(dict(W=self.I("w_q_b")[:, h * 192:h * 192 + 128], mw=128, out=self.qT[h * 192:h * 192 + 128, :], scale=sc))
                blocks.append(dict(W=self.I("w_q_b")[:, h * 192 + 128:h * 192 + 192], mw=64, out=self.qT[h * 192 + 128:h * 192 + 192, :], scale=sc))
            self.gemm_A(ph, actT, 8, blocks, "mq")
        with Phase(self, "mk") as ph:
            kvn, bk = ph.sb([128, 4, NKC], BF16, "kvnC"), self.B("actres")
            for k in range(4):
                for (c0, ncol, src) in self.kv_pieces(k):
                    c.dma(c.sp, kvn[:, k, c0:c0 + ncol], src, reads=[self.B("kvg")], writes=[bk])
            c.dma(c.sp, kvn[:, :, NKG:NKC], self.kvx[0:KVR, HALO:TS].rearrange("(kc p) t -> p kc t", p=128), writes=[bk])
            tts = []
            t = 0
            while t < NKC:
                tts.append((t, min(512, NKC - t)))
                t += 512
            blocks = [dict(W=self.I("w_kv_b")[:, h * 256:h * 256 + 128], mw=128, out=self.kT[h * 128:(h + 1) * 128, :]) for h in range(MH)]
            self.gemm_A(ph, kvn, 4, blocks, "mk", tts=tts, width=NKC)
            wv4 = self.I("w_kv_b").rearrange("(kc p) (h t) -> p kc h t", p=128, t=256)
            blocks = [dict(Wv=wv4[:, :, 4 * g:4 * g + 4, 128:256], nw=512, out=self.vTok[:, g * 512:(g + 1) * 512]) for g in range(MH // 4)]
            self.gemm_B(ph, kvn, 4, NKC, blocks, "mv", BF16)

    def ph_mla_attn(self):
        nc, c, TS, SEG, NK, NSEG = self.nc, self.c, self.TS, self.SEG, self.NK, self.NSEG
        NKC, NKG = self.NKC, self.NKG
        QT = min(512, SEG)
        with Phase(self, "at") as ph:
            self.make_rope(ph)
            kpe, bkpe = self.TB(ph, [64, NKC], BF16, "kpe")
            for (c0, ncol, src) in self.kv_pieces(4):
                c.dma(c.sp, kpe[:, c0:c0 + ncol], src, reads=[self.B("kvg")], writes=[bkpe])
            c.dma(c.sp, kpe[:, NKG:NKC], self.kvx[KVR:KVR + 64, HALO:TS], writes=[bkpe])
            nd = QT // 128
            masks = []
            for d_ in range(nd):
                m, bm = self.TB(ph, [128, QT], BF16, f"mask{d_}")

                fns = [lambda: nc.gpsimd.memset(m[:], 0.0)]
                for kh in range(2):
                    c0 = 64 * (2 * d_ + kh)
                    if c0 < QT:
                        fns.append(lambda kh=kh, c0=c0: nc.gpsimd.memset(m[64 * kh:64 * kh + 64, c0:QT], 1.0))
                c.chain(c.pool, fns, writes=[bm])
                masks.append((m, bm))
            qn = [self.TB(ph, [128, TS], BF16, f"qn{i}") for i in range(2)]
            qr = [self.TB(ph, [64, TS], BF16, f"qr{i}") for i in range(2)]
            qp, bqp = self.TB(ph, [64, TS], BF16, "qp")
            kt = [self.TB(ph, [128, NKC], BF16, f"kt{i}") for i in range(2)]
            NKT = (NKC - HALO) // 128
            vv = [self.TB(ph, [128, NKT, 128], BF16, f"vv{i}") for i in range(2)]
            vm = [self.TB(ph, [16, 128], BF16, f"vm{i}") for i in range(2)]
            gt = [self.TB(ph, [128, TS], BF16, f"gt{i}") for i in range(2)]
            gf, bgf = self.TB(ph, [128, 512], F32, "gf")
            t1, bt1 = self.TB(ph, [64, 512], F32, "t1")
            t2, bt2 = self.TB(ph, [64, 512], F32, "t2")
            pt = [self.TB(ph, [128, 512], BF16, f"pt{i}") for i in range(3)]
            rden, brden = self.TB(ph, [128, 512], F32, "rden")
            of, bof = self.TB(ph, [128, 512], F32, "of")
            ob = [self.TB(ph, [128, TS], BF16, f"ob{i}") for i in range(2)]
            pS = [self.TB(ph, [128, 512], F32, f"pS{i}", psum=True) for i in range(3)]
            pO, bpO = self.TB(ph, [128, 512], F32, "pO", psum=True)
            pD, bpD = self.TB(ph, [128, 512], F32, "pD", psum=True)
            pr, bpr = self.TB(ph, [128, 512], F32, "pr", psum=True)
            nS = 0
            for h in range(MH):
                i = h % 2
                (qn_, bqn), (qr_, bqr), (kt_, bkt), (vv_, bvv), (vm_, bvm), (gt_, bgt), (ob_, bob) = qn[i], qr[i], kt[i], vv[i], vm[i], gt[i], ob[i]
                c.dma(c.sp, qn_[:], self.qT[h * 192:h * 192 + 128, :], writes=[bqn])
                c.dma(c.sp, qr_[:], self.qT[h * 192 + 128:h * 192 + 192, :], writes=[bqr])
                c.dma(c.sp, kt_[:], self.kT[h * 128:(h + 1) * 128, :], writes=[bkt])
                c.dma(c.sp, vm_[:], self.vTok[0:HALO, h * 128:(h + 1) * 128], writes=[bvm])
                c.dma(c.sp, vv_[:], self.vTok[HALO:NKC, h * 128:(h + 1) * 128].rearrange("(kt p) v -> p kt v", p=128), writes=[bvv])
                c.dma(c.sp, gt_[:], self.projT[O_MG + h * 128:O_MG + (h + 1) * 128, :], writes=[bgt])
                for (t0, nt) in ttiles(TS, QT):
                    self.rope_apply(ph, qr_, bqr, qp, bqp, pr, bpr, t1, bt1, t2, bt2, t0, nt)
                for iq, (t0, nt) in enumerate(ttiles(TS, QT)):
                    kts = [(0, HALO, None, None, vm_[0:HALO, :])]
                    if iq > 0:
                        for j in range(NSEG - 1):
                            for ii in range(SEG // 128):
                                kti = j * (SEG // 128) + ii
                                kts.append((HALO + kti * 128, 128, self.cmeta[:, C_VIS + j:C_VIS + j + 1], None, vv_[:, kti, :]))
                        a = iq - 1
                        for ii in range(SEG // 128):
                            d_ = ii - a * nd
                            if d_ >= nd:
                                continue
                            kti = (NSEG - 1) * (SEG // 128) + ii
                            kts.append((NKG + ii * 128, 128, None, masks[d_] if d_ >= 0 else None, vv_[:, kti, :]))
                    pendq = []
                    for ik, (k0, nk, bias, mask, vl) in enumerate(kts):
                        ps_, bps = pS[nS % 3]
                        pt_, bpt = pt[nS % 3]
                        nS += 1

                        def fs():
                            nc.tensor.matmul(ps_[0:nk, 0:nt], lhsT=kt_[:, k0:k0 + nk], rhs=qn_[:, t0:t0 + nt], start=True, stop=False)
                            return nc.tensor.matmul(ps_[0:nk, 0:nt], lhsT=kpe[:, k0:k0 + nk], rhs=qp[:, t0:t0 + nt], start=False, stop=True)
                        c.op(c.pe, fs, reads=[bkt, bqn, bkpe, bqp], writes=[bps])
                        kw = {} if bias is None else {"bias": bias[0:nk, :]}
                        c.op(c.act, lambda: nc.scalar.activation(out=pt_[0:nk, 0:nt], in_=ps_[0:nk, 0:nt], func=AF.Exp, **kw),
                             reads=[bps], writes=[bpt])
                        if mask is not None:
                            c.op(c.dve, lambda: nc.vector.tensor_tensor(out=pt_[0:nk, 0:nt], in0=pt_[0:nk, 0:nt], in1=mask[0][0:nk, 0:nt], op=ALU.mult),
                                 reads=[bpt, mask[1]], writes=[bpt])
                        def mk_fo(pt_=pt_, bpt=bpt, nk=nk, vl=vl, first=(ik == 0), last=(ik == len(kts) - 1)):
                            def fo():
                                nc.tensor.matmul(pO[:, 0:nt], lhsT=vl, rhs=pt_[0:nk, 0:nt], start=first, stop=last)
                                return nc.tensor.matmul(pD[:, 0:nt], lhsT=self.onesb[0:nk, :], rhs=pt_[0:nk, 0:nt], start=first, stop=last)
                            return lambda: c.op(c.pe, fo, reads=[bpt, bvv, bvm], writes=[bpO, bpD])
                        pendq.append(mk_fo())
                        if len(pendq) > 2:
                            pendq.pop(0)()
                    for f_ in pendq:
                        f_()

                    c.op(c.dve, lambda: nc.vector.reciprocal(out=rden[:, 0:nt], in_=pD[:, 0:nt]), reads=[bpD], writes=[brden])
                    c.op(c.dve, lambda: nc.vector.tensor_tensor(out=of[:, 0:nt], in0=pO[:, 0:nt], in1=rden[:, 0:nt], op=ALU.mult),
                         reads=[bpO, brden], writes=[bof])
                    c.op(c.act, lambda: nc.scalar.activation(out=gf[:, 0:nt], in_=gt_[:, t0:t0 + nt], func=AF.Silu), reads=[bgt], writes=[bgf])
                    c.op(c.dve, lambda: nc.vector.tensor_tensor(out=ob_[:, t0:t0 + nt], in0=of[:, 0:nt], in1=gf[:, 0:nt], op=ALU.mult),
                         reads=[bof, bgf], writes=[bob])
                c.dma(c.pool, self.brT[BW + h * 128:BW + (h + 1) * 128, :], ob_[:], reads=[bob], writes=[self.B("brT")])


    def ph_pool(self):
        nc, c, TS = self.nc, self.c, self.TS
        with Phase(self, "pl") as ph:
            mixed = ph.sb([128, 16, TS], BF16, "mixed")
            bmx = self.B("actres")
            ic, bic = self.TB(ph, [128, 4, HALO], F32, "ic")

            fns = []
            for g in range(4):
                w = 2 ** (g + 1)
                fns.append(lambda g=g, w=w: nc.gpsimd.memset(ic[:, g, :], 1.0 / w))
                for t in range(w - 1):
                    fns.append(lambda g=g, t=t: nc.gpsimd.memset(ic[:, g, t:t + 1], 1.0 / (t + 1)))
            c.chain(c.pool, fns, writes=[bic])
            ub = [self.TB(ph, [128, TS], BF16, f"ub{i}") for i in range(2)]
            uf, buf_ = self.TB(ph, [128, TS], F32, "uf")
            s0, bs0 = self.TB(ph, [128, TS], F32, "s0")
            s1, bs1 = self.TB(ph, [128, TS], F32, "s1")
            th, bth = self.TB(ph, [128, HALO], F32, "th")
            for q in range(16):
                g = q // 4
                w = 2 ** (g + 1)
                u, bu = ub[q % 2]
                c.dma(c.sp, u[:], self.projT[O_PU + q * 128:O_PU + (q + 1) * 128, :], writes=[bu])
                c.op(c.act, lambda: nc.scalar.copy(out=uf[:], in_=u[:]), reads=[bu], writes=[buf_])
                cur, bcur, nxt, bnxt = uf, buf_, s0, bs0
                step = 1
                while step < w:
                    def fw():
                        nc.vector.tensor_copy(out=nxt[:, 0:step], in_=cur[:, 0:step])
                        return nc.vector.tensor_tensor(out=nxt[:, step:TS], in0=cur[:, step:TS], in1=cur[:, 0:TS - step], op=ALU.add)
                    c.op(c.dve, fw, reads=[bcur], writes=[bnxt])
                    if nxt is s0:
                        cur, bcur, nxt, bnxt = s0, bs0, s1, bs1
                    else:
                        cur, bcur, nxt, bnxt = s1, bs1, s0, bs0
                    step *= 2

                c.op(c.dve, lambda: nc.vector.scalar_tensor_tensor(out=mixed[:, q, HALO:TS], in0=cur[:, HALO:TS], scalar=1.0 / w,
                                                                   in1=uf[:, HALO:TS], op0=ALU.mult, op1=ALU.subtract),
                     reads=[bcur, buf_], writes=[bmx])
                c.op(c.dve, lambda: nc.vector.tensor_tensor(out=th[:], in0=cur[:, 0:HALO], in1=ic[:, g, :], op=ALU.mult),
                     reads=[bcur, bic], writes=[bth])
                c.op(c.dve, lambda: nc.vector.tensor_tensor(out=mixed[:, q, 0:HALO], in0=th[:], in1=uf[:, 0:HALO], op=ALU.subtract),
                     reads=[bth, buf_], writes=[bmx])
            for g in range(4):
                with Phase(self, f"pg{g}") as ph2:
                    blocks = []
                    for m in range(4):
                        r0 = g * 512 + m * 128
                        blocks.append(dict(W=self.I("w_pool")[g, :, m * 128:(m + 1) * 128], mw=128, out=self.brT[2 * BW + r0:2 * BW + r0 + 128, :],
                                           func=AF.Identity, scale=self.vecs[:, V_PS + 4 * g + m:V_PS + 4 * g + m + 1],
                                           pm=(self.projT[O_PG + r0:O_PG + r0 + 128, :], AF.Silu)))
                    self.gemm_A(ph2, mixed[:, 4 * g:4 * g + 4, :], 4, blocks, f"pg{g}")

    def ph_branch(self):
        nc, c, TS = self.nc, self.c, self.TS
        self.S("brW", "brW", [3 * D, TS], BF16)
        for i in range(3):
            with Phase(self, f"bw{i}") as ph:
                actT = self.load_actT(ph, self.brT[i * BW:(i + 1) * BW, :], 16)
                blocks = [dict(W=self.I("w_branch")[i, :, m * 128:(m + 1) * 128], mw=128,
                               out=self.brW[i * D + m * 128:i * D + (m + 1) * 128, :],
                               pm=(self.sigT[i * D + m * 128:i * D + (m + 1) * 128, :], AF.Copy)) for m in range(32)]
                self.gemm_A(ph, actT, 16, blocks, f"bw{i}")
        with Phase(self, "mg") as ph:
            tl = [[self.TB(ph, [128, TS], BF16, f"m{i}_{j}") for j in range(3)] for i in range(2)]
            acc = [self.TB(ph, [128, TS], F32, f"macc{i}") for i in range(2)]
            mo = [self.TB(ph, [128, TS], BF16, f"mo{i}") for i in range(2)]
            for kc in range(32):
                i = kc % 2
                for j in range(3):
                    c.dma(c.sp, tl[i][j][0][:], self.brW[j * D + kc * 128:j * D + (kc + 1) * 128, :], writes=[tl[i][j][1]])
                a, ba = acc[i]
                o, bo = mo[i]
                c.op(c.dve, lambda: nc.vector.tensor_tensor(out=a[:], in0=tl[i][0][0][:], in1=tl[i][1][0][:], op=ALU.add),
                     reads=[tl[i][0][1], tl[i][1][1]], writes=[ba])
                c.op(c.dve, lambda: nc.vector.tensor_tensor(out=o[:], in0=a[:], in1=tl[i][2][0][:], op=ALU.add),
                     reads=[ba, tl[i][2][1]], writes=[bo])
                c.dma(c.pool, self.mergedT[kc * 128:(kc + 1) * 128, :], o[:], reads=[bo], writes=[self.B("mergedT")])

    def ph_out(self):
        nc, c, TS = self.nc, self.c, self.TS
        with Phase(self, "op") as ph:
            actT = self.load_actT(ph, self.mergedT, 32)
            blocks = [dict(W=self.I("w_out")[:, cb * 256:(cb + 1) * 256], nw=256, out=self.outF[:, cb * 256:(cb + 1) * 256]) for cb in range(16)]
            self.gemm_B(ph, actT, 32, TS, blocks, "op", F32)
        with Phase(self, "fn") as ph:
            postw, bpw = self.TB(ph, [128, D], F32, "postw")
            c.dma(c.sp, postw[:], self.I("rows")[0:1, D:2 * D].partition_broadcast(128), writes=[bpw])
            ot = [self.TB(ph, [128, D], F32, f"fo{i}") for i in range(2)]
            ht = [self.TB(ph, [128, D], F32, f"fh{i}") for i in range(2)]
            junk, bj = self.TB(ph, [128, D], BF16, "fjunk")
            ss = [self.TB(ph, [128, 1], F32, f"fss{i}") for i in range(2)]
            for it, (t0, nt) in enumerate(ttiles(TS, 128)):
                i = it % 2
                (o, bo), (hh, bh), (s_, bs) = ot[i], ht[i], ss[i]
                c.dma(c.sp, o[0:nt, :], self.outF[t0:t0 + nt, :], writes=[bo])
                c.dma(c.sp, hh[0:nt, :], self.hsrc()[t0:t0 + nt, :], reads=[self.B("hres")], writes=[bh])
                c.op(c.act, lambda: nc.scalar.activation(out=junk[0:nt, :], in_=o[0:nt, :], func=AF.Square, accum_out=s_[0:nt, :]),
                     reads=[bo], writes=[bj, bs])
                c.op(c.act, lambda: nc.scalar.activation(out=s_[0:nt, :], in_=s_[0:nt, :], func=AF.Sqrt, scale=1.0 / D, bias=self.epsc[0:nt, :]),
                     reads=[bs], writes=[bs])
                c.op(c.dve, lambda: nc.vector.reciprocal(out=s_[0:nt, :], in_=s_[0:nt, :]), reads=[bs], writes=[bs])
                c.op(c.dve, lambda: nc.vector.scalar_tensor_tensor(out=o[0:nt, :], in0=o[0:nt, :], scalar=s_[0:nt, 0:1], in1=postw[0:nt, :],
                                                                   op0=ALU.mult, op1=ALU.mult),
                     reads=[bo, bs, bpw], writes=[bo])
                c.op(c.dve, lambda: nc.vector.tensor_tensor(out=o[0:nt, :], in0=o[0:nt, :], in1=hh[0:nt, :], op=ALU.add),
                     reads=[bo, bh], writes=[bo])
                c.dma(c.pool, self.hdst()[t0:t0 + nt, :], o[0:nt, :], reads=[bo], writes=[self.B("hdst")])


    def exchange_mid(self):
        c = self.c
        c.barrier()
        for k in range(5):
            r0 = k * 128
            nr = 128 if k < 4 else ROPE
            c.dma(c.sp, self.kvc[k][:, :], self.kvx[r0:r0 + nr, :], writes=[self.B(f"kvc{k}")])
        c.barrier()
        for k in range(5):
            c.coll("AllGather", self.groups, self.kvc[k], self.kvc_g[k], writes=[self.B("kvg")])
        c.coll("AllGather", self.groups, self.L_out, self.L_g, writes=[self.B("ssdg")])
        c.coll("AllGather", self.groups, self.D_out, self.D_g, writes=[self.B("ssdg")])
        c.barrier()

    def ph_halo_exchange(self):
        nc, c, TS, NSEG = self.nc, self.c, self.TS, self.NSEG
        with Phase(self, "hx") as ph:
            c.dma(c.sp, self.tail_loc[:, :], self.h1[TS - HALO:TS, :], writes=[self.B("tail")])
            c.barrier()
            c.coll("AllGather", self.groups, self.tail_loc, self.tails_g, writes=[self.B("tailg")])
            c.barrier()
            own, bown = self.TB(ph, [HALO, D], F32, "own")
            tl, btl = self.TB(ph, [HALO, NSEG, D], F32, "tl")
            c.dma(c.sp, own[:], self.h1[0:HALO, :], writes=[bown])
            c.dma(c.sp, tl[:], self.tails_g.rearrange("(j r) d -> r j d", r=HALO), writes=[btl])
            c.op(c.dve, lambda: nc.vector.tensor_scalar(out=own[:], in0=own[:], scalar1=self.cmeta[0:HALO, C_M0:C_M0 + 1], scalar2=None,
                                                        op0=ALU.mult), reads=[bown], writes=[bown])
            for j in range(NSEG - 1):
                c.op(c.dve, lambda: nc.vector.scalar_tensor_tensor(out=own[:], in0=tl[:, j, :], scalar=self.cmeta[0:HALO, C_OH + j:C_OH + j + 1],
                                                                   in1=own[:], op0=ALU.mult, op1=ALU.add),
                     reads=[btl, bown], writes=[bown])
            c.dma(c.pool, self.h1[0:HALO, :], own[:], reads=[bown], writes=[self.B("hdst")])


def build_program(SEG, NSEG, mode, dbg=False, phases=None):
    P = Prog(SEG, NSEG, mode, dbg)
    c = P.c
    with contextlib.ExitStack() as es:
        class _G:
            pass
        gph = Phase(P, "glob")
        gph.__enter__()
        P.load_consts(gph)
        phases = phases or (["norm", "inproj", "conv", "ssd", "mla", "pool", "out"] if mode == "B" else ["norm", "inproj", "conv", "ssd", "mla"])
        if "norm" in phases:
            P.ph_norm()
        if "inproj" in phases:
            if mode == "A":
                P.ph_inproj([(O_XBC, O_Q), (O_KV, O_MG)], False)
            else:
                P.ph_inproj([(0, IN_DIM)], True)
        if "conv" in phases:
            P.ph_conv()
        if "ssd" in phases:
            P.ph_ssd(mode == "B")
        if "mla" in phases:
            P.ph_mla_prep(mode == "B")
            if mode == "B":
                P.ph_mla_proj()
                P.ph_mla_attn()
        if "pool" in phases:
            P.ph_pool()
        if "out" in phases:
            P.ph_branch()
            P.ph_out()
        gph.__exit__(None, None, None)
    c.barrier()
    c.close()
    return P


def host_layer_inputs(inp, L):
    f = lambda a: np.ascontiguousarray(a, dtype=np.float32)
    vecs = np.zeros((128, NV), np.float32)
    cw = inp["conv_w"][L][:, 0, :]
    vecs[:, V_CW:V_CW + 96] = cw.T.reshape(24, 128, 4).transpose(1, 0, 2).reshape(128, 96)
    vecs[:, V_CB:V_CB + 24] = inp["conv_b"][L].reshape(24, 128).T
    vecs[:, V_DS:V_DS + 16] = np.repeat(inp["d_skip"][L], HD).reshape(16, 128).T
    vecs[:, V_SN:V_SN + 16] = inp["ssd_norm_w"][L].reshape(16, 128).T
    vecs[:, V_QN:V_QN + 8] = inp["q_norm_w"][L].reshape(8, 128).T
    vecs[:, V_KN:V_KN + 4] = inp["kv_norm_w"][L].reshape(4, 128).T
    vecs[:, V_PS:V_PS + 16] = inp["pool_scale"][L].reshape(16, 128).T
    vecs[0:32, V_DTB] = inp["dt_bias"][L]
    inv_freq = np.power(np.float32(10000.0), -np.arange(0, ROPE, 2, dtype=np.float32) / np.float32(ROPE)).astype(np.float32)
    vecs[0:32, V_IF] = inv_freq
    vecs[32:64, V_IF] = inv_freq
    rows = np.concatenate([inp["pre_norm_w"][L], inp["post_norm_w"][L], inp["a_log"][L]])[None, :]
    return {
        "vecs": vecs, "rows": f(rows), "w_in": f(inp["w_in"][L]), "w_gate": f(inp["w_gate"][L]),
        "w_branch": f(inp["w_branch"][L]), "w_out": f(inp["w_out"][L]), "w_q_b": f(inp["w_q_b"][L]),
        "w_kv_b": f(inp["w_kv_b"][L]), "w_pool": f(inp["w_pool"][L]),
    }


def host_cmeta(seg, NSEG, SEG):
    cm = np.zeros((128, 16), np.float32)
    cm[:, C_M0] = 1.0 if seg == 0 else 0.0
    for j in range(4):
        cm[:, C_VIS + j] = 0.0 if j < seg else NEG
        cm[:, C_OH + j] = 1.0 if (j + 1) == seg else 0.0
    cm[:, C_POS] = float(seg * SEG)
    return cm


SEG_FULL, NSEG_FULL, NBATCH = 2048, 4, 2
_PROGS = {}


def _prog(mode):
    if mode not in _PROGS:
        _PROGS[mode] = build_program(SEG_FULL, NSEG_FULL, mode)
    return _PROGS[mode]


def kernel_unfused(**inputs):
    x = np.asarray(inputs["x"], dtype=np.float32)
    meta = np.asarray(inputs["meta_tokens"], dtype=np.float32)
    params = {k: np.asarray(v) for k, v in inputs.items() if k not in ("x", "meta_tokens")}
    SEG, NSEG = SEG_FULL, NSEG_FULL
    TS = HALO + SEG
    ncores = NBATCH * NSEG
    hfull = np.concatenate([np.broadcast_to(meta[None], (NBATCH, HALO, D)), x], axis=1).astype(np.float32)
    cmetas = [host_cmeta(cid % NSEG, NSEG, SEG) for cid in range(ncores)]
    depth = params["w_in"].shape[0]
    for L in range(depth):
        lay = host_layer_inputs(params, L)
        hs = [np.ascontiguousarray(hfull[cid // NSEG, (cid % NSEG) * SEG:(cid % NSEG) * SEG + TS]) for cid in range(ncores)]
        PA = _prog("A")
        in_maps = []
        for cid in range(ncores):
            m = {}
            for name in PA.inputs:
                m[name] = hs[cid] if name == "h" else cmetas[cid] if name == "cmeta" else lay[name]
            in_maps.append(m)
        ra = run_bass_kernel_spmd(PA.nc, in_maps, core_ids=list(range(ncores))).results
        kv_all, L_all, D_all = [], [], []
        for b in range(NBATCH):
            rs = [ra[b * NSEG + s] for s in range(NSEG)]
            kv_all.append(np.ascontiguousarray(np.concatenate([np.asarray(rs[0]["kvx"])[:, :HALO]] +
                                                              [np.asarray(r_["kvx"])[:, HALO:] for r_ in rs], axis=1)))
            L_all.append(np.ascontiguousarray(np.stack([np.asarray(r_["L_out"]) for r_ in rs], axis=0)))
            D_all.append(np.ascontiguousarray(np.concatenate([np.asarray(r_["D_out"])[0] for r_ in rs])[None, :]))
        PB = _prog("B")
        in_maps = []
        for cid in range(ncores):
            b = cid // NSEG
            m = {}
            for name in PB.inputs:
                if name == "h":
                    m[name] = hs[cid]
                elif name == "cmeta":
                    m[name] = cmetas[cid]
                elif name == "kv_all":
                    m[name] = kv_all[b]
                elif name == "L_all":
                    m[name] = L_all[b]
                elif name == "D_all":
                    m[name] = D_all[b]
                else:
                    m[name] = lay[name]
            in_maps.append(m)
        rb = run_bass_kernel_spmd(PB.nc, in_maps, core_ids=list(range(ncores))).results
        for cid in range(ncores):
            b, s = cid // NSEG, cid % NSEG
            ho = np.asarray(rb[cid]["h_out"])
            hfull[b, HALO + s * SEG:HALO + (s + 1) * SEG] = ho[HALO:]
            if s == 0:
                hfull[b, 0:HALO] = ho[0:HALO]
    return np.ascontiguousarray(hfull[:, HALO:]).astype(np.float32)


def build_fused(SEG, NSEG, depth):
    P = Prog(SEG, NSEG, "F")
    gph = Phase(P, "glob")
    gph.__enter__()
    for L in range(depth):
        P.L = L
        P.h_src = None if L == 0 else P.h1
        P.h_dst = P.h1 if L < depth - 1 else P.h_out
        P.load_consts(gph)
        P.ph_norm()
        P.ph_inproj([(0, IN_DIM)], True)
        P.ph_conv()
        P.ph_ssd(False)
        P.ph_mla_prep(True)
        P.exchange_mid()
        P.ph_ssd(True)
        P.ph_mla_proj()
        P.ph_mla_attn()
        P.ph_pool()
        P.ph_branch()
        P.ph_out()
        if L < depth - 1:
            P.ph_halo_exchange()
    gph.__exit__(None, None, None)
    P.c.barrier()
    P.c.close()
    return P


def kernel(**inputs):
    x = np.asarray(inputs["x"], dtype=np.float32)
    meta = np.asarray(inputs["meta_tokens"], dtype=np.float32)
    params = {k: np.asarray(v) for k, v in inputs.items() if k not in ("x", "meta_tokens")}
    SEG, NSEG = SEG_FULL, NSEG_FULL
    TS = HALO + SEG
    ncores = NBATCH * NSEG
    depth = params["w_in"].shape[0]
    if "F" not in _PROGS:
        _PROGS["F"] = build_fused(SEG, NSEG, depth)
    P = _PROGS["F"]
    hfull = np.concatenate([np.broadcast_to(meta[None], (NBATCH, HALO, D)), x], axis=1).astype(np.float32)
    lays = [host_layer_inputs(params, L) for L in range(depth)]
    in_maps = []
    for cid in range(ncores):
        b, s = cid // NSEG, cid % NSEG
        m = {}
        for key in P.inputs:
            if key == "h":
                m[key] = np.ascontiguousarray(hfull[b, s * SEG:s * SEG + TS])
            elif key == "cmeta":
                m[key] = host_cmeta(s, NSEG, SEG)
            else:
                name, L = key.rsplit("_L", 1)
                m[key] = lays[int(L)][name]
        in_maps.append(m)
    res = run_bass_kernel_spmd(P.nc, in_maps, core_ids=list(range(ncores))).results
    out = np.empty((NBATCH, NSEG * SEG, D), np.float32)
    for cid in range(ncores):
        b, s = cid // NSEG, cid % NSEG
        out[b, s * SEG:(s + 1) * SEG] = np.asarray(res[cid]["h_out"])[HALO:]
    return out
```

```python
import contextlib
import numpy as np
import ml_dtypes
import concourse.bass as bass
import concourse.mybir as mybir
from concourse.bass_utils import run_bass_kernel_spmd

F32 = mybir.dt.float32
BF16 = mybir.dt.bfloat16
AF = mybir.ActivationFunctionType
ALU = mybir.AluOpType

D = 4096
BW = 2048
EPS = 1e-6
NH, HD, NG, NST, XBC = 32, 64, 4, 128, 3072
MH, NOPE, ROPE, VD, QR, KVR = 16, 128, 64, 128, 1024, 512
IN_DIM = 12896
O_Z, O_XBC, O_DT, O_Q, O_KV, O_KR, O_MG, O_PU, O_PG = 0, 2048, 5120, 5152, 6176, 6688, 6752, 8800, 10848
HALO = 16
NEG = -30000.0

SSD_STOP = 9.0
V_CW, V_CB, V_DS, V_SN, V_QN, V_KN, V_PS, V_DTB, V_IF, NV = 0, 96, 120, 136, 152, 160, 164, 180, 181, 182
C_M0, C_VIS, C_OH, C_POS = 0, 1, 5, 9


class Buf:
    __slots__ = ("name", "lw", "rd", "excl")

    def __init__(self, name="", excl=False):
        self.name = name
        self.lw = None
        self.rd = []
        self.excl = excl


class Eng:
    def __init__(self, name, h, sem):
        self.name, self.h, self.sem = name, h, sem
        self.cnt = 0
        self.known = {}

    def wait_ev(self, ev):
        sem, val, _ = ev
        if self.known.get(id(sem), 0) < val:
            self.h.wait_ge(sem, val)
            self.known[id(sem)] = val


class Ctx:
    def __init__(self, nc, n_dma_sems=8):
        self.nc = nc
        self.es = contextlib.ExitStack()
        mk = lambda n: self.es.enter_context(nc.semaphore(n))
        self.pe = Eng("pe", nc.tensor, mk("s_pe"))
        self.act = Eng("act", nc.scalar, mk("s_act"))
        self.dve = Eng("dve", nc.vector, mk("s_dve"))
        self.pool = Eng("pool", nc.gpsimd, mk("s_pool"))
        self.sp = Eng("sp", nc.sync, mk("s_sp"))
        self.engs = [self.pe, self.act, self.dve, self.pool, self.sp]
        self.dsems = {}
        for e in (self.sp, self.pool, self.act):
            self.dsems[e.name] = [[mk(f"d_{e.name}{i}"), 0] for i in range(n_dma_sems)]
        self.drr = {e.name: 0 for e in (self.sp, self.pool, self.act)}
        self.n_ops = 0
        self.cc_sem = mk("s_cc")
        self.cc_cnt = 0

    def close(self):
        self.es.close()

    def _deps(self, eng, reads, writes, same_raw):
        for r in reads:
            if r.lw is not None and (r.lw[2] != eng.name or same_raw):
                eng.wait_ev(r.lw)
        for w in writes:
            if w.lw is not None and (w.lw[2] != eng.name or same_raw):
                eng.wait_ev(w.lw)
            for ev in w.rd:
                if ev[2] != eng.name:
                    eng.wait_ev(ev)

    def _commit(self, ev, reads, writes):
        for r in reads:
            r.rd.append(ev)
            if len(r.rd) > 48:
                last = {}
                for e in r.rd:
                    if id(e[0]) not in last or last[id(e[0])][1] < e[1]:
                        last[id(e[0])] = e
                r.rd = list(last.values())
        for w in writes:
            w.lw = ev
            w.rd = []

    def op(self, eng, fn, reads=(), writes=()):
        ex = [r for r in reads if r.excl]
        if ex:
            writes = list(writes) + ex
            reads = [r for r in reads if not r.excl]
        self._deps(eng, reads, writes, same_raw=(eng is not self.pe))
        ins = fn()
        ins.then_inc(eng.sem, 1)
        eng.cnt += 1
        ev = (eng.sem, eng.cnt, eng.name)
        self._commit(ev, reads, writes)
        self.n_ops += 1
        return ev

    def chain(self, eng, fns, reads=(), writes=()):
        ev = None
        for fn in fns:
            ev = self.op(eng, fn, reads=reads, writes=writes)
        return ev

    def dma(self, eng, out, in_, reads=(), writes=(), **kw):
        pool = self.dsems[eng.name]
        i = self.drr[eng.name]
        self.drr[eng.name] = (i + 1) % len(pool)
        slot = pool[i]
        if slot[1] > 0:
            eng.wait_ev((slot[0], slot[1], "dma"))
        self._deps(eng, reads, writes, same_raw=True)
        eng.h.dma_start(out=out, in_=in_, **kw).then_inc(slot[0], 16)
        slot[1] += 16
        ev = (slot[0], slot[1], "dma_" + eng.name + str(i))
        self._commit(ev, reads, writes)
        self.n_ops += 1
        return ev

    def coll(self, kind, groups, src, dst, reads=(), writes=()):
        eng = self.pool
        if self.cc_cnt > 0:
            eng.wait_ev((self.cc_sem, self.cc_cnt, "coll"))
        self._deps(eng, reads, writes, same_raw=True)
        self.nc.gpsimd.collective_compute(kind, ALU.bypass, replica_groups=groups, ins=[src.opt()], outs=[dst.opt()]).then_inc(self.cc_sem)
        self.cc_cnt += 1
        ev = (self.cc_sem, self.cc_cnt, "coll")
        self._commit(ev, reads, writes)
        return ev

    def barrier(self):
        evs = []
        for e in self.engs:
            if e.cnt > 0:
                evs.append((e.sem, e.cnt, e.name))
        for name, pool in self.dsems.items():
            for s in pool:
                if s[1] > 0:
                    evs.append((s[0], s[1], "dma"))
        if self.cc_cnt > 0:
            evs.append((self.cc_sem, self.cc_cnt, "coll"))
        for e in self.engs:
            for ev in evs:
                if ev[0] is not e.sem:
                    e.wait_ev(ev)


class Phase:
    _seq = [0]

    def __init__(self, P, name):
        Phase._seq[0] += 1
        self.P, self.name = P, f"{name}x{Phase._seq[0]}"
        self.es = contextlib.ExitStack()
        self.n = 0

    def __enter__(self):
        self.P.c.barrier()
        return self

    def __exit__(self, *a):
        self.P.c.barrier()
        self.es.close()
        return False

    def sb(self, shape, dt, name=None):
        self.n += 1
        return self.es.enter_context(self.P.nc.sbuf_tensor(f"{self.name}_{name or 's'}{self.n}", list(shape), dt))

    def ps(self, shape, dt, name=None):
        self.n += 1
        return self.es.enter_context(self.P.nc.psum_tensor(f"{self.name}_{name or 'p'}{self.n}", list(shape), dt))


def ttiles(TS, n):
    out = [(0, HALO)]
    t = HALO
    while t < TS:
        m = min(n, TS - t)
        out.append((t, m))
        t += m
    return out


class Prog:
    def __init__(self, SEG, NSEG, mode="B", dbg=False):
        self.SEG, self.NSEG, self.mode, self.dbg = SEG, NSEG, mode, dbg
        self.TS = TS = HALO + SEG
        self.NK = HALO + NSEG * SEG
        self.NKG = HALO + (NSEG - 1) * SEG
        self.NKC = self.NKG + SEG
        nc = self.nc = bass.Bass("TRN2", target_bir_lowering=False)
        self.c = Ctx(nc)
        skind = "ExternalOutput" if dbg else "Internal"
        dt_ = lambda n, s, t, k: nc.dram_tensor(n, list(s), t, kind=k).ap()
        self._dt = dt_
        NK = self.NK
        self.ispec = {
            "h": ([TS, D], F32), "cmeta": ([128, 16], F32), "vecs": ([128, NV], F32), "rows": ([1, 2 * D + 32], F32),
            "w_in": ([D, IN_DIM], F32), "w_gate": ([3, D, D], F32), "w_branch": ([3, BW, D], F32),
            "w_out": ([D, D], F32), "w_q_b": ([QR, MH * (NOPE + ROPE)], F32), "w_kv_b": ([KVR, MH * (NOPE + VD)], F32),
            "w_pool": ([4, 512, 512], F32), "kv_all": ([KVR + ROPE, NK], BF16), "L_all": ([NSEG, 128, BW], F32),
            "D_all": ([1, NSEG * NH], F32),
        }
        self.inputs = {}
        self.L = 0
        self.per_layer = {"vecs", "rows", "w_in", "w_gate", "w_branch", "w_out", "w_q_b", "w_kv_b", "w_pool"}
        if mode == "F":
            self.h_out = dt_("h_out", [TS, D], F32, "ExternalOutput")
            self.h1 = dt_("h1", [TS, D], F32, "Internal")
            self.L_out = dt_("L_loc", [128, BW], F32, "Internal")
            self.D_out = dt_("D_loc", [128, NH], F32, "Internal")
            self.kvc = [dt_(f"kvc{k}", [128 if k < 4 else ROPE, TS], BF16, "Internal") for k in range(5)]
            self.kvc_g = [dt_(f"kvcg{k}", [NSEG * (128 if k < 4 else ROPE), TS], BF16, "Internal") for k in range(5)]
            self.L_g = dt_("L_g", [NSEG * 128, BW], F32, "Internal")
            self.D_g = dt_("D_g", [NSEG * 128, NH], F32, "Internal")
            self.tail_loc = dt_("tail_loc", [HALO, D], F32, "Internal")
            self.tails_g = dt_("tails_g", [NSEG * HALO, D], F32, "Internal")
            self.groups = [list(range(b * NSEG, (b + 1) * NSEG)) for b in range(2)]
        elif mode == "B":
            self.h_out = dt_("h_out", [TS, D], F32, "ExternalOutput")
        else:
            self.kvx_out = dt_("kvx", [KVR + ROPE, TS], BF16, "ExternalOutput")
            self.L_out = dt_("L_out", [128, BW], F32, "ExternalOutput")
            self.D_out = dt_("D_out", [128, NH], F32, "ExternalOutput")
        self.xnT = dt_("xnT", [D, TS], BF16, skind)
        self.projT = dt_("projT", [IN_DIM, TS], BF16, skind)
        self.dtT = dt_("dtT", [NH, TS], F32, skind)
        self.xbcT = dt_("xbcT", [XBC, TS], BF16, skind)
        self.kvx = dt_("kvx_s", [KVR + ROPE, TS], BF16, skind) if mode in ("B", "F") else self.kvx_out
        self.h_src = None
        self.h_dst = None
        if mode in ("B", "F"):
            self.sigT = dt_("sigT", [3 * D, TS], BF16, skind)
            self.brT = dt_("brT", [3 * BW, TS], BF16, skind)
            self.mergedT = dt_("mergedT", [D, TS], BF16, skind)
            self.outF = dt_("outF", [TS, D], F32, skind)
        self.bufs = {}

    def I(self, name):
        key = f"{name}_L{self.L}" if (self.mode == "F" and name in self.per_layer) else name
        if key not in self.inputs:
            shp, t = self.ispec[name]
            self.inputs[key] = self._dt(key, shp, t, "ExternalInput")
        return self.inputs[key]

    def S(self, attr, name, shape, dt):
        if not hasattr(self, attr) or getattr(self, attr) is None:
            setattr(self, attr, self._dt(name, shape, dt, "Internal"))
        return getattr(self, attr)

    def hsrc(self):
        return self.h_src if self.h_src is not None else self.I("h")

    def hdst(self):
        return self.h_dst if self.h_dst is not None else self.h_out

    def kv_pieces(self, k):
        SEG, NSEG, TS = self.SEG, self.NSEG, self.TS
        r0 = k * 128
        nr = 128 if k < 4 else ROPE
        if self.mode != "F":
            return [(0, self.NKG, self.I("kv_all")[r0:r0 + nr, 0:self.NKG])]
        g = self.kvc_g[k]
        out = [(0, HALO, g[0:nr, 0:HALO])]
        for j in range(NSEG - 1):
            out.append((HALO + j * SEG, SEG, g[j * nr:(j + 1) * nr, HALO:TS]))
        return out

    def B(self, name):
        if name not in self.bufs:
            self.bufs[name] = Buf(name)
        return self.bufs[name]

    def load_consts(self, ph):
        nc, c = self.nc, self.c
        bc = self.B("consts")
        if getattr(self, "_consts_done", False):
            c.barrier()
            c.dma(c.sp, self.vecs[:], self.I("vecs")[:, :], writes=[bc])
            c.barrier()
            return
        self._consts_done = True
        self.cmeta = ph.sb([128, 16], F32, "cmeta")
        self.vecs = ph.sb([128, NV], F32, "vecs")
        c.dma(c.sp, self.cmeta[:], self.I("cmeta")[:, :], writes=[bc])
        c.dma(c.sp, self.vecs[:], self.I("vecs")[:, :], writes=[bc])
        self.identf = ph.sb([128, 128], F32, "identf")
        self.ident = ph.sb([128, 128], BF16, "ident")
        self.onesf = ph.sb([128, 128], F32, "onesf")
        self.onesb = ph.sb([128, 128], BF16, "onesb")
        self.epsc = ph.sb([128, 1], F32, "epsc")

        c.chain(c.pool, [
            lambda: nc.gpsimd.memset(self.identf[:], 0.0),
            lambda: nc.gpsimd.memset(self.onesf[:], 1.0),
            lambda: nc.gpsimd.memset(self.onesb[:], 1.0),
            lambda: nc.gpsimd.memset(self.epsc[:], EPS),
            lambda: nc.gpsimd.affine_select(out=self.identf[:], in_=self.identf[:], pattern=[[-1, 128]],
                                            compare_op=ALU.not_equal, fill=1.0, base=0, channel_multiplier=1),
            lambda: nc.gpsimd.tensor_copy(out=self.ident[:], in_=self.identf[:]),
        ], writes=[bc])
        c.barrier()

    def ph_norm(self):
        nc, c, TS = self.nc, self.c, self.TS
        with Phase(self, "n1") as ph:
            prew = ph.sb([128, D], F32, "prew")
            bpw = self.B("prew")
            c.dma(c.sp, prew[:], self.I("rows")[0:1, 0:D].partition_broadcast(128), writes=[bpw])
            ht = [ph.sb([128, D], F32, f"ht{i}") for i in range(2)]
            junk = ph.sb([128, D], BF16, "junk")
            xnb = [ph.sb([128, D], BF16, f"xnb{i}") for i in range(2)]
            ss = [ph.sb([128, 1], F32, f"ss{i}") for i in range(2)]
            xT = [ph.sb([128, 32, 128], BF16, f"xT{i}") for i in range(2)]
            pT = [ph.ps([128, 8, 128], BF16, f"pT{i}") for i in range(4)]
            bht = [self.B(f"n1ht{i}") for i in range(2)]
            bxn = [self.B(f"n1xn{i}") for i in range(2)]
            bss = [self.B(f"n1ss{i}") for i in range(2)]
            bxT = [self.B(f"n1xT{i}") for i in range(2)]
            bpT = [self.B(f"n1pT{i}") for i in range(4)]
            bj = self.B("n1junk")
            xnT_v = self.xnT.rearrange("(kc p) t -> p kc t", p=128)
            npt = 0
            for it, (t0, nt) in enumerate(ttiles(TS, 128)):
                i = it % 2
                c.dma(c.sp, ht[i][0:nt, :], self.hsrc()[t0:t0 + nt, :], reads=[self.B("hres")], writes=[bht[i]])
                c.op(c.act, lambda: nc.scalar.activation(out=junk[0:nt, :], in_=ht[i][0:nt, :], func=AF.Square,
                                                         accum_out=ss[i][0:nt, :]),
                     reads=[bht[i]], writes=[bj, bss[i]])

                c.op(c.act, lambda: nc.scalar.activation(out=ss[i][0:nt, :], in_=ss[i][0:nt, :], func=AF.Sqrt,
                                                         scale=1.0 / D, bias=self.epsc[0:nt, :]),
                     reads=[bss[i]], writes=[bss[i]])
                c.op(c.dve, lambda: nc.vector.reciprocal(out=ss[i][0:nt, :], in_=ss[i][0:nt, :]),
                     reads=[bss[i]], writes=[bss[i]])
                c.op(c.dve, lambda: nc.vector.scalar_tensor_tensor(out=xnb[i][0:nt, :], in0=ht[i][0:nt, :],
                                                                   scalar=ss[i][0:nt, 0:1], in1=prew[0:nt, :],
                                                                   op0=ALU.mult, op1=ALU.mult),
                     reads=[bht[i], bss[i], bpw], writes=[bxn[i]])
                for g in range(4):
                    j = npt % 4
                    npt += 1

                    def ft():
                        for k in range(8):
                            ins = nc.tensor.transpose(pT[j][:, k, 0:nt], xnb[i][0:nt, (g * 8 + k) * 128:(g * 8 + k + 1) * 128],
                                                      self.ident[0:nt, 0:nt])
                        return ins
                    c.op(c.pe, ft, reads=[bxn[i]], writes=[bpT[j]])
                    if g % 2 == 0:
                        c.op(c.act, lambda: nc.scalar.copy(out=xT[i][:, g * 8:(g + 1) * 8, 0:nt], in_=pT[j][:, :, 0:nt]),
                             reads=[bpT[j]], writes=[bxT[i]])
                    else:
                        c.op(c.dve, lambda: nc.vector.tensor_copy(out=xT[i][:, g * 8:(g + 1) * 8, 0:nt], in_=pT[j][:, :, 0:nt]),
                             reads=[bpT[j]], writes=[bxT[i]])
                c.dma(c.pool, xnT_v[:, :, t0:t0 + nt], xT[i][:, :, 0:nt], reads=[bxT[i]], writes=[self.B("xnT")])

    def gemm_A(self, ph, actT, KC, blocks, tag, n_tile=512, tts=None, width=None):
        nc, c, TS = self.nc, self.c, (width or self.TS)
        wst1 = ph.sb([128, KC, 256], F32, "wst")
        wbf2 = [ph.sb([128, KC, 256], BF16, f"wbf{i}") for i in range(2)]
        ost_b = [ph.sb([128, TS], BF16, f"ostb{i}") for i in range(2)]
        ostf_s = ph.sb([128, 512], F32, "ostfs") if any(b.get("odt", BF16) == F32 for b in blocks) else None
        bof = self.B(f"{tag}ostfs")
        pss = [ph.ps([128, 512], F32, f"ps{i}") for i in range(4)]
        bw1 = self.B(f"{tag}wst")
        bwb2 = [self.B(f"{tag}wbf{i}") for i in range(2)]
        bo = [self.B(f"{tag}ost{i}") for i in range(2)]
        bp = [self.B(f"{tag}ps{i}") for i in range(4)]
        bact = self.B("actres")
        if any(b.get("pm") is not None for b in blocks):
            pmb = [ph.sb([128, TS], BF16, f"pmb{i}") for i in range(2)]
            pmf = [ph.sb([128, TS], F32, f"pmf{i}") for i in range(2)]
            evf = [ph.sb([128, 512], F32, f"evf{i}") for i in range(2)]
            bpmb = [self.B(f"{tag}pmb{i}") for i in range(2)]
            bpm = [self.B(f"{tag}pm{i}") for i in range(2)]
            bev = [self.B(f"{tag}ev{i}") for i in range(2)]
        tts = tts or ttiles(TS, n_tile)
        npp = 0
        groups = []
        ib = 0
        while ib < len(blocks):
            b0 = blocks[ib]
            if (ib + 1 < len(blocks) and b0["mw"] == 128 and blocks[ib + 1]["mw"] == 128 and b0.get("wkey") is not None
                    and blocks[ib + 1].get("wkey") == b0["wkey"] and blocks[ib + 1]["c0"] == b0["c0"] + 128):
                groups.append([ib, ib + 1])
                ib += 2
            else:
                groups.append([ib])
                ib += 1
        gslot = {}
        for ig, grp in enumerate(groups):
            for off, ib_ in enumerate(grp):
                gslot[ib_] = (ig, off, len(grp))
        for ib, blk in enumerate(blocks):
            i = ib % 2
            mw = blk["mw"]
            ig, goff, glen = gslot[ib]
            wb_t, bwb_g = wbf2[ig % 2], bwb2[ig % 2]
            if goff == 0:
                gw = mw if glen == 1 else 256
                Wsrc = blk["W"] if glen == 1 else blk["wfull"][:, blk["c0"]:blk["c0"] + 256]
                Wv = Wsrc.rearrange("(kc p) m -> p kc m", p=128)
                c.dma(c.sp, wst1[:, :, 0:gw], Wv, writes=[bw1])
                half = KC // 2 if KC >= 2 else KC
                c.op(c.dve, lambda: nc.vector.tensor_copy(out=wb_t[:, 0:half, 0:gw], in_=wst1[:, 0:half, 0:gw]),
                     reads=[bw1], writes=[bwb_g])
                if half < KC:
                    c.op(c.pool, lambda: nc.gpsimd.tensor_copy(out=wb_t[:, half:KC, 0:gw], in_=wst1[:, half:KC, 0:gw]),
                         reads=[bw1], writes=[bwb_g])
            wcol = 128 * goff
            odt = blk.get("odt", BF16)
            ost = ost_b[i]
            if blk.get("pm") is not None:
                src, pfunc = blk["pm"]
                c.dma(c.sp, pmb[i][0:mw, :], src, writes=[bpmb[i]])
                c.op(c.act, lambda: nc.scalar.activation(out=pmf[i][0:mw, :], in_=pmb[i][0:mw, :], func=pfunc),
                     reads=[bpmb[i]], writes=[bpm[i]])
            for (t0, nt) in tts:
                j = npp % 4
                npp += 1

                def fm():
                    for k in range(KC):
                        ins = nc.tensor.matmul(pss[j][0:mw, 0:nt], lhsT=wb_t[:, k, wcol:wcol + mw], rhs=actT[:, k, t0:t0 + nt],
                                               start=(k == 0), stop=(k == KC - 1))
                    return ins
                c.op(c.pe, fm, reads=[bwb_g, bact], writes=[bp[j]])
                kw = {}
                if blk.get("bias") is not None:
                    kw["bias"] = blk["bias"]
                if odt == F32:
                    c.op(c.act, lambda: nc.scalar.activation(out=ostf_s[0:mw, 0:nt], in_=pss[j][0:mw, 0:nt],
                                                             func=blk.get("func", AF.Copy), scale=blk.get("scale", 1.0), **kw),
                         reads=[bp[j]], writes=[bof])
                    c.dma(c.pool, blk["out"][:, t0:t0 + nt], ostf_s[0:mw, 0:nt], reads=[bof], writes=[self.B(blk.get("obuf", "gemm_out"))])
                elif blk.get("pm") is None:
                    c.op(c.act, lambda: nc.scalar.activation(out=ost[0:mw, t0:t0 + nt], in_=pss[j][0:mw, 0:nt],
                                                             func=blk.get("func", AF.Copy), scale=blk.get("scale", 1.0), **kw),
                         reads=[bp[j]], writes=[bo[i]])
                else:
                    jj = npp % 2
                    c.op(c.act, lambda: nc.scalar.activation(out=evf[jj][0:mw, 0:nt], in_=pss[j][0:mw, 0:nt],
                                                             func=blk.get("func", AF.Copy), scale=blk.get("scale", 1.0), **kw),
                         reads=[bp[j]], writes=[bev[jj]])
                    c.op(c.dve, lambda: nc.vector.tensor_tensor(out=ost[0:mw, t0:t0 + nt], in0=evf[jj][0:mw, 0:nt],
                                                                in1=pmf[i][0:mw, t0:t0 + nt], op=ALU.mult),
                         reads=[bev[jj], bpm[i]], writes=[bo[i]])
            if odt != F32:
                c.dma(c.pool, blk["out"], ost[0:mw, :], reads=[bo[i]], writes=[self.B(blk.get("obuf", "gemm_out"))])

    def load_actT(self, ph, src, KC, name="actT"):
        c = self.c
        t = ph.sb([128, KC, self.TS], BF16, name)
        v = src.rearrange("(kc p) t -> p kc t", p=128)
        step = max(1, KC // 4)
        for k0 in range(0, KC, step):
            c.dma(c.sp, t[:, k0:k0 + step, :], v[:, k0:k0 + step, :], writes=[self.B("actres")])
        return t

    def ph_inproj(self, col_ranges, with_gates):
        nc, c = self.nc, self.c
        with Phase(self, "g2") as ph:
            actT = self.load_actT(ph, self.xnT, 32)
            blocks = []
            for (c0, c1) in col_ranges:
                cc = c0
                while cc < c1:
                    lim = c1
                    for b in (O_DT, O_Q, O_KR, O_MG):
                        if cc < b < lim:
                            lim = b
                    mw = min(128, lim - cc)
                    if cc == O_DT:
                        blocks.append(dict(W=self.I("w_in")[:, cc:cc + mw], mw=mw, out=self.dtT[:, :], odt=F32))
                    else:
                        blocks.append(dict(W=self.I("w_in")[:, cc:cc + mw], mw=mw, out=self.projT[cc:cc + mw, :],
                                           wkey="w_in", c0=cc, wfull=self.I("w_in")))
                    cc += mw
            if with_gates:
                for i in range(3):
                    for m in range(32):
                        blocks.append(dict(W=self.I("w_gate")[i, :, m * 128:(m + 1) * 128], mw=128,
                                           out=self.sigT[i * D + m * 128:i * D + (m + 1) * 128, :], func=AF.Sigmoid,
                                           wkey=("w_gate", i), c0=m * 128, wfull=self.I("w_gate")[i]))
            self.gemm_A(ph, actT, 32, blocks, "g2")


    def TB(self, ph, shape, dt, name, psum=False):
        t = ph.ps(shape, dt, name) if psum else ph.sb(shape, dt, name)
        b = self.B(f"{ph.name}_{name}_{ph.n}")
        b.excl = psum
        return t, b

    def ph_conv(self):
        nc, c, TS = self.nc, self.c, self.TS
        with Phase(self, "cv") as ph:
            xin = [self.TB(ph, [128, TS + 3], BF16, f"xin{i}") for i in range(2)]
            acc = [self.TB(ph, [128, TS], F32, f"acc{i}") for i in range(2)]
            ot = [self.TB(ph, [128, TS], BF16, f"ot{i}") for i in range(2)]
            for i in range(2):
                c.op(c.pool, lambda: nc.gpsimd.memset(xin[i][0][:, 0:3], 0.0), writes=[xin[i][1]])
            for kc in range(24):
                i = kc % 2
                x, bx = xin[i]
                a, ba = acc[i]
                o, bo = ot[i]
                c.dma(c.sp, x[:, 3:3 + TS], self.projT[O_XBC + kc * 128:O_XBC + (kc + 1) * 128, :], writes=[bx])
                w = lambda k: self.vecs[:, V_CW + kc * 4 + k:V_CW + kc * 4 + k + 1]

                fns = [lambda: nc.vector.tensor_scalar(out=a[:], in0=x[:, 0:TS], scalar1=w(0), scalar2=None, op0=ALU.mult)]
                for k in range(1, 4):
                    fns.append(lambda k=k: nc.vector.scalar_tensor_tensor(out=a[:], in0=x[:, k:k + TS], scalar=w(k), in1=a[:],
                                                                          op0=ALU.mult, op1=ALU.add))
                c.chain(c.dve, fns, reads=[bx], writes=[ba])
                c.op(c.act, lambda: nc.scalar.activation(out=o[:], in_=a[:], func=AF.Silu,
                                                         bias=self.vecs[:, V_CB + kc:V_CB + kc + 1]),
                     reads=[ba], writes=[bo])
                c.dma(c.pool, self.xbcT[kc * 128:(kc + 1) * 128, :], o[:], reads=[bo], writes=[self.B("xbcT")])

    def ph_ssd(self, with_output):
        nc, c, TS, SEG, NSEG = self.nc, self.c, self.TS, self.SEG, self.NSEG
        with Phase(self, "sd") as ph:
            xbc = self.load_actT(ph, self.xbcT, 24, "xbc")
            bxbc = self.B("actres")
            if with_output:
                zc = [self.TB(ph, [128, 16, 64], BF16, f"zc{i}") for i in range(2)]
                zv = self.projT[0:BW, :].rearrange("(kc p) t -> p kc t", p=128)
            dts, bdts = self.TB(ph, [32, TS], F32, "dts")
            negA, bnA = self.TB(ph, [128, 32], F32, "negA")
            c.dma(c.sp, dts[:], self.dtT[:, :], writes=[bdts])
            c.dma(c.sp, negA[:], self.I("rows")[0:1, 2 * D:2 * D + 32].partition_broadcast(128), writes=[bnA])

            c.chain(c.act, [
                lambda: nc.scalar.activation(out=dts[:], in_=dts[:], func=AF.Exp, bias=self.vecs[0:32, V_DTB:V_DTB + 1]),
                lambda: nc.scalar.activation(out=dts[:], in_=dts[:], func=AF.Ln, bias=self.onesf[0:32, 0:1]),
            ], reads=[bdts], writes=[bdts])
            c.op(c.act, lambda: nc.scalar.activation(out=negA[:], in_=negA[:], func=AF.Exp), reads=[bnA], writes=[bnA])
            c.op(c.dve, lambda: nc.vector.tensor_scalar(out=negA[:], in0=negA[:], scalar1=-1.0, scalar2=None, op0=ALU.mult),
                 reads=[bnA], writes=[bnA])
            c.op(c.dve, lambda: nc.vector.tensor_scalar(out=dts[:, 0:HALO], in0=dts[:, 0:HALO],
                                                        scalar1=self.cmeta[0:32, C_M0:C_M0 + 1], scalar2=None, op0=ALU.mult),
                 reads=[bdts], writes=[bdts])
            tri, btri = self.TB(ph, [64, 64], F32, "tri")
            t2, bt2 = self.TB(ph, [64, 64], F32, "t2")
            ones64 = self.onesf

            c.chain(c.pool, [
                lambda: nc.gpsimd.memset(tri[:], 1.0),
                lambda: nc.gpsimd.memset(t2[:], 1.0),
                lambda: nc.gpsimd.affine_select(out=tri[:], in_=tri[:], pattern=[[1, 64]], compare_op=ALU.is_ge, fill=0.0,
                                                base=0, channel_multiplier=-1),
                lambda: nc.gpsimd.affine_select(out=t2[:], in_=t2[:], pattern=[[-1, 64]], compare_op=ALU.is_gt, fill=0.0,
                                                base=0, channel_multiplier=1),
            ], writes=[btri, bt2])
            S, bS = self.TB(ph, [128, BW], F32, "S")
            Sb, bSb = self.TB(ph, [128, BW], BF16, "Sb")
            tacc, btacc = self.TB(ph, [128, 32], F32, "tacc")
            stmp, bstmp = self.TB(ph, [128, 512], F32, "stmp")
            c.op(c.dve, lambda: nc.vector.memset(S[:], 0.0), writes=[bS])
            c.op(c.dve, lambda: nc.vector.memset(tacc[:], 0.0), writes=[btacc])
            if with_output and NSEG > 1:
                Dall, bD = self.TB(ph, [128, NSEG * NH], F32, "Dall")
                if self.mode == "F":
                    for j in range(NSEG):
                        c.dma(c.sp, Dall[:, j * NH:(j + 1) * NH], self.D_g[j * 128:j * 128 + 1, :].partition_broadcast(128),
                              reads=[self.B("ssdg")], writes=[bD])
                else:
                    c.dma(c.sp, Dall[:], self.I("D_all")[0:1, :].partition_broadcast(128), writes=[bD])
                T, bT = self.TB(ph, [128, BW], F32, "T")
                Lj, bLj = self.TB(ph, [128, BW], F32, "Lj")
                c.op(c.dve, lambda: nc.vector.memset(T[:], 0.0), writes=[bT])
                for j in range(NSEG - 1):
                    Lsrc = self.L_g[j * 128:(j + 1) * 128, :] if self.mode == "F" else self.I("L_all")[j, :, :]
                    c.dma(c.sp, Lj[:], Lsrc, reads=[self.B("ssdg")], writes=[bLj])
                    Tv = T[:].rearrange("p (h d) -> p h d", d=HD)
                    c.op(c.dve, lambda: nc.vector.tensor_tensor(
                        out=Tv, in0=Tv, in1=Dall[:, j * NH:(j + 1) * NH].unsqueeze(2).to_broadcast([128, NH, HD]), op=ALU.mult),
                        reads=[bT, bD], writes=[bT])
                    c.op(c.dve, lambda: nc.vector.tensor_tensor(out=T[:], in0=T[:], in1=Lj[:], op=ALU.add),
                         reads=[bT, bLj], writes=[bT])
                    c.op(c.dve, lambda: nc.vector.scalar_tensor_tensor(out=S[:], in0=T[:], scalar=self.cmeta[:, C_OH + j:C_OH + j + 1],
                                                                       in1=S[:], op0=ALU.mult, op1=ALU.add),
                         reads=[bT, bS], writes=[bS])
            c.op(c.act, lambda: nc.scalar.copy(out=Sb[:], in_=S[:]), reads=[bS], writes=[bSb])
            pA, bpA = self.TB(ph, [128, 512], F32, "pA", psum=True)
            pX, bpX = self.TB(ph, [128, 1024], BF16, "pX", psum=True)
            pR, bpR = self.TB(ph, [128, 512], F32, "pR", psum=True)
            pY, bpY = self.TB(ph, [128, 512], F32, "pY", psum=True)
            pYo, bpYo = self.TB(ph, [128, 512], F32, "pYo", psum=True)
            pSt, bpSt = self.TB(ph, [128, 512], F32, "pSt", psum=True)
            pYT, bpYT = self.TB(ph, [128, 16, 64], F32, "pYT", psum=True)
            dtk, bdtk = self.TB(ph, [64, 32], F32, "dtk")
            dak, bdak = self.TB(ph, [64, 32], F32, "dak")
            acs, bacs = self.TB(ph, [64, 32], F32, "acs")
            ea, bea = self.TB(ph, [64, 32], F32, "ea")
            dte, bdte = self.TB(ph, [64, 32], F32, "dte")
            cdec, bcdec = self.TB(ph, [128, 32], F32, "cdec")
            xdt, bxdt = self.TB(ph, [64, 512], BF16, "xdt")
            xdtw, bxdtw = self.TB(ph, [64, 512], BF16, "xdtw")
            btok, bbtok = self.TB(ph, [64, 128], BF16, "btok")
            Xg, bXg = self.TB(ph, [64, 512], F32, "Xg")
            dec, bdec = self.TB(ph, [64, 512], F32, "dec")
            gm, bgm = self.TB(ph, [64, 64], F32, "gm")
            mt, bmt = self.TB(ph, [64, 512], BF16, "mt")
            ytmp, bytmp = self.TB(ph, [64, 512], F32, "ytmp")
            ytok, bytok = self.TB(ph, [64, BW], F32, "ytok")
            y1, by1 = self.TB(ph, [128, 16, 64], F32, "y1")
            sz, bsz = self.TB(ph, [128, 16, 64], F32, "sz")
            sq, bsq = self.TB(ph, [128, 16, 64], BF16, "sq")
            rs, brs = self.TB(ph, [128, 4, 64], F32, "rs")
            yo = [self.TB(ph, [128, 16, 64], BF16, f"yo{i}") for i in range(2)]
            brv = self.brT[0:BW, :].rearrange("(kc p) t -> p kc t", p=128) if with_output else None
            chunks = [(0, HALO)] + [(HALO + 64 * i, 64) for i in range(SEG // 64)]
            if SSD_STOP == 0:
                chunks = []
            for ic, (t0, L) in enumerate(chunks):
                c.op(c.pe, lambda: nc.tensor.transpose(pA[0:L, 0:32], dts[0:32, t0:t0 + L], self.identf[0:32, 0:32]),
                     reads=[bdts], writes=[bpA])
                c.op(c.act, lambda: nc.scalar.copy(out=dtk[0:L, :], in_=pA[0:L, 0:32]), reads=[bpA], writes=[bdtk])
                c.op(c.dve, lambda: nc.vector.tensor_tensor(out=dak[0:L, :], in0=dtk[0:L, :], in1=negA[0:L, :], op=ALU.mult),
                     reads=[bdtk, bnA], writes=[bdak])

                def fcs():
                    nc.tensor.matmul(pA[0:L, 32:64], lhsT=tri[0:L, 0:L], rhs=dak[0:L, :], start=True, stop=True)
                    nc.tensor.matmul(pA[0:L, 64:96], lhsT=ones64[0:L, 0:L], rhs=dak[0:L, :], start=True, stop=True)
                    return nc.tensor.matmul(pA[0:128, 96:128], lhsT=ones64[0:L, 0:128], rhs=dak[0:L, :], start=True, stop=True)
                c.op(c.pe, fcs, reads=[bdak, btri], writes=[bpA])
                c.op(c.act, lambda: nc.scalar.copy(out=acs[0:L, :], in_=pA[0:L, 32:64]), reads=[bpA], writes=[bacs])
                c.op(c.act, lambda: nc.scalar.activation(out=ea[0:L, :], in_=pA[0:L, 32:64], func=AF.Exp), reads=[bpA], writes=[bea])
                c.op(c.dve, lambda: nc.vector.tensor_tensor(out=dte[0:L, :], in0=pA[0:L, 64:96], in1=acs[0:L, :], op=ALU.subtract),
                     reads=[bpA, bacs], writes=[bdte])
                c.op(c.act, lambda: nc.scalar.activation(out=dte[0:L, :], in_=dte[0:L, :], func=AF.Exp), reads=[bdte], writes=[bdte])
                c.op(c.act, lambda: nc.scalar.activation(out=cdec[:], in_=pA[0:128, 96:128], func=AF.Exp), reads=[bpA], writes=[bcdec])
                c.op(c.dve, lambda: nc.vector.tensor_tensor(out=tacc[:], in0=tacc[:], in1=pA[0:128, 96:128], op=ALU.add),
                     reads=[bpA, btacc], writes=[btacc])
                if SSD_STOP <= 1:
                    continue
                for g in range(NG):
                    hs = slice(8 * g, 8 * g + 8)
                    gs = slice(512 * g, 512 * (g + 1))
                    def ftr():
                        for q in range(4):
                            nc.tensor.transpose(pX[0:L, q * 128:(q + 1) * 128], xbc[:, 4 * g + q, t0:t0 + L], self.ident[:, :])
                        return nc.tensor.transpose(pX[0:L, 512:640], xbc[:, 16 + g, t0:t0 + L], self.ident[:, :])
                    c.op(c.pe, ftr, reads=[bxbc], writes=[bpX])
                    if SSD_STOP <= 1.1:
                        continue
                    bc8 = lambda t: t[0:L, hs].unsqueeze(2).to_broadcast([L, 8, HD])
                    v3 = lambda t: t[0:L, 0:512].rearrange("p (h d) -> p h d", d=HD)
                    c.op(c.dve, lambda: nc.vector.tensor_tensor(out=v3(xdt), in0=v3(pX), in1=bc8(dtk), op=ALU.mult),
                         reads=[bpX, bdtk], writes=[bxdt])
                    c.op(c.act, lambda: nc.scalar.copy(out=btok[0:L, :], in_=pX[0:L, 512:640]), reads=[bpX], writes=[bbtok])
                    c.op(c.dve, lambda: nc.vector.tensor_tensor(out=v3(xdtw), in0=v3(xdt), in1=bc8(dte), op=ALU.mult),
                         reads=[bxdt, bdte], writes=[bxdtw])
                    if with_output and SSD_STOP > 2:
                        vL = lambda t: t[0:L, 0:8 * L].rearrange("p (h l) -> p h l", l=L)
                        c.op(c.dve, lambda: nc.vector.tensor_tensor(
                            out=vL(Xg), in0=dak[0:L, hs].unsqueeze(2).to_broadcast([L, 8, L]),
                            in1=tri[0:L, 0:L].unsqueeze(1).to_broadcast([L, 8, L]), op=ALU.mult),
                            reads=[bdak, btri], writes=[bXg])
                        c.op(c.pe, lambda: nc.tensor.matmul(pR[0:L, 0:8 * L], lhsT=t2[0:L, 0:L], rhs=Xg[0:L, 0:8 * L], start=True, stop=True),
                             reads=[bXg, bt2], writes=[bpR])
                        c.op(c.act, lambda: nc.scalar.activation(out=dec[0:L, 0:8 * L], in_=pR[0:L, 0:8 * L], func=AF.Exp),
                             reads=[bpR], writes=[bdec])
                        c.op(c.pe, lambda: nc.tensor.matmul(pA[0:L, 128:128 + L], lhsT=xbc[:, 16 + g, t0:t0 + L], rhs=xbc[:, 20 + g, t0:t0 + L],
                                                            start=True, stop=True),
                             reads=[bxbc], writes=[bpA])
                        c.op(c.dve, lambda: nc.vector.tensor_tensor(out=gm[0:L, 0:L], in0=pA[0:L, 128:128 + L], in1=tri[0:L, 0:L], op=ALU.mult),
                             reads=[bpA, btri], writes=[bgm])
                        c.op(c.dve, lambda: nc.vector.tensor_tensor(out=vL(mt), in0=vL(dec),
                                                                    in1=gm[0:L, 0:L].unsqueeze(1).to_broadcast([L, 8, L]), op=ALU.mult),
                             reads=[bdec, bgm], writes=[bmt])

                        def fy():
                            for h8 in range(8):
                                ins = nc.tensor.matmul(pY[0:L, h8 * HD:(h8 + 1) * HD], lhsT=mt[0:L, h8 * L:(h8 + 1) * L],
                                                       rhs=xdt[0:L, h8 * HD:(h8 + 1) * HD], start=True, stop=True)
                            return ins
                        c.op(c.pe, fy, reads=[bmt, bxdt], writes=[bpY])
                        c.op(c.pe, lambda: nc.tensor.matmul(pYo[0:L, :], lhsT=xbc[:, 20 + g, t0:t0 + L], rhs=Sb[:, gs], start=True, stop=True),
                             reads=[bxbc, bSb], writes=[bpYo])
                        c.op(c.dve, lambda: nc.vector.tensor_tensor(out=v3(ytmp), in0=v3(pYo), in1=bc8(ea), op=ALU.mult),
                             reads=[bpYo, bea], writes=[bytmp])
                        c.op(c.dve, lambda: nc.vector.tensor_tensor(out=ytok[0:L, gs], in0=ytmp[0:L, :], in1=pY[0:L, :], op=ALU.add),
                             reads=[bytmp, bpY], writes=[bytok])
                    if SSD_STOP <= 1.2:
                        continue
                    c.op(c.pe, lambda: nc.tensor.matmul(pSt[:, :], lhsT=btok[0:L, :], rhs=xdtw[0:L, :], start=True, stop=True),
                         reads=[bbtok, bxdtw], writes=[bpSt])
                    Sv = S[:, gs].rearrange("p (h d) -> p h d", d=HD)
                    c.op(c.dve, lambda: nc.vector.tensor_tensor(out=stmp[:].rearrange("p (h d) -> p h d", d=HD), in0=Sv,
                                                                in1=cdec[:, hs].unsqueeze(2).to_broadcast([128, 8, HD]), op=ALU.mult),
                         reads=[bS, bcdec], writes=[bstmp])
                    c.op(c.dve, lambda: nc.vector.tensor_tensor(out=S[:, gs], in0=stmp[:], in1=pSt[:, :], op=ALU.add),
                         reads=[bstmp, bpSt], writes=[bS])
                    c.op(c.act, lambda: nc.scalar.copy(out=Sb[:, gs], in_=S[:, gs]), reads=[bS], writes=[bSb])
                if not with_output or SSD_STOP <= 3:
                    continue
                def fyt():
                    for q in range(16):
                        ins = nc.tensor.transpose(pYT[:, q, 0:L], ytok[0:L, q * 128:(q + 1) * 128], self.identf[0:L, 0:L])
                    return ins
                c.op(c.pe, fyt, reads=[bytok], writes=[bpYT])
                bq = lambda col: self.vecs[:, col:col + 16].unsqueeze(2).to_broadcast([128, 16, L])
                c.op(c.dve, lambda: nc.vector.tensor_tensor(out=y1[:, :, 0:L], in0=xbc[:, 0:16, t0:t0 + L], in1=bq(V_DS), op=ALU.mult),
                     reads=[bxbc], writes=[by1])
                c.op(c.dve, lambda: nc.vector.tensor_tensor(out=y1[:, :, 0:L], in0=y1[:, :, 0:L], in1=pYT[:, :, 0:L], op=ALU.add),
                     reads=[by1, bpYT], writes=[by1])
                zt_, bz = zc[ic % 2]
                c.dma(c.sp, zt_[:, :, 0:L], zv[:, :, t0:t0 + L], writes=[bz])
                c.op(c.act, lambda: nc.scalar.activation(out=sz[:, :, 0:L], in_=zt_[:, :, 0:L], func=AF.Silu), reads=[bz], writes=[bsz])
                c.op(c.dve, lambda: nc.vector.tensor_tensor(out=y1[:, :, 0:L], in0=y1[:, :, 0:L], in1=sz[:, :, 0:L], op=ALU.mult),
                     reads=[by1, bsz], writes=[by1])
                c.op(c.dve, lambda: nc.vector.tensor_tensor(out=sq[:, :, 0:L], in0=y1[:, :, 0:L], in1=y1[:, :, 0:L], op=ALU.mult),
                     reads=[by1], writes=[bsq])

                def fss():
                    for g in range(4):
                        for q in range(4):
                            ins = nc.tensor.matmul(pR[:, g * 64:g * 64 + L], lhsT=self.onesb[:, :], rhs=sq[:, 4 * g + q, 0:L],
                                                   start=(q == 0), stop=(q == 3))
                    return ins
                c.op(c.pe, fss, reads=[bsq], writes=[bpR])
                pRv = pR[:, 0:256].rearrange("p (g l) -> p g l", l=64)
                c.op(c.act, lambda: nc.scalar.activation(out=rs[:, :, 0:L], in_=pRv[:, :, 0:L], func=AF.Sqrt, scale=1.0 / 512,
                                                         bias=self.epsc[:, :]),
                     reads=[bpR], writes=[brs])
                c.op(c.dve, lambda: nc.vector.reciprocal(out=rs[:, :, 0:L], in_=rs[:, :, 0:L]), reads=[brs], writes=[brs])
                y1v = y1[:, :, 0:L].rearrange("p (g q) l -> p g q l", q=4)
                c.op(c.dve, lambda: nc.vector.tensor_tensor(out=y1v, in0=y1v, in1=rs[:, :, 0:L].unsqueeze(2).to_broadcast([128, 4, 4, L]),
                                                            op=ALU.mult),
                     reads=[by1, brs], writes=[by1])
                o, bo = yo[ic % 2]
                c.op(c.dve, lambda: nc.vector.tensor_tensor(out=o[:, :, 0:L], in0=y1[:, :, 0:L], in1=bq(V_SN), op=ALU.mult),
                     reads=[by1], writes=[bo])
                c.dma(c.pool, brv[:, :, t0:t0 + L], o[:, :, 0:L], reads=[bo], writes=[self.B("brT")])
            if not with_output:
                c.dma(c.pool, self.L_out[:, :], S[:], reads=[bS], writes=[self.B("ssdl")])
                c.op(c.act, lambda: nc.scalar.activation(out=tacc[:], in_=tacc[:], func=AF.Exp), reads=[btacc], writes=[btacc])
                c.dma(c.pool, self.D_out[:, :], tacc[:], reads=[btacc], writes=[self.B("ssdl")])


    def gemm_B(self, ph, actT, KC, NT, blocks, tag, odt):
        nc, c = self.nc, self.c
        NW = max(b["nw"] for b in blocks)
        wst, bw = self.TB(ph, [128, KC, NW], F32, "bwst")
        wbf = [self.TB(ph, [128, KC, NW], BF16, f"bwbf{i}") for i in range(2)]
        ost = [self.TB(ph, [128, NW], odt, f"bost{i}") for i in range(3)]
        pss = [self.TB(ph, [128, 512], F32, f"bps{i}", psum=True) for i in range(3)]
        bact = self.B("actres")
        n = 0
        for ib, blk in enumerate(blocks):
            nw = blk["nw"]
            wb, bwb = wbf[ib % 2]
            step = max(1, KC // 4)
            if blk.get("Wv") is not None:
                Wv = blk["Wv"]
                for k0 in range(0, KC, step):
                    c.dma(c.sp, wst[:, k0:k0 + step, 0:nw].rearrange("p k (a b) -> p k a b", b=Wv.shape[3]), Wv[:, k0:k0 + step, :, :],
                          writes=[bw])
            else:
                Wv = blk["W"].rearrange("(kc p) m -> p kc m", p=128)
                for k0 in range(0, KC, step):
                    c.dma(c.sp, wst[:, k0:k0 + step, 0:nw], Wv[:, k0:k0 + step, :], writes=[bw])
            half = max(1, KC // 2)
            c.op(c.dve, lambda: nc.vector.tensor_copy(out=wb[:, 0:half, 0:nw], in_=wst[:, 0:half, 0:nw]), reads=[bw], writes=[bwb])
            if half < KC:
                c.op(c.pool, lambda: nc.gpsimd.tensor_copy(out=wb[:, half:KC, 0:nw], in_=wst[:, half:KC, 0:nw]), reads=[bw], writes=[bwb])
            t0 = 0
            while t0 < NT:
                nt = min(128, NT - t0)
                p, bp = pss[n % 3]
                o, bo = ost[n % 3]
                n += 1

                def fm():
                    for k in range(KC):
                        ins = nc.tensor.matmul(p[0:nt, 0:nw], lhsT=actT[:, k, t0:t0 + nt], rhs=wb[:, k, 0:nw],
                                               start=(k == 0), stop=(k == KC - 1))
                    return ins
                c.op(c.pe, fm, reads=[bwb, bact], writes=[bp])
                c.op(c.act, lambda: nc.scalar.copy(out=o[0:nt, 0:nw], in_=p[0:nt, 0:nw]), reads=[bp], writes=[bo])
                c.dma(c.pool, blk["out"][t0:t0 + nt, :], o[0:nt, 0:nw], reads=[bo], writes=[self.B("gemmB_out")])
                t0 += nt

    def fm_rmsnorm(self, ph, row0, KC, wcol, out_dram, tag):
        nc, c, TS = self.nc, self.c, self.TS
        x, bx = self.TB(ph, [128, KC, TS], BF16, tag + "x")
        sq, bsq = self.TB(ph, [128, KC, 512], BF16, tag + "sq")
        rs, brs = self.TB(ph, [128, 512], F32, tag + "rs")
        o, bo = self.TB(ph, [128, KC, TS], BF16, tag + "o")
        p, bp = self.TB(ph, [128, 512], F32, tag + "p", psum=True)
        c.dma(c.sp, x[:], self.projT[row0:row0 + KC * 128, :].rearrange("(kc p) t -> p kc t", p=128), writes=[bx])
        for (t0, nt) in ttiles(TS, 512):
            c.op(c.dve, lambda: nc.vector.tensor_tensor(out=sq[:, :, 0:nt], in0=x[:, :, t0:t0 + nt], in1=x[:, :, t0:t0 + nt], op=ALU.mult),
                 reads=[bx], writes=[bsq])

            def fs():
                for k in range(KC):
                    ins = nc.tensor.matmul(p[:, 0:nt], lhsT=self.onesb[:, :], rhs=sq[:, k, 0:nt], start=(k == 0), stop=(k == KC - 1))
                return ins
            c.op(c.pe, fs, reads=[bsq], writes=[bp])
            c.op(c.act, lambda: nc.scalar.activation(out=rs[:, 0:nt], in_=p[:, 0:nt], func=AF.Sqrt, scale=1.0 / (KC * 128),
                                                     bias=self.epsc[:, :]), reads=[bp], writes=[brs])
            c.op(c.dve, lambda: nc.vector.reciprocal(out=rs[:, 0:nt], in_=rs[:, 0:nt]), reads=[brs], writes=[brs])
            for k in range(KC):
                c.op(c.dve, lambda: nc.vector.scalar_tensor_tensor(out=o[:, k, t0:t0 + nt], in0=x[:, k, t0:t0 + nt],
                                                                   scalar=self.vecs[:, wcol + k:wcol + k + 1], in1=rs[:, 0:nt],
                                                                   op0=ALU.mult, op1=ALU.mult),
                     reads=[bx, brs], writes=[bo])
        c.dma(c.pool, out_dram.rearrange("(kc p) t -> p kc t", p=128), o[:], reads=[bo], writes=[self.B(tag + "out")])

    def make_rope(self, ph):
        nc, c, TS = self.nc, self.c, self.TS
        ang, ba = self.TB(ph, [64, TS], F32, "ang")
        self.cos2, self.bcos = self.TB(ph, [64, TS], F32, "cos2")
        self.sin2, self.bsin = self.TB(ph, [64, TS], F32, "sin2")
        fr, bfr = self.TB(ph, [64, 1], F32, "fr")
        rmf, brm = self.TB(ph, [64, 64], F32, "rmf")
        self.rm, self.brm = self.TB(ph, [64, 64], BF16, "rm")
        pi = float(np.pi)
        negpi, bnp = self.TB(ph, [64, 1], F32, "negpi")

        kf, bkf = self.TB(ph, [64, TS], F32, "kf")
        ki, bki = self.TB(ph, [64, TS], mybir.dt.int32, "ki")
        wr, bwr = self.TB(ph, [64, TS], F32, "wr")

        c.chain(c.pool, [
            lambda: nc.gpsimd.iota(ang[:], pattern=[[1, TS]], base=0, channel_multiplier=0, allow_small_or_imprecise_dtypes=True),
            lambda: nc.gpsimd.memset(rmf[:], 0.0),
            lambda: nc.gpsimd.affine_select(out=rmf[:], in_=rmf[:], pattern=[[1, 64]], compare_op=ALU.not_equal, fill=1.0, base=-32, channel_multiplier=-1),
            lambda: nc.gpsimd.affine_select(out=rmf[:], in_=rmf[:], pattern=[[-1, 64]], compare_op=ALU.not_equal, fill=-1.0, base=-32, channel_multiplier=1),
            lambda: nc.gpsimd.tensor_copy(out=self.rm[:], in_=rmf[:]),
        ], writes=[ba, brm, self.brm])
        c.op(c.dve, lambda: nc.vector.tensor_scalar(out=ang[:], in0=ang[:], scalar1=self.cmeta[0:64, C_POS:C_POS + 1],
                                                    scalar2=self.vecs[0:64, V_IF:V_IF + 1], op0=ALU.add, op1=ALU.mult),
             reads=[ba], writes=[ba])
        C1 = 6.28125
        C2 = float(2 * np.pi - C1)
        for (dst, bdst, shift) in ((self.sin2, self.bsin, 0.0), (self.cos2, self.bcos, pi / 2)):
            c.chain(c.dve, [
                lambda: nc.vector.tensor_scalar(out=dst[:], in0=ang[:], scalar1=shift, scalar2=None, op0=ALU.add),
                lambda: nc.vector.tensor_scalar(out=kf[:], in0=dst[:], scalar1=1.0 / (2 * pi), scalar2=None, op0=ALU.mult),
                lambda: nc.vector.tensor_copy(out=ki[:], in_=kf[:]),
                lambda: nc.vector.tensor_copy(out=kf[:], in_=ki[:]),
                lambda: nc.vector.scalar_tensor_tensor(out=dst[:], in0=kf[:], scalar=-C1, in1=dst[:], op0=ALU.mult, op1=ALU.add),
                lambda: nc.vector.scalar_tensor_tensor(out=dst[:], in0=kf[:], scalar=-C2, in1=dst[:], op0=ALU.mult, op1=ALU.add),
                lambda: nc.vector.tensor_scalar(out=wr[:], in0=dst[:], scalar1=pi, scalar2=-2 * pi, op0=ALU.is_gt, op1=ALU.mult),
                lambda: nc.vector.tensor_tensor(out=dst[:], in0=dst[:], in1=wr[:], op=ALU.add),
                lambda: nc.vector.tensor_scalar(out=wr[:], in0=dst[:], scalar1=-pi, scalar2=2 * pi, op0=ALU.is_lt, op1=ALU.mult),
                lambda: nc.vector.tensor_tensor(out=dst[:], in0=dst[:], in1=wr[:], op=ALU.add),
                lambda: nc.vector.tensor_scalar(out=dst[:], in0=dst[:], scalar1=pi, scalar2=-pi, op0=ALU.min, op1=ALU.max),
            ], reads=[ba], writes=[bdst, bkf, bki, bwr])
            c.op(c.act, lambda: nc.scalar.activation(out=dst[:], in_=dst[:], func=AF.Sin), reads=[bdst], writes=[bdst])

    def rope_apply(self, ph, src, bsrc, dst, bdst, pr, bpr, t1, bt1, t2, bt2, t0, nt):
        nc, c = self.nc, self.c
        c.op(c.pe, lambda: nc.tensor.matmul(pr[0:64, 0:nt], lhsT=self.rm[:, :], rhs=src[0:64, t0:t0 + nt], start=True, stop=True),
             reads=[bsrc, self.brm], writes=[bpr])
        c.op(c.dve, lambda: nc.vector.tensor_tensor(out=t1[0:64, 0:nt], in0=src[0:64, t0:t0 + nt], in1=self.cos2[:, t0:t0 + nt], op=ALU.mult),
             reads=[bsrc, self.bcos], writes=[bt1])
        c.op(c.dve, lambda: nc.vector.tensor_tensor(out=t2[0:64, 0:nt], in0=pr[0:64, 0:nt], in1=self.sin2[:, t0:t0 + nt], op=ALU.mult),
             reads=[bpr, self.bsin], writes=[bt2])
        c.op(c.dve, lambda: nc.vector.tensor_tensor(out=dst[0:64, t0:t0 + nt], in0=t1[0:64, 0:nt], in1=t2[0:64, 0:nt], op=ALU.add),
             reads=[bt1, bt2], writes=[bdst])

    def ph_mla_prep(self, with_q):
        nc, c, TS = self.nc, self.c, self.TS
        with Phase(self, "mp") as ph:
            self.make_rope(ph)
            if with_q:
                self.S("qnT", "qnT", [QR, TS], BF16)
                self.fm_rmsnorm(ph, O_Q, 8, V_QN, self.qnT[:, :], "qn")
            self.fm_rmsnorm(ph, O_KV, 4, V_KN, self.kvx[0:KVR, :], "kn")
            kr, bkr = self.TB(ph, [64, TS], BF16, "kr")
            ko, bko = self.TB(ph, [64, TS], BF16, "ko")
            t1, bt1 = self.TB(ph, [64, 512], F32, "t1")
            t2, bt2 = self.TB(ph, [64, 512], F32, "t2")
            pr, bpr = self.TB(ph, [128, 512], F32, "pr", psum=True)
            c.dma(c.sp, kr[:], self.projT[O_KR:O_KR + 64, :], writes=[bkr])
            for (t0, nt) in ttiles(TS, 512):
                self.rope_apply(ph, kr, bkr, ko, bko, pr, bpr, t1, bt1, t2, bt2, t0, nt)
            c.dma(c.pool, self.kvx[KVR:KVR + 64, :], ko[:], reads=[bko], writes=[self.B("kvx")])

    def ph_mla_proj(self):
        nc, c, TS, SEG, NK = self.nc, self.c, self.TS, self.SEG, self.NK
        NKC, NKG = self.NKC, self.NKG
        sc = 1.0 / float(np.sqrt(NOPE + ROPE))
        self.S("qT", "qT", [MH * 192, TS], BF16)
        self.S("kT", "kT", [MH * 128, NKC], BF16)
        self.S("vTok", "vTok", [NKC, MH * 128], BF16)
        with Phase(self, "mq") as ph:
            actT = self.load_actT(ph, self.qnT, 8)
            blocks = []
            for h in range(MH):
                blocks.append(dict(W=self.I("w_q_b")[:, h * 192:h * 192 + 128], mw=128, out=self.qT[h * 192:h * 192 + 128, :], scale=sc))
                blocks.append(dict(W=self.I("w_q_b")[:, h * 192 + 128:h * 192 + 192], mw=64, out=self.qT[h * 192 + 128:h * 192 + 192, :], scale=sc))
            self.gemm_A(ph, actT, 8, blocks, "mq")
        with Phase(self, "mk") as ph:
            kvn, bk = ph.sb([128, 4, NKC], BF16, "kvnC"), self.B("actres")
            for k in range(4):
                for (c0, ncol, src) in self.kv_pieces(k):
                    c.dma(c.sp, kvn[:, k, c0:c0 + ncol], src, reads=[self.B("kvg")], writes=[bk])
            c.dma(c.sp, kvn[:, :, NKG:NKC], self.kvx[0:KVR, HALO:TS].rearrange("(kc p) t -> p kc t", p=128), writes=[bk])
            tts = []
            t = 0
            while t < NKC:
                tts.append((t, min(512, NKC - t)))
                t += 512
            blocks = [dict(W=self.I("w_kv_b")[:, h * 256:h * 256 + 128], mw=128, out=self.kT[h * 128:(h + 1) * 128, :]) for h in range(MH)]
            self.gemm_A(ph, kvn, 4, blocks, "mk", tts=tts, width=NKC)
            wv4 = self.I("w_kv_b").rearrange("(kc p) (h t) -> p kc h t", p=128, t=256)
            blocks = [dict(Wv=wv4[:, :, 4 * g:4 * g + 4, 128:256], nw=512, out=self.vTok[:, g * 512:(g + 1) * 512]) for g in range(MH // 4)]
            self.gemm_B(ph, kvn, 4, NKC, blocks, "mv", BF16)

    def ph_mla_attn(self):
        nc, c, TS, SEG, NK, NSEG = self.nc, self.c, self.TS, self.SEG, self.NK, self.NSEG
        NKC, NKG = self.NKC, self.NKG
        QT = min(512, SEG)
        with Phase(self, "at") as ph:
            self.make_rope(ph)
            kpe, bkpe = self.TB(ph, [64, NKC], BF16, "kpe")
            for (c0, ncol, src) in self.kv_pieces(4):
                c.dma(c.sp, kpe[:, c0:c0 + ncol], src, reads=[self.B("kvg")], writes=[bkpe])
            c.dma(c.sp, kpe[:, NKG:NKC], self.kvx[KVR:KVR + 64, HALO:TS], writes=[bkpe])
            nd = QT // 128
            masks = []
            for d_ in range(nd):
                m, bm = self.TB(ph, [128, QT], BF16, f"mask{d_}")

                fns = [lambda: nc.gpsimd.memset(m[:], 0.0)]
                for kh in range(2):
                    c0 = 64 * (2 * d_ + kh)
                    if c0 < QT:
                        fns.append(lambda kh=kh, c0=c0: nc.gpsimd.memset(m[64 * kh:64 * kh + 64, c0:QT], 1.0))
                c.chain(c.pool, fns, writes=[bm])
                masks.append((m, bm))
            qn = [self.TB(ph, [128, TS], BF16, f"qn{i}") for i in range(2)]
            qr = [self.TB(ph, [64, TS], BF16, f"qr{i}") for i in range(2)]
            qp, bqp = self.TB(ph, [64, TS], BF16, "qp")
            kt = [self.TB(ph, [128, NKC], BF16, f"kt{i}") for i in range(2)]
            NKT = (NKC - HALO) // 128
            vv = [self.TB(ph, [128, NKT, 128], BF16, f"vv{i}") for i in range(2)]
            vm = [self.TB(ph, [16, 128], BF16, f"vm{i}") for i in range(2)]
            gt = [self.TB(ph, [128, TS], BF16, f"gt{i}") for i in range(2)]
            gf, bgf = self.TB(ph, [128, 512], F32, "gf")
            t1, bt1 = self.TB(ph, [64, 512], F32, "t1")
            t2, bt2 = self.TB(ph, [64, 512], F32, "t2")
            pt = [self.TB(ph, [128, 512], BF16, f"pt{i}") for i in range(3)]
            rden, brden = self.TB(ph, [128, 512], F32, "rden")
            of, bof = self.TB(ph, [128, 512], F32, "of")
            ob = [self.TB(ph, [128, TS], BF16, f"ob{i}") for i in range(2)]
            pS = [self.TB(ph, [128, 512], F32, f"pS{i}", psum=True) for i in range(3)]
            pO, bpO = self.TB(ph, [128, 512], F32, "pO", psum=True)
            pD, bpD = self.TB(ph, [128, 512], F32, "pD", psum=True)
            pr, bpr = self.TB(ph, [128, 512], F32, "pr", psum=True)
            nS = 0
            for h in range(MH):
                i = h % 2
                (qn_, bqn), (qr_, bqr), (kt_, bkt), (vv_, bvv), (vm_, bvm), (gt_, bgt), (ob_, bob) = qn[i], qr[i], kt[i], vv[i], vm[i], gt[i], ob[i]
                c.dma(c.sp, qn_[:], self.qT[h * 192:h * 192 + 128, :], writes=[bqn])
                c.dma(c.sp, qr_[:], self.qT[h * 192 + 128:h * 192 + 192, :], writes=[bqr])
                c.dma(c.sp, kt_[:], self.kT[h * 128:(h + 1) * 128, :], writes=[bkt])
                c.dma(c.sp, vm_[:], self.vTok[0:HALO, h * 128:(h + 1) * 128], writes=[bvm])
                c.dma(c.sp, vv_[:], self.vTok[HALO:NKC, h * 128:(h + 1) * 128].rearrange("(kt p) v -> p kt v", p=128), writes=[bvv])
                c.dma(c.sp, gt_[:], self.projT[O_MG + h * 128:O_MG + (h + 1) * 128, :], writes=[bgt])
                for (t0, nt) in ttiles(TS, QT):
                    self.rope_apply(ph, qr_, bqr, qp, bqp, pr, bpr, t1, bt1, t2, bt2, t0, nt)
                for iq, (t0, nt) in enumerate(ttiles(TS, QT)):
                    kts = [(0, HALO, None, None, vm_[0:HALO, :])]
                    if iq > 0:
                        for j in range(NSEG - 1):
                            for ii in range(SEG // 128):
                                kti = j * (SEG // 128) + ii
                                kts.append((HALO + kti * 128, 128, self.cmeta[:, C_VIS + j:C_VIS + j + 1], None, vv_[:, kti, :]))
                        a = iq - 1
                        for ii in range(SEG // 128):
                            d_ = ii - a * nd
                            if d_ >= nd:
                                continue
                            kti = (NSEG - 1) * (SEG // 128) + ii
                            kts.append((NKG + ii * 128, 128, None, masks[d_] if d_ >= 0 else None, vv_[:, kti, :]))
                    pendq = []
                    for ik, (k0, nk, bias, mask, vl) in enumerate(kts):
                        ps_, bps = pS[nS % 3]
                        pt_, bpt = pt[nS % 3]
                        nS += 1

                        def fs():
                            nc.tensor.matmul(ps_[0:nk, 0:nt], lhsT=kt_[:, k0:k0 + nk], rhs=qn_[:, t0:t0 + nt], start=True, stop=False)
                            return nc.tensor.matmul(ps_[0:nk, 0:nt], lhsT=kpe[:, k0:k0 + nk], rhs=qp[:, t0:t0 + nt], start=False, stop=True)
                        c.op(c.pe, fs, reads=[bkt, bqn, bkpe, bqp], writes=[bps])
                        kw = {} if bias is None else {"bias": bias[0:nk, :]}
                        c.op(c.act, lambda: nc.scalar.activation(out=pt_[0:nk, 0:nt], in_=ps_[0:nk, 0:nt], func=AF.Exp, **kw),
                             reads=[bps], writes=[bpt])
                        if mask is not None:
                            c.op(c.dve, lambda: nc.vector.tensor_tensor(out=pt_[0:nk, 0:nt], in0=pt_[0:nk, 0:nt], in1=mask[0][0:nk, 0:nt], op=ALU.mult),
                                 reads=[bpt, mask[1]], writes=[bpt])
                        def mk_fo(pt_=pt_, bpt=bpt, nk=nk, vl=vl, first=(ik == 0), last=(ik == len(kts) - 1)):
                            def fo():
                                nc.tensor.matmul(pO[:, 0:nt], lhsT=vl, rhs=pt_[0:nk, 0:nt], start=first, stop=last)
                                return nc.tensor.matmul(pD[:, 0:nt], lhsT=self.onesb[0:nk, :], rhs=pt_[0:nk, 0:nt], start=first, stop=last)
                            return lambda: c.op(c.pe, fo, reads=[bpt, bvv, bvm], writes=[bpO, bpD])
                        pendq.append(mk_fo())
                        if len(pendq) > 2:
                            pendq.pop(0)()
                    for f_ in pendq:
                        f_()

                    c.op(c.dve, lambda: nc.vector.reciprocal(out=rden[:, 0:nt], in_=pD[:, 0:nt]), reads=[bpD], writes=[brden])
                    c.op(c.dve, lambda: nc.vector.tensor_tensor(out=of[:, 0:nt], in0=pO[:, 0:nt], in1=rden[:, 0:nt], op=ALU.mult),
                         reads=[bpO, brden], writes=[bof])
                    c.op(c.act, lambda: nc.scalar.activation(out=gf[:, 0:nt], in_=gt_[:, t0:t0 + nt], func=AF.Silu), reads=[bgt], writes=[bgf])
                    c.op(c.dve, lambda: nc.vector.tensor_tensor(out=ob_[:, t0:t0 + nt], in0=of[:, 0:nt], in1=gf[:, 0:nt], op=ALU.mult),
                         reads=[bof, bgf], writes=[bob])
                c.dma(c.pool, self.brT[BW + h * 128:BW + (h + 1) * 128, :], ob_[:], reads=[bob], writes=[self.B("brT")])


    def ph_pool(self):
        nc, c, TS = self.nc, self.c, self.TS
        with Phase(self, "pl") as ph:
            mixed = ph.sb([128, 16, TS], BF16, "mixed")
            bmx = self.B("actres")
            ic, bic = self.TB(ph, [128, 4, HALO], F32, "ic")

            fns = []
            for g in range(4):
                w = 2 ** (g + 1)
                fns.append(lambda g=g, w=w: nc.gpsimd.memset(ic[:, g, :], 1.0 / w))
                for t in range(w - 1):
                    fns.append(lambda g=g, t=t: nc.gpsimd.memset(ic[:, g, t:t + 1], 1.0 / (t + 1)))
            c.chain(c.pool, fns, writes=[bic])
            ub = [self.TB(ph, [128, TS], BF16, f"ub{i}") for i in range(2)]
            uf, buf_ = self.TB(ph, [128, TS], F32, "uf")
            s0, bs0 = self.TB(ph, [128, TS], F32, "s0")
            s1, bs1 = self.TB(ph, [128, TS], F32, "s1")
            th, bth = self.TB(ph, [128, HALO], F32, "th")
            for q in range(16):
                g = q // 4
                w = 2 ** (g + 1)
                u, bu = ub[q % 2]
                c.dma(c.sp, u[:], self.projT[O_PU + q * 128:O_PU + (q + 1) * 128, :], writes=[bu])
                c.op(c.act, lambda: nc.scalar.copy(out=uf[:], in_=u[:]), reads=[bu], writes=[buf_])
                cur, bcur, nxt, bnxt = uf, buf_, s0, bs0
                step = 1
                while step < w:
                    def fw():
                        nc.vector.tensor_copy(out=nxt[:, 0:step], in_=cur[:, 0:step])
                        return nc.vector.tensor_tensor(out=nxt[:, step:TS], in0=cur[:, step:TS], in1=cur[:, 0:TS - step], op=ALU.add)
                    c.op(c.dve, fw, reads=[bcur], writes=[bnxt])
                    if nxt is s0:
                        cur, bcur, nxt, bnxt = s0, bs0, s1, bs1
                    else:
                        cur, bcur, nxt, bnxt = s1, bs1, s0, bs0
                    step *= 2

                c.op(c.dve, lambda: nc.vector.scalar_tensor_tensor(out=mixed[:, q, HALO:TS], in0=cur[:, HALO:TS], scalar=1.0 / w,
                                                                   in1=uf[:, HALO:TS], op0=ALU.mult, op1=ALU.subtract),
                     reads=[bcur, buf_], writes=[bmx])
                c.op(c.dve, lambda: nc.vector.tensor_tensor(out=th[:], in0=cur[:, 0:HALO], in1=ic[:, g, :], op=ALU.mult),
                     reads=[bcur, bic], writes=[bth])
                c.op(c.dve, lambda: nc.vector.tensor_tensor(out=mixed[:, q, 0:HALO], in0=th[:], in1=uf[:, 0:HALO], op=ALU.subtract),
                     reads=[bth, buf_], writes=[bmx])
            for g in range(4):
                with Phase(self, f"pg{g}") as ph2:
                    blocks = []
                    for m in range(4):
                        r0 = g * 512 + m * 128
                        blocks.append(dict(W=self.I("w_pool")[g, :, m * 128:(m + 1) * 128], mw=128, out=self.brT[2 * BW + r0:2 * BW + r0 + 128, :],
                                           func=AF.Identity, scale=self.vecs[:, V_PS + 4 * g + m:V_PS + 4 * g + m + 1],
                                           pm=(self.projT[O_PG + r0:O_PG + r0 + 128, :], AF.Silu)))
                    self.gemm_A(ph2, mixed[:, 4 * g:4 * g + 4, :], 4, blocks, f"pg{g}")

    def ph_branch(self):
        nc, c, TS = self.nc, self.c, self.TS
        self.S("brW", "brW", [3 * D, TS], BF16)
        for i in range(3):
            with Phase(self, f"bw{i}") as ph:
                actT = self.load_actT(ph, self.brT[i * BW:(i + 1) * BW, :], 16)
                blocks = [dict(W=self.I("w_branch")[i, :, m * 128:(m + 1) * 128], mw=128,
                               wkey=("w_branch", i), c0=m * 128, wfull=self.I("w_branch")[i],
                               out=self.brW[i * D + m * 128:i * D + (m + 1) * 128, :],
                               pm=(self.sigT[i * D + m * 128:i * D + (m + 1) * 128, :], AF.Copy)) for m in range(32)]
                self.gemm_A(ph, actT, 16, blocks, f"bw{i}")
        with Phase(self, "mg") as ph:
            tl = [[self.TB(ph, [128, TS], BF16, f"m{i}_{j}") for j in range(3)] for i in range(2)]
            acc = [self.TB(ph, [128, TS], F32, f"macc{i}") for i in range(2)]
            mo = [self.TB(ph, [128, TS], BF16, f"mo{i}") for i in range(2)]
            for kc in range(32):
                i = kc % 2
                for j in range(3):
                    c.dma(c.sp, tl[i][j][0][:], self.brW[j * D + kc * 128:j * D + (kc + 1) * 128, :], writes=[tl[i][j][1]])
                a, ba = acc[i]
                o, bo = mo[i]
                c.op(c.dve, lambda: nc.vector.tensor_tensor(out=a[:], in0=tl[i][0][0][:], in1=tl[i][1][0][:], op=ALU.add),
                     reads=[tl[i][0][1], tl[i][1][1]], writes=[ba])
                c.op(c.dve, lambda: nc.vector.tensor_tensor(out=o[:], in0=a[:], in1=tl[i][2][0][:], op=ALU.add),
                     reads=[ba, tl[i][2][1]], writes=[bo])
                c.dma(c.pool, self.mergedT[kc * 128:(kc + 1) * 128, :], o[:], reads=[bo], writes=[self.B("mergedT")])

    def ph_out(self):
        nc, c, TS = self.nc, self.c, self.TS
        with Phase(self, "op") as ph:
            actT = self.load_actT(ph, self.mergedT, 32)
            blocks = [dict(W=self.I("w_out")[:, cb * 256:(cb + 1) * 256], nw=256, out=self.outF[:, cb * 256:(cb + 1) * 256]) for cb in range(16)]
            self.gemm_B(ph, actT, 32, TS, blocks, "op", F32)
        with Phase(self, "fn") as ph:
            postw, bpw = self.TB(ph, [128, D], F32, "postw")
            c.dma(c.sp, postw[:], self.I("rows")[0:1, D:2 * D].partition_broadcast(128), writes=[bpw])
            ot = [self.TB(ph, [128, D], F32, f"fo{i}") for i in range(2)]
            ht = [self.TB(ph, [128, D], F32, f"fh{i}") for i in range(2)]
            junk, bj = self.TB(ph, [128, D], BF16, "fjunk")
            ss = [self.TB(ph, [128, 1], F32, f"fss{i}") for i in range(2)]
            for it, (t0, nt) in enumerate(ttiles(TS, 128)):
                i = it % 2
                (o, bo), (hh, bh), (s_, bs) = ot[i], ht[i], ss[i]
                c.dma(c.sp, o[0:nt, :], self.outF[t0:t0 + nt, :], writes=[bo])
                c.dma(c.sp, hh[0:nt, :], self.hsrc()[t0:t0 + nt, :], reads=[self.B("hres")], writes=[bh])
                c.op(c.act, lambda: nc.scalar.activation(out=junk[0:nt, :], in_=o[0:nt, :], func=AF.Square, accum_out=s_[0:nt, :]),
                     reads=[bo], writes=[bj, bs])
                c.op(c.act, lambda: nc.scalar.activation(out=s_[0:nt, :], in_=s_[0:nt, :], func=AF.Sqrt, scale=1.0 / D, bias=self.epsc[0:nt, :]),
                     reads=[bs], writes=[bs])
                c.op(c.dve, lambda: nc.vector.reciprocal(out=s_[0:nt, :], in_=s_[0:nt, :]), reads=[bs], writes=[bs])
                c.op(c.dve, lambda: nc.vector.scalar_tensor_tensor(out=o[0:nt, :], in0=o[0:nt, :], scalar=s_[0:nt, 0:1], in1=postw[0:nt, :],
                                                                   op0=ALU.mult, op1=ALU.mult),
                     reads=[bo, bs, bpw], writes=[bo])
                c.op(c.dve, lambda: nc.vector.tensor_tensor(out=o[0:nt, :], in0=o[0:nt, :], in1=hh[0:nt, :], op=ALU.add),
                     reads=[bo, bh], writes=[bo])
                c.dma(c.pool, self.hdst()[t0:t0 + nt, :], o[0:nt, :], reads=[bo], writes=[self.B("hdst")])


    def exchange_mid(self):
        c = self.c
        c.barrier()
        for k in range(5):
            r0 = k * 128
            nr = 128 if k < 4 else ROPE
            c.dma(c.sp, self.kvc[k][:, :], self.kvx[r0:r0 + nr, :], writes=[self.B(f"kvc{k}")])
        c.barrier()
        for k in range(5):
            c.coll("AllGather", self.groups, self.kvc[k], self.kvc_g[k], writes=[self.B("kvg")])
        c.coll("AllGather", self.groups, self.L_out, self.L_g, writes=[self.B("ssdg")])
        c.coll("AllGather", self.groups, self.D_out, self.D_g, writes=[self.B("ssdg")])
        c.barrier()

    def ph_halo_exchange(self):
        nc, c, TS, NSEG = self.nc, self.c, self.TS, self.NSEG
        with Phase(self, "hx") as ph:
            c.dma(c.sp, self.tail_loc[:, :], self.h1[TS - HALO:TS, :], writes=[self.B("tail")])
            c.barrier()
            c.coll("AllGather", self.groups, self.tail_loc, self.tails_g, writes=[self.B("tailg")])
            c.barrier()
            own, bown = self.TB(ph, [HALO, D], F32, "own")
            tl, btl = self.TB(ph, [HALO, NSEG, D], F32, "tl")
            c.dma(c.sp, own[:], self.h1[0:HALO, :], writes=[bown])
            c.dma(c.sp, tl[:], self.tails_g.rearrange("(j r) d -> r j d", r=HALO), writes=[btl])
            c.op(c.dve, lambda: nc.vector.tensor_scalar(out=own[:], in0=own[:], scalar1=self.cmeta[0:HALO, C_M0:C_M0 + 1], scalar2=None,
                                                        op0=ALU.mult), reads=[bown], writes=[bown])
            for j in range(NSEG - 1):
                c.op(c.dve, lambda: nc.vector.scalar_tensor_tensor(out=own[:], in0=tl[:, j, :], scalar=self.cmeta[0:HALO, C_OH + j:C_OH + j + 1],
                                                                   in1=own[:], op0=ALU.mult, op1=ALU.add),
                     reads=[btl, bown], writes=[bown])
            c.dma(c.pool, self.h1[0:HALO, :], own[:], reads=[bown], writes=[self.B("hdst")])


def build_program(SEG, NSEG, mode, dbg=False, phases=None):
    P = Prog(SEG, NSEG, mode, dbg)
    c = P.c
    with contextlib.ExitStack() as es:
        class _G:
            pass
        gph = Phase(P, "glob")
        gph.__enter__()
        P.load_consts(gph)
        phases = phases or (["norm", "inproj", "conv", "ssd", "mla", "pool", "out"] if mode == "B" else ["norm", "inproj", "conv", "ssd", "mla"])
        if "norm" in phases:
            P.ph_norm()
        if "inproj" in phases:
            if mode == "A":
                P.ph_inproj([(O_XBC, O_Q), (O_KV, O_MG)], False)
            else:
                P.ph_inproj([(0, IN_DIM)], True)
        if "conv" in phases:
            P.ph_conv()
        if "ssd" in phases:
            P.ph_ssd(mode == "B")
        if "mla" in phases:
            P.ph_mla_prep(mode == "B")
            if mode == "B":
                P.ph_mla_proj()
                P.ph_mla_attn()
        if "pool" in phases:
            P.ph_pool()
        if "out" in phases:
            P.ph_branch()
            P.ph_out()
        gph.__exit__(None, None, None)
    c.barrier()
    c.close()
    return P


def host_layer_inputs(inp, L):
    f = lambda a: np.ascontiguousarray(a, dtype=np.float32)
    vecs = np.zeros((128, NV), np.float32)
    cw = inp["conv_w"][L][:, 0, :]
    vecs[:, V_CW:V_CW + 96] = cw.T.reshape(24, 128, 4).transpose(1, 0, 2).reshape(128, 96)
    vecs[:, V_CB:V_CB + 24] = inp["conv_b"][L].reshape(24, 128).T
    vecs[:, V_DS:V_DS + 16] = np.repeat(inp["d_skip"][L], HD).reshape(16, 128).T
    vecs[:, V_SN:V_SN + 16] = inp["ssd_norm_w"][L].reshape(16, 128).T
    vecs[:, V_QN:V_QN + 8] = inp["q_norm_w"][L].reshape(8, 128).T
    vecs[:, V_KN:V_KN + 4] = inp["kv_norm_w"][L].reshape(4, 128).T
    vecs[:, V_PS:V_PS + 16] = inp["pool_scale"][L].reshape(16, 128).T
    vecs[0:32, V_DTB] = inp["dt_bias"][L]
    inv_freq = np.power(np.float32(10000.0), -np.arange(0, ROPE, 2, dtype=np.float32) / np.float32(ROPE)).astype(np.float32)
    vecs[0:32, V_IF] = inv_freq
    vecs[32:64, V_IF] = inv_freq
    rows = np.concatenate([inp["pre_norm_w"][L], inp["post_norm_w"][L], inp["a_log"][L]])[None, :]
    return {
        "vecs": vecs, "rows": f(rows), "w_in": f(inp["w_in"][L]), "w_gate": f(inp["w_gate"][L]),
        "w_branch": f(inp["w_branch"][L]), "w_out": f(inp["w_out"][L]), "w_q_b": f(inp["w_q_b"][L]),
        "w_kv_b": f(inp["w_kv_b"][L]), "w_pool": f(inp["w_pool"][L]),
    }


def host_cmeta(seg, NSEG, SEG):
    cm = np.zeros((128, 16), np.float32)
    cm[:, C_M0] = 1.0 if seg == 0 else 0.0
    for j in range(4):
        cm[:, C_VIS + j] = 0.0 if j < seg else NEG
        cm[:, C_OH + j] = 1.0 if (j + 1) == seg else 0.0
    cm[:, C_POS] = float(seg * SEG)
    return cm


SEG_FULL, NSEG_FULL, NBATCH = 2048, 4, 2
_PROGS = {}


def _prog(mode):
    if mode not in _PROGS:
        _PROGS[mode] = build_program(SEG_FULL, NSEG_FULL, mode)
    return _PROGS[mode]


def kernel_unfused(**inputs):
    x = np.asarray(inputs["x"], dtype=np.float32)
    meta = np.asarray(inputs["meta_tokens"], dtype=np.float32)
    params = {k: np.asarray(v) for k, v in inputs.items() if k not in ("x", "meta_tokens")}
    SEG, NSEG = SEG_FULL, NSEG_FULL
    TS = HALO + SEG
    ncores = NBATCH * NSEG
    hfull = np.concatenate([np.broadcast_to(meta[None], (NBATCH, HALO, D)), x], axis=1).astype(np.float32)
    cmetas = [host_cmeta(cid % NSEG, NSEG, SEG) for cid in range(ncores)]
    depth = params["w_in"].shape[0]
    for L in range(depth):
        lay = host_layer_inputs(params, L)
        hs = [np.ascontiguousarray(hfull[cid // NSEG, (cid % NSEG) * SEG:(cid % NSEG) * SEG + TS]) for cid in range(ncores)]
        PA = _prog("A")
        in_maps = []
        for cid in range(ncores):
            m = {}
            for name in PA.inputs:
                m[name] = hs[cid] if name == "h" else cmetas[cid] if name == "cmeta" else lay[name]
            in_maps.append(m)
        ra = run_bass_kernel_spmd(PA.nc, in_maps, core_ids=list(range(ncores))).results
        kv_all, L_all, D_all = [], [], []
        for b in range(NBATCH):
            rs = [ra[b * NSEG + s] for s in range(NSEG)]
            kv_all.append(np.ascontiguousarray(np.concatenate([np.asarray(rs[0]["kvx"])[:, :HALO]] +
                                                              [np.asarray(r_["kvx"])[:, HALO:] for r_ in rs], axis=1)))
            L_all.append(np.ascontiguousarray(np.stack([np.asarray(r_["L_out"]) for r_ in rs], axis=0)))
            D_all.append(np.ascontiguousarray(np.concatenate([np.asarray(r_["D_out"])[0] for r_ in rs])[None, :]))
        PB = _prog("B")
        in_maps = []
        for cid in range(ncores):
            b = cid // NSEG
            m = {}
            for name in PB.inputs:
                if name == "h":
                    m[name] = hs[cid]
                elif name == "cmeta":
                    m[name] = cmetas[cid]
                elif name == "kv_all":
                    m[name] = kv_all[b]
                elif name == "L_all":
                    m[name] = L_all[b]
                elif name == "D_all":
                    m[name] = D_all[b]
                else:
                    m[name] = lay[name]
            in_maps.append(m)
        rb = run_bass_kernel_spmd(PB.nc, in_maps, core_ids=list(range(ncores))).results
        for cid in range(ncores):
            b, s = cid // NSEG, cid % NSEG
            ho = np.asarray(rb[cid]["h_out"])
            hfull[b, HALO + s * SEG:HALO + (s + 1) * SEG] = ho[HALO:]
            if s == 0:
                hfull[b, 0:HALO] = ho[0:HALO]
    return np.ascontiguousarray(hfull[:, HALO:]).astype(np.float32)


def build_fused(SEG, NSEG, depth):
    P = Prog(SEG, NSEG, "F")
    gph = Phase(P, "glob")
    gph.__enter__()
    for L in range(depth):
        P.L = L
        P.h_src = None if L == 0 else P.h1
        P.h_dst = P.h1 if L < depth - 1 else P.h_out
        P.load_consts(gph)
        P.ph_norm()
        P.ph_inproj([(0, IN_DIM)], True)
        P.ph_conv()
        P.ph_ssd(False)
        P.ph_mla_prep(True)
        P.exchange_mid()
        P.ph_ssd(True)
        P.ph_mla_proj()
        P.ph_mla_attn()
        P.ph_pool()
        P.ph_branch()
        P.ph_out()
        if L < depth - 1:
            P.ph_halo_exchange()
    gph.__exit__(None, None, None)
    P.c.barrier()
    P.c.close()
    return P


def kernel(**inputs):
    x = np.asarray(inputs["x"], dtype=np.float32)
    meta = np.asarray(inputs["meta_tokens"], dtype=np.float32)
    params = {k: np.asarray(v) for k, v in inputs.items() if k not in ("x", "meta_tokens")}
    SEG, NSEG = SEG_FULL, NSEG_FULL
    TS = HALO + SEG
    ncores = NBATCH * NSEG
    depth = params["w_in"].shape[0]
    if "F" not in _PROGS:
        _PROGS["F"] = build_fused(SEG, NSEG, depth)
    P = _PROGS["F"]
    hfull = np.concatenate([np.broadcast_to(meta[None], (NBATCH, HALO, D)), x], axis=1).astype(np.float32)
    lays = [host_layer_inputs(params, L) for L in range(depth)]
    in_maps = []
    for cid in range(ncores):
        b, s = cid // NSEG, cid % NSEG
        m = {}
        for key in P.inputs:
            if key == "h":
                m[key] = np.ascontiguousarray(hfull[b, s * SEG:s * SEG + TS])
            elif key == "cmeta":
                m[key] = host_cmeta(s, NSEG, SEG)
            else:
                name, L = key.rsplit("_L", 1)
                m[key] = lays[int(L)][name]
        in_maps.append(m)
    res = run_bass_kernel_spmd(P.nc, in_maps, core_ids=list(range(ncores))).results
    out = np.empty((NBATCH, NSEG * SEG, D), np.float32)
    for cid in range(ncores):
        b, s = cid // NSEG, cid % NSEG
        out[b, s * SEG:(s + 1) * SEG] = np.asarray(res[cid]["h_out"])[HALO:]
    return out
```
